# Optimizing a Trainium2 kernel written in Bass

```python
import jax, jax.numpy as jnp
from jax import lax
import numpy as np

D_MODEL = 2048
BATCH = 32
SEQ = 256
DEPTH = 4
DEC_BATCH = 8
DEC_SEQ = 1024
PAST_LEN = 512

GRID_W = 64
HEAD_DIM = 128
N_GROUPS = 4
GROUP_W = D_MODEL // N_GROUPS
GROUP_HEADS = GROUP_W // HEAD_DIM
NORM_EPS = 1e-6
NEG_INF = -1e30

LRU_W = GROUP_W
LRU_BLOCKS = GROUP_HEADS
LRU_BLOCK = LRU_W // LRU_BLOCKS
LRU_C = 8.0
CONV_W = 4
CONV_PAD = (1, 2)

NA_HEADS = GROUP_HEADS
NA_ROWS = 8
NA_COLS = 16
NA_QC = 16
NA_KC = 32

GQA_HEADS = GROUP_HEADS
GQA_KV_HEADS = 2
KV_W = GQA_KV_HEADS * HEAD_DIM
ROPE_THETA = 10000.0
Q_BLOCK = 128

ML_HEADS = GROUP_HEADS
ML_CHUNK = 64

IN_LAYOUT = (
    ('lru_x', LRU_W), ('lru_gate', LRU_W),
    ('na_q', GROUP_W), ('na_k', GROUP_W), ('na_v', GROUP_W), ('na_gate', GROUP_W),
    ('gqa_q', GROUP_W), ('gqa_k', KV_W), ('gqa_v', KV_W), ('gqa_gate', GROUP_W),
    ('ml_q', GROUP_W), ('ml_k', GROUP_W), ('ml_v', GROUP_W), ('ml_o', GROUP_W), ('ml_gate', GROUP_W),
    ('ml_if', 4 * ML_HEADS),
)
IN_DIM = sum(width for _, width in IN_LAYOUT)

kernel_name = 'hybrid_diffusion_parallel_heads_step'


def rmsnorm(x, w):
    xf = x.astype(jnp.float32)
    y = xf * lax.rsqrt(jnp.mean(xf * xf, axis=-1, keepdims=True) + NORM_EPS)
    return (y * w.astype(jnp.float32)).astype(x.dtype)


def split_projection(p):
    out, start = {}, 0
    for name, width in IN_LAYOUT:
        out[name] = p[..., start:start + width]
        start += width
    return out


def dwconv(x, w, b):
    y = lax.conv_general_dilated(x, w[:, None, :], window_strides=(1,), padding=[CONV_PAD],
                                 dimension_numbers=('NWC', 'WIO', 'NWC'), feature_group_count=x.shape[-1])
    return y + b


def linear_scan(a, b, h0, reverse):
    if reverse:
        a, b = jnp.flip(a, 1), jnp.flip(b, 1)
    b = b.at[:, 0].add(a[:, 0] * h0)

    def comb(l, r):
        return (l[0] * r[0], r[0] * l[1] + r[1])

    _, h = lax.associative_scan(comb, (a, b), axis=1)
    h_last = h[:, -1]
    if reverse:
        h = jnp.flip(h, 1)
    return h, h_last


def rglru(xc, w_r, b_r, w_i, b_i, lam, h0, reverse):
    B, T, W = xc.shape
    xb = xc.reshape(B, T, LRU_BLOCKS, LRU_BLOCK)
    r = jax.nn.sigmoid((jnp.einsum('btgi,gij->btgj', xb, w_r).reshape(B, T, W) + b_r).astype(jnp.float32))
    i = jax.nn.sigmoid((jnp.einsum('btgi,gij->btgj', xb, w_i).reshape(B, T, W) + b_i).astype(jnp.float32))
    log_a = LRU_C * r * jax.nn.log_sigmoid(lam.astype(jnp.float32))
    a = jnp.exp(log_a)
    b = jnp.sqrt(-jnp.expm1(2.0 * log_a)) * i * xc.astype(jnp.float32)
    return linear_scan(a, b, h0.astype(jnp.float32), reverse)


def blocked_attention(q, k, v):
    B, Tq, H, d = q.shape
    Hkv = k.shape[2]
    g = H // Hkv
    nb = Tq // Q_BLOCK
    qb = jnp.moveaxis(q.reshape(B, nb, Q_BLOCK, Hkv, g, d), 1, 0)

    def one_block(qblk):
        s = jnp.einsum('bqkgd,bskd->bkgqs', qblk, k, preferred_element_type=jnp.float32) * (d ** -0.5)
        p = jax.nn.softmax(s, axis=-1).astype(v.dtype)
        return jnp.einsum('bkgqs,bskd->bqkgd', p, v)

    o = lax.map(one_block, qb)
    return jnp.moveaxis(o, 0, 1).reshape(B, Tq, H, d)


def axial_rope(x):
    B, S, H, d = x.shape
    t = jnp.arange(S)
    half = d // 2
    nf = half // 2
    inv = ROPE_THETA ** (-jnp.arange(nf, dtype=jnp.float32) / nf)
    xf = x.astype(jnp.float32)

    def rotate(xa, pos):
        ang = pos.astype(jnp.float32)[:, None] * inv
        cos = jnp.cos(ang)[None, :, None, :]
        sin = jnp.sin(ang)[None, :, None, :]
        x1, x2 = xa[..., :nf], xa[..., nf:]
        return jnp.concatenate([x1 * cos - x2 * sin, x2 * cos + x1 * sin], axis=-1)

    out = jnp.concatenate([rotate(xf[..., :half], t // GRID_W), rotate(xf[..., half:], t % GRID_W)], axis=-1)
    return out.astype(x.dtype)


def natten_latent(q, k, v, kc, vc, rpb):
    B, S, H, d = q.shape
    rows = S // GRID_W
    wr = min(NA_ROWS, rows)
    ncb = GRID_W // NA_QC
    r = np.arange(rows)
    row_idx = np.clip(r - wr // 2, 0, rows - wr)[:, None] + np.arange(wr)
    cb = np.arange(ncb)
    col_idx = np.clip(cb * NA_QC - (NA_KC - NA_QC) // 2, 0, GRID_W - NA_KC)[:, None] + np.arange(NA_KC)
    qcol = cb[:, None] * NA_QC + np.arange(NA_QC)
    qcol_start = np.clip(qcol - NA_COLS // 2, 0, GRID_W - NA_COLS)
    valid = (col_idx[:, None, :] >= qcol_start[..., None]) & (col_idx[:, None, :] < qcol_start[..., None] + NA_COLS)
    row_off = row_idx - r[:, None] + NA_ROWS - 1
    col_off = np.clip(col_idx[:, None, :] - qcol[..., None] + NA_COLS - 1, 0, 2 * NA_COLS - 2)
    bias = rpb[:, row_off[:, None, None, :, None], col_off[None, :, :, None, :]].astype(jnp.float32)
    bias = jnp.where(valid[None, None, :, :, None, :], bias, NEG_INF)
    n_loc = wr * NA_KC
    bias = bias.reshape(H, rows, ncb, NA_QC, n_loc)
    kg = k.reshape(B, rows, GRID_W, H, d)
    vg = v.reshape(B, rows, GRID_W, H, d)
    gidx = (row_idx[:, None, :, None], col_idx[None, :, None, :])
    k_blk = kg[:, gidx[0], gidx[1]].reshape(B, rows, ncb, n_loc, H, d)
    v_blk = vg[:, gidx[0], gidx[1]].reshape(B, rows, ncb, n_loc, H, d)
    qb = q.reshape(B, rows, ncb, NA_QC, H, d)
    scale = d ** -0.5
    s_loc = jnp.einsum('brcqhd,brckhd->bhrcqk', qb, k_blk, preferred_element_type=jnp.float32) * scale + bias
    s_ctx = jnp.einsum('brcqhd,blhd->bhrcql', qb, kc, preferred_element_type=jnp.float32) * scale
    p = jax.nn.softmax(jnp.concatenate([s_loc, s_ctx], axis=-1), axis=-1).astype(v.dtype)
    o = (jnp.einsum('bhrcqk,brckhd->brcqhd', p[..., :n_loc], v_blk)
         + jnp.einsum('bhrcql,blhd->brcqhd', p[..., n_loc:], vc))
    return o.reshape(B, S, H, d)


def mlstm_chunkwise(q, k, v, li, lf, C0, n0, m0, reverse):
    B, T, H, d = q.shape
    if reverse:
        q, k, v, li, lf = (jnp.flip(a, 1) for a in (q, k, v, li, lf))
    nc = T // ML_CHUNK

    def chunks(a):
        a = a.astype(jnp.float32).reshape(B, nc, ML_CHUNK, H, *a.shape[3:])
        return jnp.moveaxis(a, (1, 3), (0, 2))

    qs = chunks(q) * (d ** -0.5)
    ks, vs, lis, lfs = chunks(k), chunks(v), chunks(li), chunks(lf)
    tri = jnp.tril(jnp.ones((ML_CHUNK, ML_CHUNK), dtype=bool))

    def step(carry, inp):
        C, n, m = carry
        qq, kk, vv, ii, ff = inp
        b = jnp.cumsum(ff, axis=-1)
        log_w = jnp.where(tri, b[..., :, None] - b[..., None, :] + ii[..., None, :], NEG_INF)
        m_inter = b + m[..., None]
        m_t = jnp.maximum(m_inter, jnp.max(log_w, axis=-1))
        w_inter = jnp.exp(m_inter - m_t)
        scores = jnp.einsum('bhtd,bhsd->bhts', qq, kk) * jnp.exp(log_w - m_t[..., None])
        num = jnp.einsum('bhts,bhse->bhte', scores, vv) + w_inter[..., None] * jnp.einsum('bhtd,bhde->bhte', qq, C)
        den = jnp.sum(scores, axis=-1) + w_inter * jnp.einsum('bhtd,bhd->bht', qq, n)
        h = num / jnp.maximum(jnp.abs(den), jnp.exp(-m_t))[..., None]
        b_last = b[..., -1]
        log_u = b_last[..., None] - b + ii
        m_new = jnp.maximum(b_last + m, jnp.max(log_u, axis=-1))
        w_prev = jnp.exp(b_last + m - m_new)
        u = jnp.exp(log_u - m_new[..., None])
        C_new = w_prev[..., None, None] * C + jnp.einsum('bhs,bhsd,bhse->bhde', u, kk, vv)
        n_new = w_prev[..., None] * n + jnp.einsum('bhs,bhsd->bhd', u, kk)
        return (C_new, n_new, m_new), h

    init = (C0.astype(jnp.float32), n0.astype(jnp.float32), m0.astype(jnp.float32))
    (C, n, m), h = lax.scan(step, init, (qs, ks, vs, lis, lfs))
    h = jnp.moveaxis(h, (0, 2), (1, 3)).reshape(B, T, H, d)
    if reverse:
        h = jnp.flip(h, 1)
    return h, (C, n, m)


def trunk_layer(x, cond, lp, ctx):
    B, T, _ = x.shape
    f32 = jnp.float32
    mod = jax.nn.silu(cond) @ lp['ada_w'] + lp['ada_b']
    if mod.ndim == 2:
        mod = mod[:, None, :]
    shift, scale, gate = jnp.split(mod, 3, axis=-1)
    xm = rmsnorm(x, lp['norm_w']) * (1.0 + scale) + shift
    pr = split_projection(xm @ lp['w_in'])

    if ctx is None:
        lru_h0 = jnp.zeros((B, 2, LRU_W), f32)
        C0 = jnp.zeros((B, 2, ML_HEADS, HEAD_DIM, HEAD_DIM), f32)
        n0 = jnp.zeros((B, 2, ML_HEADS, HEAD_DIM), f32)
        m0 = jnp.zeros((B, 2, ML_HEADS), f32)
    else:
        na_kc, na_vc, gqa_kc, gqa_vc, lru_h0, C0, n0, m0 = ctx

    xa = dwconv(pr['lru_x'], lp['lru_conv_w'], lp['lru_conv_b'])
    ha_f, sa_f = rglru(xa, lp['lru_wr'][0], lp['lru_br'][0], lp['lru_wi'][0], lp['lru_bi'][0],
                       lp['lru_lambda'][0], lru_h0[:, 0], False)
    ha_b, sa_b = rglru(xa, lp['lru_wr'][1], lp['lru_br'][1], lp['lru_wi'][1], lp['lru_bi'][1],
                       lp['lru_lambda'][1], lru_h0[:, 1], True)
    out_a = (ha_f + ha_b).astype(x.dtype) * jax.nn.silu(pr['lru_gate'])

    nq = pr['na_q'].reshape(B, T, NA_HEADS, HEAD_DIM)
    nk = pr['na_k'].reshape(B, T, NA_HEADS, HEAD_DIM)
    nv = pr['na_v'].reshape(B, T, NA_HEADS, HEAD_DIM)
    if ctx is None:
        ob = blocked_attention(nq, nk, nv)
    else:
        ob = natten_latent(nq, nk, nv, na_kc, na_vc, lp['na_rpb'])
    out_b = ob.reshape(B, T, GROUP_W) * jax.nn.silu(pr['na_gate'])

    gq = rmsnorm(pr['gqa_q'].reshape(B, T, GQA_HEADS, HEAD_DIM), lp['gqa_qnorm'])
    gk = rmsnorm(pr['gqa_k'].reshape(B, T, GQA_KV_HEADS, HEAD_DIM), lp['gqa_knorm'])
    gv = pr['gqa_v'].reshape(B, T, GQA_KV_HEADS, HEAD_DIM)
    if ctx is None:
        oc = blocked_attention(gq, gk, gv)
    else:
        oc = blocked_attention(axial_rope(gq), jnp.concatenate([axial_rope(gk), gqa_kc], axis=1),
                               jnp.concatenate([gv, gqa_vc], axis=1))
    out_c = oc.reshape(B, T, GROUP_W) * jax.nn.silu(pr['gqa_gate'])

    mq = pr['ml_q'].reshape(B, T, ML_HEADS, HEAD_DIM)
    mk = pr['ml_k'].reshape(B, T, ML_HEADS, HEAD_DIM)
    mv = pr['ml_v'].reshape(B, T, ML_HEADS, HEAD_DIM)
    gates = (pr['ml_if'] + lp['ml_gate_b']).astype(f32).reshape(B, T, 2, 2, ML_HEADS)
    h_sum = jnp.zeros((B, T, ML_HEADS, HEAD_DIM), f32)
    ml_states = []
    for dr in range(2):
        h_dir, st = mlstm_chunkwise(mq, mk, mv, gates[:, :, dr, 0], jax.nn.log_sigmoid(gates[:, :, dr, 1]),
                                    C0[:, dr], n0[:, dr], m0[:, dr], dr == 1)
        h_sum = h_sum + h_dir
        ml_states.append(st)
    hd = rmsnorm(h_sum, lp['ml_out_norm'].reshape(ML_HEADS, HEAD_DIM)).reshape(B, T, GROUP_W).astype(x.dtype)
    out_d = hd * jax.nn.sigmoid(pr['ml_o']) * jax.nn.silu(pr['ml_gate'])

    y = jnp.concatenate([out_a, out_b, out_c, out_d], axis=-1) @ lp['w_out']
    x = x + gate * y
    if ctx is not None:
        return x, None
    new_ctx = (nk, nv, gk, gv,
               jnp.stack([sa_f, sa_b], axis=1),
               jnp.stack([ml_states[0][0], ml_states[1][0]], axis=1),
               jnp.stack([ml_states[0][1], ml_states[1][1]], axis=1),
               jnp.stack([ml_states[0][2], ml_states[1][2]], axis=1))
    return x, new_ctx


def setup_inputs(seed: int = 0) -> dict:
    key = jax.random.key(seed)
    ks = jax.random.split(key, 40)
    f32 = jnp.float32

    def nrm(k, shape, s):
        return s * jax.random.normal(k, shape, f32)

    D = D_MODEL
    a0 = jax.random.uniform(ks[20], (DEPTH, 2, LRU_W), f32, minval=0.9, maxval=0.999)
    sig = a0 ** (1.0 / LRU_C)
    lru_lambda = jnp.log(sig) - jnp.log1p(-sig)
    i_bias = nrm(ks[21], (DEPTH, 2, 1, ML_HEADS), 0.1)
    f_bias = jnp.linspace(3.0, 6.0, ML_HEADS, dtype=f32) + nrm(ks[22], (DEPTH, 2, 1, ML_HEADS), 0.02)
    ml_gate_b = jnp.concatenate([i_bias, f_bias], axis=2).reshape(DEPTH, 4 * ML_HEADS)
    return {
        'x_prompt': nrm(ks[0], (BATCH, SEQ, D), 1.0),
        'x_sample': nrm(ks[1], (DEC_BATCH, DEC_SEQ, D), 1.0),
        'cache_na_k': nrm(ks[2], (DEC_BATCH, DEPTH, PAST_LEN, NA_HEADS, HEAD_DIM), 1.0),
        'cache_na_v': nrm(ks[3], (DEC_BATCH, DEPTH, PAST_LEN, NA_HEADS, HEAD_DIM), 1.0),
        'cache_gqa_k': nrm(ks[4], (DEC_BATCH, DEPTH, PAST_LEN, GQA_KV_HEADS, HEAD_DIM), 1.0),
        'cache_gqa_v': nrm(ks[5], (DEC_BATCH, DEPTH, PAST_LEN, GQA_KV_HEADS, HEAD_DIM), 1.0),
        'state_lru': nrm(ks[6], (DEC_BATCH, DEPTH, 2, LRU_W), 1.0),
        'state_mlstm_C': nrm(ks[7], (DEC_BATCH, DEPTH, 2, ML_HEADS, HEAD_DIM, HEAD_DIM), 0.3),
        'state_mlstm_n': nrm(ks[8], (DEC_BATCH, DEPTH, 2, ML_HEADS, HEAD_DIM), 0.3),
        'state_mlstm_m': 1.0 + nrm(ks[9], (DEC_BATCH, DEPTH, 2, ML_HEADS), 0.5),
        'c': nrm(ks[10], (DEC_BATCH, D), 1.0),
        'c_ctx': nrm(ks[11], (D,), 1.0),
        'norm_w': 1.0 + nrm(ks[12], (DEPTH, D), 0.02),
        'ada_w': nrm(ks[13], (DEPTH, D, 3 * D), D ** -0.5),
        'ada_b': nrm(ks[14], (DEPTH, 3 * D), 0.02),
        'w_in': nrm(ks[15], (DEPTH, D, IN_DIM), D ** -0.5),
        'lru_conv_w': nrm(ks[16], (DEPTH, CONV_W, LRU_W), CONV_W ** -0.5),
        'lru_conv_b': nrm(ks[17], (DEPTH, LRU_W), 0.02),
        'lru_wr': nrm(ks[18], (DEPTH, 2, LRU_BLOCKS, LRU_BLOCK, LRU_BLOCK), LRU_BLOCK ** -0.5),
        'lru_br': nrm(ks[19], (DEPTH, 2, LRU_W), 0.02),
        'lru_wi': nrm(ks[23], (DEPTH, 2, LRU_BLOCKS, LRU_BLOCK, LRU_BLOCK), LRU_BLOCK ** -0.5),
        'lru_bi': nrm(ks[24], (DEPTH, 2, LRU_W), 0.02),
        'lru_lambda': lru_lambda,
        'na_rpb': nrm(ks[25], (DEPTH, NA_HEADS, 2 * NA_ROWS - 1, 2 * NA_COLS - 1), 0.1),
        'gqa_qnorm': 1.0 + nrm(ks[26], (DEPTH, HEAD_DIM), 0.02),
        'gqa_knorm': 1.0 + nrm(ks[27], (DEPTH, HEAD_DIM), 0.02),
        'ml_gate_b': ml_gate_b,
        'ml_out_norm': 1.0 + nrm(ks[28], (DEPTH, GROUP_W), 0.02),
        'w_out': nrm(ks[29], (DEPTH, D, D), D ** -0.5),
        'final_norm_w': 1.0 + nrm(ks[30], (D,), 0.02),
    }


def reference(x_prompt, x_sample, cache_na_k, cache_na_v, cache_gqa_k, cache_gqa_v, state_lru,
              state_mlstm_C, state_mlstm_n, state_mlstm_m, c, c_ctx, norm_w, ada_w, ada_b, w_in,
              lru_conv_w, lru_conv_b, lru_wr, lru_br, lru_wi, lru_bi, lru_lambda, na_rpb,
              gqa_qnorm, gqa_knorm, ml_gate_b, ml_out_norm, w_out, final_norm_w):
    xp, xs = x_prompt, x_sample
    per_layer = []
    for l in range(DEPTH):
        lp = dict(norm_w=norm_w[l], ada_w=ada_w[l], ada_b=ada_b[l], w_in=w_in[l],
                  lru_conv_w=lru_conv_w[l], lru_conv_b=lru_conv_b[l], lru_wr=lru_wr[l], lru_br=lru_br[l],
                  lru_wi=lru_wi[l], lru_bi=lru_bi[l], lru_lambda=lru_lambda[l], na_rpb=na_rpb[l],
                  gqa_qnorm=gqa_qnorm[l], gqa_knorm=gqa_knorm[l], ml_gate_b=ml_gate_b[l],
                  ml_out_norm=ml_out_norm[l], w_out=w_out[l])
        xp, ctx_l = trunk_layer(xp, c_ctx, lp, None)
        per_layer.append(ctx_l)
        cached = (cache_na_k[:, l], cache_na_v[:, l], cache_gqa_k[:, l], cache_gqa_v[:, l], state_lru[:, l],
                  state_mlstm_C[:, l], state_mlstm_n[:, l], state_mlstm_m[:, l])
        xs, _ = trunk_layer(xs, c, lp, cached)
    y_prompt = rmsnorm(xp, final_norm_w)
    y_sample = rmsnorm(xs, final_norm_w)
    (new_na_k, new_na_v, new_gqa_k, new_gqa_v, new_lru,
     new_mlstm_C, new_mlstm_n, new_mlstm_m) = (jnp.stack([s[j] for s in per_layer], axis=1) for j in range(8))
    return (y_prompt, y_sample, new_na_k, new_na_v, new_gqa_k, new_gqa_v, new_lru,
            new_mlstm_C, new_mlstm_n, new_mlstm_m)
```

```python
import numpy as np
import concourse.bass as bass
import concourse.mybir as mybir
from concourse.bass_utils import run_bass_kernel_spmd
from concourse.ap import AP

F32 = mybir.dt.float32
BF16 = mybir.dt.bfloat16
ALU = mybir.AluOpType
AF = mybir.ActivationFunctionType
AX = mybir.AxisListType

L = 4
D = 2048
NT = 1024
NCORE = 8
EPS = 1e-6
SCL = 128.0 ** -0.5
PP = {}
_o = 0
for _n, _w in (("ada_b", 48), ("norm_w", 16), ("conv_w", 16), ("conv_b", 4), ("lru_b", 16), ("lam", 8),
               ("qnw", 1), ("knw", 1), ("gb", 16), ("won", 512), ("h0", 8), ("m0", 8)):
    PP[_n] = (_o, _w)
    _o += _w
NPP = _o


class Buf:
    __slots__ = ("name", "w", "r")

    def __init__(self, name):
        self.name = name
        self.w = None
        self.r = []


class PS:
    def __init__(self, t, b):
        self.t = t
        self.b = b
        self.ap = t[:]


class Node:
    __slots__ = ("i", "e", "fns", "deps", "dur", "lat", "kind", "args", "succ", "nd", "ready", "fin", "tag", "raw", "fs")

    def __init__(self, i, e, kind):
        self.i = i
        self.e = e
        self.kind = kind
        self.fns = []
        self.deps = set()
        self.raw = set()
        self.fs = None
        self.dur = 0.0
        self.lat = 0.0
        self.args = None
        self.succ = []
        self.tag = None


class KB:
    def __init__(self, nc):
        self.nc = nc
        self.eng = {"pe": nc.tensor, "act": nc.scalar, "dve": nc.vector, "pool": nc.gpsimd, "sp": nc.sync}
        self.sem = {k: nc.alloc_semaphore("s_" + k) for k in ("pe", "act", "dve", "pool")}
        self.cnt = {k: 0 for k in self.sem}
        self.seen = {k: {} for k in self.eng}
        self.dsem = []
        self.dval = []
        self.dq = {"sp": [], "pool": [], "act": []}
        self.dqi = {"sp": 0, "pool": 0, "act": 0}
        for q, n in (("sp", 14), ("pool", 8), ("act", 2)):
            for i in range(n):
                self.dq[q].append(len(self.dsem))
                self.dsem.append(nc.alloc_semaphore("d_%s%d" % (q, i)))
                self.dval.append(0)
        self.banks = []
        for i in range(8):
            t = nc.alloc_psum_tensor("ps%d" % i, [128, 512], F32)
            self.banks.append(PS(t, Buf("ps%d" % i)))
        self.pinned = set()
        self.slots = []
        for b in (6, 7):
            for j in range(3):
                self.slots.append((self.banks[b].t[:, j * 170:(j + 1) * 170], Buf("slot%d_%d" % (b, j))))
        self.si = 0
        self.pbi = 0
        self.bi = 0
        self.nins = 0
        self.npe = 0
        self.marks = []
        self.nodes = []
        self.open_pe = None

    def _deps(self, node, reads, writes):
        d = node.deps
        for b in reads:
            if b.w is not None:
                d.add(b.w)
                node.raw.add(b.w)
        for b in writes:
            if b.w is not None:
                d.add(b.w)
            d.update(b.r)
        for b in reads:
            b.r.append(node.i)
        for b in writes:
            b.w = node.i
            b.r = []

    def op(self, e, fn, reads=(), writes=(), inc=True, c=0.1, fs=None):
        self.nins += 1
        if e == "pe":
            self.npe += 1
            if self.open_pe is not None:
                node = self.open_pe
            else:
                node = Node(len(self.nodes), e, "op")
                self.nodes.append(node)
            node.fns.append(fn)
            node.dur += c
            self._deps(node, reads, writes)
            node.deps.discard(node.i)
            self.open_pe = None if inc else node
            return
        node = Node(len(self.nodes), e, "op")
        self.nodes.append(node)
        node.fns.append(fn)
        node.dur = c
        node.fs = fs
        self._deps(node, reads, writes)

    def dma(self, q, out, in_, reads=(), writes=(), nbytes=65536):
        self.nins += 1
        node = Node(len(self.nodes), q, "dma")
        self.nodes.append(node)
        node.args = (out, in_)
        node.dur = 0.06
        node.lat = 2.0 + nbytes / 150e3
        self._deps(node, reads, writes)

    def schedule(self):
        import heapq
        nodes = self.nodes
        for n in nodes:
            n.nd = len(n.deps)
            n.ready = 0.0
        for n in nodes:
            for d in n.deps:
                nodes[d].succ.append(n.i)
        XL = 0.5
        tail = [0.0] * len(nodes)
        for n in reversed(nodes):
            t = 0.0
            for si in n.succ:
                x = tail[si] + ((0.25 if n.e == "dve" else 0.05) if nodes[si].e == n.e else XL)
                if x > t:
                    t = x
            tail[n.i] = t + n.dur + n.lat
        engs = list(self.eng.keys())
        waiting = {e: [] for e in engs}
        avail = {e: [] for e in engs}
        free_at = {e: 0.0 for e in engs}
        for n in nodes:
            if n.nd == 0:
                heapq.heappush(waiting[n.e], (0.0, n.i))
        order = []
        cur_fs = [None]
        remaining = len(nodes)
        while remaining:
            best = None
            for e in engs:
                w = waiting[e]
                a = avail[e]
                fa = free_at[e]
                while w and w[0][0] <= fa:
                    wi = heapq.heappop(w)[1]
                    heapq.heappush(a, (-tail[wi], wi))
                if a:
                    cand = (fa, a[0][1], e, True)
                elif w:
                    cand = (w[0][0], w[0][1], e, False)
                else:
                    continue
                if best is None or cand < best:
                    best = cand
            st, i, e, fromavail = best
            if fromavail:
                if e == "act":
                    a = avail[e]
                    cand_l = [heapq.heappop(a) for _ in range(min(8, len(a)))]
                    pick = 0
                    for ci_, (_, ni_) in enumerate(cand_l):
                        if nodes[ni_].fs is None or nodes[ni_].fs == cur_fs[0]:
                            pick = ci_
                            break
                    i = cand_l[pick][1]
                    for ci_, it_ in enumerate(cand_l):
                        if ci_ != pick:
                            heapq.heappush(a, it_)
                else:
                    heapq.heappop(avail[e])
            else:
                heapq.heappop(waiting[e])
            n = nodes[i]
            if e == "act" and n.fs is not None:
                if n.fs != cur_fs[0]:
                    st += 1.3
                cur_fs[0] = n.fs
            free_at[e] = st + n.dur
            n.fin = st + n.dur + n.lat
            order.append(i)
            remaining -= 1
            for si in n.succ:
                sn = nodes[si]
                r = n.fin + ((0.25 if e == "dve" else 0.05) if sn.e == e else XL)
                if r > sn.ready:
                    sn.ready = r
                sn.nd -= 1
                if sn.nd == 0:
                    heapq.heappush(waiting[sn.e], (sn.ready, si))
        self.makespan = max(free_at.values())
        return order

    def _wait(self, e, key, val):
        if self.seen[e].get(key, 0) >= val:
            return
        sem = self.sem[key] if isinstance(key, str) else self.dsem[key[1]]
        self.eng[e].wait_ge(sem, val)
        self.seen[e][key] = val

    def finish(self, sched=True):
        assert self.open_pe is None
        nodes = self.nodes
        order = self.schedule() if sched else list(range(len(nodes)))
        for i in order:
            n = nodes[i]
            e = n.e
            need = {}
            for d in n.deps:
                k, v = nodes[d].tag
                if k == e and (e == "pe" or (e == "act" and d not in n.raw)):
                    continue
                if need.get(k, 0) < v:
                    need[k] = v
            for k, v in need.items():
                self._wait(e, k, v)
            if n.kind == "dma":
                lst = self.dq[e]
                si = lst[self.dqi[e] % len(lst)]
                self.dqi[e] += 1
                self._wait(e, ("d", si), self.dval[si])
                ins = self.eng[e].dma_start(out=n.args[0], in_=n.args[1])
                self.dval[si] += 16
                ins.then_inc(self.dsem[si], 16)
                n.tag = (("d", si), self.dval[si])
            else:
                ins = None
                for fn in n.fns:
                    ins = fn()
                self.cnt[e] += 1
                ins.then_inc(self.sem[e], 1)
                n.tag = (e, self.cnt[e])
        for si in range(len(self.dsem)):
            if self.dval[si]:
                self._wait("sp", ("d", si), self.dval[si])
        for k in self.sem:
            if self.cnt[k]:
                self._wait("sp", k, self.cnt[k])

    def psum(self):
        for _ in range(16):
            i = 2 + self.bi % 6
            self.bi += 1
            if i not in self.pinned:
                return self.banks[i]
        raise RuntimeError("no psum")

    def ppsum(self):
        i = self.pbi % 2
        self.pbi += 1
        return self.banks[i]

    def pslot(self):
        p = self.psum()
        return p.ap, p.b

    def pin(self):
        p = self.psum()
        self.pinned.add(self.banks.index(p))
        return p

    def unpin(self, p):
        self.pinned.discard(self.banks.index(p))


def rev(a):
    pairs = [list(p) for p in a.ap]
    st, n = pairs[-1]
    pairs[-1] = [-st, n]
    return AP(a.tensor, a.offset + st * (n - 1), pairs)


def build(depth=L, dbg=False, stage=99):
    nc = bass.Bass("TRN2", target_bir_lowering=False)
    K = KB(nc)
    E = K.eng

    def din(name, shape):
        return nc.dram_tensor(name, list(shape), F32, kind="ExternalInput").ap()

    def dout(name, shape):
        return nc.dram_tensor(name, list(shape), F32, kind="ExternalOutput").ap()

    x0T = din("x0T", [2, 16, 128, NT])
    condT = din("condT", [128, 16, 2])
    ada_wT = din("ada_wT", [L, 12, 128, 16, 512])
    w_inT = din("w_inT", [L, 14, 128, 16, 512])
    w_ifT = din("w_ifT", [L, 128, 16, 16])
    w_outT = din("w_outT", [L, 4, 128, 16, 512])
    ppd = din("pp", [L, 128, NPP])
    fnw = din("fnw", [128, 16])
    lru_wT = din("lru_wT", [L, 128, 16, 128])
    rpbT = din("rpbT", [L, 4, 128, 16, 64])
    maskT = din("maskT", [128, 16, 64])
    c_nakT = din("c_nakT", [L, 128, 4, 512])
    c_nav = din("c_nav", [L, 128, 4, 512])
    c_gkT = din("c_gkT", [L, 128, 2, 512])
    c_gv = din("c_gv", [L, 128, 4, 256])
    ropeC = din("ropeC", [128, NT])
    ropeS = din("ropeS", [128, NT])
    cst = din("cst", [128, 11, 128])
    C0n0 = din("C0n0", [L, 128, 2, 4, 129])
    yT = dout("yT", [2, 16, 128, NT])
    o_nak = dout("o_nak", [L, 128, 4, NT])
    o_nav = dout("o_nav", [L, 128, 8, 512])
    o_gk = dout("o_gk", [L, 128, 2, NT])
    o_gv = dout("o_gv", [L, 128, 8, 256])
    o_lru = dout("o_lru", [L, 128, 2, 4, 4])
    o_C = dout("o_C", [L, 128, 2, 4, 4, 129])
    o_m = dout("o_m", [L, 32, 1])
    xsc = nc.dram_tensor("xsc", [2, 16, 128, NT], F32, kind="Internal").ap()
    if dbg:
        o_dbg = dout("o_dbg", [2, 16, 128, NT])
    xsb = [[[Buf("xs") for _ in range(2)] for _ in range(16)] for _ in range(2)]

    def sb(name, shape, dt):
        return nc.alloc_sbuf_tensor(name, list(shape), dt)

    Wt = [sb("W%d" % i, [128, 16, 512], BF16) for i in range(2)]
    Wb = [[Buf("W%d_%d" % (i, j)) for j in range(4)] for i in range(2)]
    xmT = sb("xmT", [128, 16, NT], BF16)
    xmB = [Buf("xm%d" % i) for i in range(16)]
    catT = sb("catT", [128, 16, NT], BF16)
    catB = [Buf("cat%d" % i) for i in range(16)]
    NREG = 6
    Rg = [sb("R%d" % i, [128, 4096], BF16) for i in range(NREG)]
    RgB = [Buf("R%d" % i) for i in range(NREG)]
    NTMP = 6
    Tp = [sb("T%d" % i, [128, 1024], F32) for i in range(NTMP)]
    TpB = [Buf("T%d" % i) for i in range(NTMP)]
    cs = sb("cs", [128, 11, 128], F32)
    csB = Buf("cs")
    cs_bf = sb("cs_bf", [128, 2, 128], BF16)
    rstd = sb("rstd", [128, 2, NT], F32)
    rstdB = [Buf("rstd0"), Buf("rstd1")]
    ppt = [sb("pp0", [128, NPP], F32)] * 2
    ppB = [Buf("pp0")] * 2
    fnw_sb = sb("fnw_sb", [128, 16], F32)
    sc_bf = sb("sc_bf", [128, 16, 2], BF16)
    cond_sb = sb("cond_sb", [128, 16, 2], F32)
    scB = Buf("sc")
    modL = [sb("mod%d" % i, [128, 48, 2], F32) for i in range(2)]
    gmodL = [sb("gmod%d" % i, [128, 16, 2], F32) for i in range(2)]
    modBL = [Buf("mod0"), Buf("mod1")]
    abt = sb("abt", [128, 64], F32)
    abtB = Buf("abt")
    wl = Rg[2][:, 2048:4096].rearrange("p (a t) -> p a t", a=16)
    wlB = RgB[2]
    clam = sb("clam", [128, 8], F32)
    clamB = Buf("clam")
    lruS = sb("lruS", [128, 2, 4, 4], F32)
    lruSB = Buf("lruS")
    maskS = sb("maskS", [128, 16, 64], F32)[:]
    maskB = Buf("mask")
    wif = sb("wif", [128, 16, 16], BF16)
    wifB = Buf("wif")
    NPT = 4
    pTt = [sb("pT%d" % i, [128, 512], BF16) for i in range(NPT)]
    pTB = [Buf("pT%d" % i) for i in range(NPT)]
    NSM = 8
    smt = [sb("sm%d" % i, [128, 128], BF16) for i in range(NSM)]
    smB = [Buf("sm%d" % i) for i in range(NSM)]
    St = [[sb("St%d%d" % (d, h), [128, 129], F32) for h in range(4)] for d in range(2)]
    Stb = [[sb("Stb%d%d" % (d, h), [128, 129], BF16) for h in range(4)] for d in range(2)]
    StB = [[Buf("St") for h in range(4)] for d in range(2)]
    StbB = [[Buf("Stb") for h in range(4)] for d in range(2)]
    gt = sb("gt", [128, 8, 16], F32)
    nlf = sb("nlf", [128, 2, 8, 4], F32)
    nbt = sb("nbt", [128, 2, 8, 4], F32)
    eat = sb("eat", [128, 2, 8, 4], F32)
    ea01 = sb("ea01", [128, 2, 2, 8, 4], F32)
    enb = sb("enb", [128, 2, 8, 4], F32)
    eBt = sb("eBt", [128, 2, 2, 8, 4], F32)
    zt = sb("zt", [128, 2, 32], F32)
    nlfz = sb("nlfz", [128, 2, 32], F32)
    gB = Buf("gates")
    mfs = sb("mfs", [32, 8], F32)
    emfbc = sb("emfbc", [128, 32], F32)
    em0 = sb("em0", [128, 8], F32)
    tiny = [sb("tiny%d" % i, [128, 8], F32) for i in range(6)]
    tinyB = [Buf("tiny%d" % i) for i in range(6)]
    stg = [sb("stg%d" % i, [128, 129], F32) for i in range(2)]
    stgB = [Buf("stg0"), Buf("stg1")]
    cnt = {"pt": 0, "sm": 0, "tiny": 0, "stg": 0, "tp": 0}

    def nxt(kind, n):
        i = cnt[kind] % n
        cnt[kind] += 1
        return i

    reserved = set()

    def tmp():
        while True:
            i = cnt["tp"] % NTMP
            cnt["tp"] += 1
            if i not in reserved:
                return i

    IDf, ONf, PMf, TRF, TRB, TFF, TFB, H0, H1, MBF, MBB = [cs[:, i, :] for i in range(11)]
    IDb = cs_bf[:, 0, :]
    ONb = cs_bf[:, 1, :]

    def fsz(a):
        n = 1
        for x in a.shape[1:]:
            n *= x
        return n

    def mm(out, lhsT, rhs, st, sp, R, Wr, inc):
        n = fsz(rhs)
        c = 0.03 + max(n, 64) * 0.00045
        if rhs.dtype == F32:
            c *= 4
        K.op("pe", lambda: nc.tensor.matmul(out, lhsT, rhs, start=st, stop=sp, skip_group_check=True), R, Wr, inc, c=c)

    def act(out, in_, func, R, Wr, scale=1.0, bias=0.0):
        fs = {AF.Exp: "e", AF.Ln: "e", AF.Silu: "s", AF.Sigmoid: "g", AF.Sqrt: "q"}.get(func)
        K.op("act", lambda: nc.scalar.activation(out=out, in_=in_, func=func, bias=bias, scale=scale), R, Wr, c=0.2 + fsz(out) * 0.00105, fs=fs)

    def tt(out, in0, in1, op, R, Wr, e="dve"):
        K.op(e, lambda: E[e].tensor_tensor(out=out, in0=in0, in1=in1, op=op), R, Wr, c=0.1 + fsz(out) * 0.00105)

    def ts(out, in0, s1, s2, op0, op1, R, Wr, e="dve"):
        c = 0.1 + fsz(out) * 0.0008
        if s2 is None:
            K.op(e, lambda: E[e].tensor_scalar(out=out, in0=in0, scalar1=s1, scalar2=None, op0=op0), R, Wr, c=c)
        else:
            K.op(e, lambda: E[e].tensor_scalar(out=out, in0=in0, scalar1=s1, scalar2=s2, op0=op0, op1=op1), R, Wr, c=c)

    def stt(out, in0, sc, in1, op0, op1, R, Wr, e="dve"):
        K.op(e, lambda: E[e].scalar_tensor_tensor(out=out, in0=in0, scalar=sc, in1=in1, op0=op0, op1=op1), R, Wr, c=0.1 + fsz(out) * 0.00105)

    def rsq(out, in_, mul, R, Wr):
        act(out, in_, AF.Ln, R, Wr, scale=mul, bias=EPS)
        act(out, out, AF.Exp, Wr, Wr, scale=-0.5)

    def cp(out, in_, R, Wr, e="dve"):
        K.op(e, lambda: E[e].tensor_copy(out=out, in_=in_), R, Wr, c=0.1 + fsz(out) * 0.0008)

    wsched = []
    for t in range(12):
        wsched.append(ada_wT[0, t])
    for l in range(depth):
        for u in range(2):
            for t in range(14):
                wsched.append(w_inT[l, t])
            if u == 1 and l + 1 < depth:
                for t in range(12):
                    wsched.append(ada_wT[l + 1, t])
            for t in range(4):
                wsched.append(w_outT[l, t])
    wstate = {"issued": 0, "used": 0}

    def w_issue():
        i = wstate["issued"]
        if i >= len(wsched):
            return
        s = i % 2
        for qd in range(4):
            K.dma("pool", Wt[s][:, qd * 4:(qd + 1) * 4, :], wsched[i][:, qd * 4:(qd + 1) * 4, :], [], [Wb[s][qd]], nbytes=1 << 20)
        wstate["issued"] += 1

    def w_get():
        i = wstate["used"]
        while wstate["issued"] <= min(i + 1, len(wsched) - 1):
            w_issue()
        wstate["used"] += 1
        return i % 2

    K.dma("sp", cs[:], cst, [], [csB])
    cp(cs_bf[:, 0, :], cs[:, 0, :], [csB], [csB])
    cp(cs_bf[:, 1, :], cs[:, 1, :], [csB], [csB])
    K.dma("sp", cond_sb[:], condT, [], [scB])
    act(sc_bf[:], cond_sb[:], AF.Silu, [scB], [scB])
    K.dma("sp", fnw_sb[:], fnw, [], [scB])
    K.dma("sp", maskS, maskT, [], [maskB])

    def compute_rstd(u, ssp):
        for hh in range(2):
            sl = slice(hh * 512, (hh + 1) * 512)
            rsq(rstd[:, u, sl], ssp[hh].ap, 1.0 / D, [ssp[hh].b], [rstdB[u]])

    def prologue(u):
        ssp = [K.pin(), K.pin()]
        for kc in range(16):
            a = tmp()
            b = tmp()
            K.dma("sp", Tp[a][:], x0T[u, kc], [], [TpB[a]])
            sqb = Tp[b][:].bitcast(BF16)
            act(sqb[:, :NT], Tp[a][:], AF.Square, [TpB[a]], [TpB[b]])
            for hh in range(2):
                mm(ssp[hh].ap, ONb, sqb[:, hh * 512:(hh + 1) * 512], kc == 0, kc == 15, [TpB[b], csB], [ssp[hh].b], True)
        compute_rstd(u, ssp)
        K.unpin(ssp[0])
        K.unpin(ssp[1])

    def xsrc(l, u, fc):
        return x0T[u, fc] if l == 0 else xsc[u, fc]

    def phaseN(l, u, pcur):
        pb = ppB[l % 2]
        sh0 = 0
        for kc in range(16):
            a = tmp()
            rds = [] if l == 0 else [xsb[u][kc][0], xsb[u][kc][1]]
            K.dma("sp", Tp[a][:], xsrc(l, u, kc), rds, [TpB[a]])
            tt(Tp[a][:], Tp[a][:], rstd[:, u, :], ALU.mult, [TpB[a], rstdB[u]], [TpB[a]])
            act(xmT[:, kc, :], Tp[a][:], AF.Identity, [TpB[a], modBL[l % 2]], [xmB[kc]],
                scale=gmodL[l % 2][:, kc, u:u + 1], bias=modL[l % 2][:, sh0 + kc, u:u + 1])

    def proj_fm(s, col0, ncols, evac):
        for j in range(ncols // 128):
            for hh in range(2):
                ps = K.ppsum()
                for kc in range(16):
                    mm(ps.ap, Wt[s][:, kc, col0 + j * 128: col0 + (j + 1) * 128], xmT[:, kc, hh * 512:(hh + 1) * 512],
                       kc == 0, kc == 15, [Wb[s][kc // 4], xmB[kc]], [ps.b], kc == 15)
                evac(j, hh, ps)

    def proj_tm(s, col0, ncols, evac, wt=None, wb=None):
        for t8 in range(8):
            ps = K.ppsum()
            for kc in range(16):
                if wt is None:
                    rhs = Wt[s][:, kc, col0:col0 + ncols]
                    rb = Wb[s][kc // 4]
                else:
                    rhs = wt[:, kc, :]
                    rb = wb
                mm(ps.ap[:, :ncols], xmT[:, kc, t8 * 128:(t8 + 1) * 128], rhs, kc == 0, kc == 15, [rb, xmB[kc]], [ps.b], kc == 15)
            evac(t8, ps)

    def R4(i):
        return Rg[i][:].rearrange("p (a t) -> p a t", a=4)

    def R8(i):
        return Rg[i][:].rearrange("p (a t) -> p a t", a=8)

    def Rf(i):
        return Rg[i][:].bitcast(F32).rearrange("p (a t) -> p a t", a=2)

    def attn(steps, nq, out_ap, outB, gate_ap, gateB):
        O = K.pin()
        Rr = K.pin()
        nst = len(steps)

        def emitS(st):
            kT, kR, q, qR, n, off, bias, bR, v, vR = st
            kTs = kT if isinstance(kT, list) else [kT]
            G = len(kTs)
            ps = K.psum()
            ps_ap, ps_b = ps.ap, ps.b
            for j in range(G):
                mm(ps_ap[:, j * n:(j + 1) * n], kTs[j], q, True, True, kR + qR, [ps_b], j == G - 1)
            pi = nxt("pt", NPT)
            w = G * n
            if bias is not None:
                a = tmp()
                if G > 1:
                    o_ = Tp[a][:, :w].rearrange("p (g t) -> p g t", g=G)
                    i_ = ps_ap[:, :w].rearrange("p (g t) -> p g t", g=G)
                else:
                    o_, i_ = Tp[a][:, :w], ps_ap[:, :w]
                stt(o_, i_, SCL, bias, ALU.mult, ALU.add, [ps_b] + bR, [TpB[a]])
                act(pTt[pi][:, :w], Tp[a][:, :w], AF.Exp, [TpB[a]], [pTB[pi]])
            else:
                act(pTt[pi][:, :w], ps_ap[:, :w], AF.Exp, [ps_b], [pTB[pi]], scale=SCL)
            return pi

        cur = emitS(steps[0])
        for i in range(nst):
            nx = emitS(steps[i + 1]) if i + 1 < nst else None
            kT, kR, q, qR, n, off, bias, bR, v, vR = steps[i]
            vs = v if isinstance(v, list) else [v]
            G = len(vs)
            last = i == nst - 1
            for j in range(G):
                lj = last and j == G - 1
                mm(O.ap[:, off:off + n], vs[j], pTt[cur][:, j * n:(j + 1) * n], i == 0 and j == 0, lj, vR + [pTB[cur]], [O.b], False)
                mm(Rr.ap[:, off:off + n], ONb, pTt[cur][:, j * n:(j + 1) * n], i == 0 and j == 0, lj, [pTB[cur], csB], [Rr.b, O.b], j == G - 1)
            cur = nx
        a = tmp()
        act(Tp[a][:, :nq], Rr.ap[:, :nq], AF.Ln, [Rr.b], [TpB[a]])
        act(Tp[a][:, :nq], Tp[a][:, :nq], AF.Exp, [TpB[a]], [TpB[a]], scale=-1.0)
        tt(Tp[a][:, :nq], O.ap[:, :nq], Tp[a][:, :nq], ALU.mult, [O.b, TpB[a]], [TpB[a]])
        tt(out_ap, Tp[a][:, :nq], gate_ap, ALU.mult, [TpB[a]] + gateB, outB)
        K.unpin(O)
        K.unpin(Rr)

    def branchA(l, u):
        pc = ppt[l % 2]
        pb = ppB[l % 2]
        XT = R4(0)
        GT = R4(1)
        xabB = Buf("xab")
        s = w_get()
        proj_fm(s, 0, 512, lambda j, hh, ps: cp(XT[:, j, hh * 512:(hh + 1) * 512], ps.ap, [ps.b], [RgB[0]]))
        s = w_get()
        proj_fm(s, 0, 512, lambda j, hh, ps: act(GT[:, j, hh * 512:(hh + 1) * 512], ps.ap, AF.Silu, [ps.b], [RgB[1]]))
        cw0 = PP["conv_w"][0]
        cb0 = PP["conv_b"][0]
        lb0 = PP["lru_b"][0]
        h00 = PP["h0"][0]
        nseq, Ts = (4, 256) if u == 0 else (1, 1024)
        tl = list(range(NTMP))
        K.dma("pool", wl, lru_wT[l], [], [wlB])
        for g in range(4):
            X = XT[:, g, :]
            XB = RgB[0]
            ixa, ir, ii, itm, ihf, ihb = tl
            xa = Tp[ixa][:]
            xab = Rg[2][:, :NT]
            ts(xa, X, pc[:, cw0 + g * 4 + 1: cw0 + g * 4 + 2], pc[:, cb0 + g: cb0 + g + 1], ALU.mult, ALU.add, [XB, pb], [TpB[ixa]])
            Xv = X.rearrange("p (s t) -> p s t", s=nseq)
            xav = xa.rearrange("p (s t) -> p s t", s=nseq)
            stt(xav[:, :, 1:], Xv[:, :, :Ts - 1], pc[:, cw0 + g * 4: cw0 + g * 4 + 1], xav[:, :, 1:], ALU.mult, ALU.add, [XB, pb, TpB[ixa]], [TpB[ixa]])
            stt(xav[:, :, :Ts - 1], Xv[:, :, 1:], pc[:, cw0 + g * 4 + 2: cw0 + g * 4 + 3], xav[:, :, :Ts - 1], ALU.mult, ALU.add, [XB, pb, TpB[ixa]], [TpB[ixa]])
            stt(xav[:, :, :Ts - 2], Xv[:, :, 2:], pc[:, cw0 + g * 4 + 3: cw0 + g * 4 + 4], xav[:, :, :Ts - 2], ALU.mult, ALU.add, [XB, pb, TpB[ixa]], [TpB[ixa]])
            act(xab, xa, AF.Copy, [TpB[ixa]], [RgB[2]])
            for dr in range(2):
                gts = [Tp[ir][:], Tp[ii][:]]
                gtb = [TpB[ir], TpB[ii]]
                for ri in range(2):
                    for hh in range(2):
                        ps = K.psum()
                        mm(ps.ap, wl[:, dr * 8 + ri * 4 + g, :], xab[:, hh * 512:(hh + 1) * 512], True, True, [wlB, RgB[2]], [ps.b], True)
                        c = lb0 + dr * 8 + ri * 4 + g
                        act(gts[ri][:, hh * 512:(hh + 1) * 512], ps.ap, AF.Sigmoid, [ps.b, pb], [gtb[ri]], bias=pc[:, c:c + 1])
                av = Tp[ir][:]
                bv = Tp[ii][:]
                tm = Tp[itm][:]
                act(av, av, AF.Exp, [TpB[ir], clamB], [TpB[ir]], scale=clam[:, dr * 4 + g: dr * 4 + g + 1])
                tt(tm, av, av, ALU.mult, [TpB[ir]], [TpB[itm]])
                act(tm, tm, AF.Sqrt, [TpB[itm]], [TpB[itm]], scale=-1.0, bias=1.0)
                tt(bv, tm, bv, ALU.mult, [TpB[itm], TpB[ii]], [TpB[ii]])
                tt(bv, bv, xa, ALU.mult, [TpB[ii], TpB[ixa]], [TpB[ii]])
                ih = ihf if dr == 0 else ihb
                hv = Tp[ih][:]
                for sq in range(nseq):
                    sl = slice(sq * Ts, (sq + 1) * Ts)
                    if u == 0:
                        init = 0.0
                    else:
                        init = pc[:, h00 + dr * 4 + g: h00 + dr * 4 + g + 1]
                    if dr == 0:
                        o_, a_, b_ = hv[:, sl], av[:, sl], bv[:, sl]
                    else:
                        o_, a_, b_ = rev(hv[:, sl]), rev(av[:, sl]), rev(bv[:, sl])
                    K.op("dve", lambda o_=o_, a_=a_, b_=b_, init=init: nc.vector.tensor_tensor_scan(
                        out=o_, data0=a_, data1=b_, initial=init, op0=ALU.mult, op1=ALU.add),
                        [TpB[ir], TpB[ii], pb], [TpB[ih]], c=0.1 + Ts * 0.00105)
                if u == 0:
                    hvv = hv.rearrange("p (s t) -> p s t", s=4)
                    col = 255 if dr == 0 else 0
                    act(lruS[:, dr, g, :], hvv[:, :, col], AF.Copy, [TpB[ih]], [lruSB])
            tt(Tp[ihf][:], Tp[ihf][:], Tp[ihb][:], ALU.add, [TpB[ihf], TpB[ihb]], [TpB[ihf]])
            tt(catT[:, g, :], Tp[ihf][:], GT[:, g, :], ALU.mult, [TpB[ihf], RgB[1]], [catB[g]])
        if u == 0:
            K.dma("sp", o_lru[l], lruS[:], [lruSB], [])

    def branchB(l, u):
        qT, kT, vv, gT = R4(3), R4(4), R8(5), R4(0)
        s = w_get()
        proj_fm(s, 0, 512, lambda j, hh, ps: cp(qT[:, j, hh * 512:(hh + 1) * 512], ps.ap, [ps.b], [RgB[3]]))
        s = w_get()

        def ev_k(j, hh, ps):
            sl = slice(hh * 512, (hh + 1) * 512)
            if u == 0:
                a = tmp()
                act(Tp[a][:, :512], ps.ap, AF.Copy, [ps.b], [TpB[a]])
                K.dma("sp", o_nak[l, :, j, sl], Tp[a][:, :512], [TpB[a]], [])
                cp(kT[:, j, sl], Tp[a][:, :512], [TpB[a]], [RgB[4]])
            else:
                cp(kT[:, j, sl], ps.ap, [ps.b], [RgB[4]])
        import os
        sub = int(os.environ.get("KSUB", "9"))
        if sub == -1:
            [w_get() for _ in range(3)]
            return
        proj_fm(s, 0, 512, ev_k)
        s = w_get()
        if sub == -2:
            [w_get() for _ in range(2)]
            return

        def ev_v(t8, ps):
            if u == 0:
                a = tmp()
                act(Tp[a][:, :512], ps.ap, AF.Copy, [ps.b], [TpB[a]])
                K.dma("sp", o_nav[l, :, t8, :], Tp[a][:, :512], [TpB[a]], [])
                cp(vv[:, t8, :], Tp[a][:, :512], [TpB[a]], [RgB[5]])
            else:
                cp(vv[:, t8, :], ps.ap, [ps.b], [RgB[5]])
        proj_tm(s, 0, 512, ev_v)
        s = w_get()
        proj_fm(s, 0, 512, lambda j, hh, ps: act(gT[:, j, hh * 512:(hh + 1) * 512], ps.ap, AF.Silu, [ps.b], [RgB[0]]))
        if sub == 0 or (sub == 1 and u == 1):
            return
        if u == 0:
            for sq in range(4):
                for h in range(4):
                    qs = slice(sq * 256, (sq + 1) * 256)
                    steps = []
                    for c in range(2):
                        t0 = sq * 256 + c * 128
                        steps.append((kT[:, h, t0:t0 + 128], [RgB[4]], qT[:, h, qs], [RgB[3]], 256, 0, None, [],
                                      vv[:, sq * 2 + c, h * 128:(h + 1) * 128], [RgB[5]]))
                    attn(steps, 256, catT[:, 4 + h, qs], [catB[4 + h]], gT[:, h, qs], [RgB[0]])
        else:
            ick = tmp()
            icv = tmp()
            ckT = Tp[ick][:].bitcast(BF16).rearrange("p (a t) -> p a t", a=4)
            cv = Tp[icv][:].bitcast(BF16).rearrange("p (a t) -> p a t", a=4)
            K.dma("pool", ckT, c_nakT[l], [], [TpB[ick]])
            K.dma("pool", cv, c_nav[l], [], [TpB[icv]])
            ibs = tmp()
            reserved.update([ick, icv, ibs])
            for h in range(4):
                bias = Tp[ibs][:].rearrange("p (a t) -> p a t", a=16)
                K.dma("sp", bias, rpbT[l, h], [], [TpB[ibs]])
                tt(bias, bias, maskS, ALU.add, [TpB[ibs], maskB], [TpB[ibs]])
                for qb in range(2):
                    qs = slice(qb * 512, (qb + 1) * 512)
                    steps = []
                    for c in range(4):
                        steps.append((ckT[:, h, c * 128:(c + 1) * 128], [TpB[ick]], qT[:, h, qs], [RgB[3]], 512, 0, None, [],
                                      cv[:, c, h * 128:(h + 1) * 128], [TpB[icv]]))
                    for r in range(8):
                        rr = qb * 8 + r
                        R0 = min(max(rr - 4, 0), 8)
                        q64 = qT[:, h, rr * 64:(rr + 1) * 64]
                        if R0 % 2 == 0:
                            lst = [(R0 // 2 + c, R0 + 2 * c - rr + 7) for c in range(4)]
                        else:
                            m0 = (R0 - 1) // 2
                            lst = [(m0, 14)] + [(m0 + c, 2 * (m0 + c) - rr + 7) for c in (1, 2, 3)] + [(m0 + 4, 15)]
                        bias2 = bias.rearrange("p (a two) t -> p a two t", two=2)

                        def grp(sub):
                            ms = [m for (m, _) in sub]
                            s0 = sub[0][1]
                            if len(sub) == 1:
                                bv = bias[:, s0, :]
                            else:
                                bv = bias2[:, s0 // 2: s0 // 2 + len(sub), s0 % 2, :]
                            steps.append(([kT[:, h, m * 128:(m + 1) * 128] for m in ms], [RgB[4]], q64, [RgB[3]], 64, r * 64,
                                          bv, [TpB[ibs]], [vv[:, m, h * 128:(h + 1) * 128] for m in ms], [RgB[5]]))
                        if R0 % 2 == 0:
                            grp(lst)
                        else:
                            grp(lst[0:1])
                            grp(lst[1:4])
                            grp(lst[4:5])
                    attn(steps, 512, catT[:, 4 + h, qs], [catB[4 + h]], gT[:, h, qs], [RgB[0]])
            reserved.clear()

    def branchC(l, u):
        pc = ppt[l % 2]
        pb = ppB[l % 2]
        qn, kn, vv, gT = R4(1), Rg[2][:, :2048].rearrange("p (a t) -> p a t", a=2), Rg[2][:, 2048:].rearrange("p (a t) -> p a t", a=8), R4(3)
        if u == 1:
            irc = tmp()
            irs = tmp()
            K.dma("sp", Tp[irc][:], ropeC, [], [TpB[irc]])
            K.dma("sp", Tp[irs][:], ropeS, [], [TpB[irs]])
            reserved.update([irc, irs])

        def normrope(ps, wcol, hh, out_bf, outB, dma_out):
            sl = slice(hh * 512, (hh + 1) * 512)
            ia = tmp()
            xf = Tp[ia][:, :512]
            sq = Tp[ia][:, 512:]
            act(xf, ps.ap, AF.Copy, [ps.b], [TpB[ia]])
            act(sq, ps.ap, AF.Square, [ps.b], [TpB[ia]])
            p2 = K.psum()
            mm(p2.ap, ONf, sq, True, True, [TpB[ia], csB], [p2.b], True)
            rsq(sq, p2.ap, 1.0 / 128, [p2.b], [TpB[ia]])
            stt(xf, xf, wcol, sq, ALU.mult, ALU.mult, [TpB[ia], pb], [TpB[ia]])
            if dma_out is not None:
                K.dma("sp", dma_out, xf, [TpB[ia]], [])
            if u == 0:
                cp(out_bf, xf, [TpB[ia]], outB)
            else:
                p3 = K.psum()
                mm(p3.ap, PMf, xf, True, True, [TpB[ia], csB], [p3.b], True)
                tt(sq, p3.ap, Tp[irs][:, sl], ALU.mult, [p3.b, TpB[irs]], [TpB[ia]])
                tt(xf, xf, Tp[irc][:, sl], ALU.mult, [TpB[ia], TpB[irc]], [TpB[ia]])
                tt(out_bf, xf, sq, ALU.add, [TpB[ia]], outB)

        qw = PP["qnw"][0]
        kw = PP["knw"][0]
        s = w_get()
        proj_fm(s, 0, 512, lambda j, hh, ps: normrope(ps, pc[:, qw:qw + 1], hh, qn[:, j, hh * 512:(hh + 1) * 512], [RgB[1]], None))
        s = w_get()
        proj_fm(s, 0, 256, lambda j, hh, ps: normrope(ps, pc[:, kw:kw + 1], hh, kn[:, j, hh * 512:(hh + 1) * 512], [RgB[2]],
                                                       o_gk[l, :, j, hh * 512:(hh + 1) * 512] if u == 0 else None))

        def ev_v(t8, ps):
            if u == 0:
                a = tmp()
                act(Tp[a][:, :256], ps.ap[:, :256], AF.Copy, [ps.b], [TpB[a]])
                K.dma("sp", o_gv[l, :, t8, :], Tp[a][:, :256], [TpB[a]], [])
                cp(vv[:, t8, :], Tp[a][:, :256], [TpB[a]], [RgB[2]])
            else:
                cp(vv[:, t8, :], ps.ap[:, :256], [ps.b], [RgB[2]])
        proj_tm(s, 256, 256, ev_v)
        s = w_get()
        proj_fm(s, 0, 512, lambda j, hh, ps: act(gT[:, j, hh * 512:(hh + 1) * 512], ps.ap, AF.Silu, [ps.b], [RgB[3]]))
        if u == 0:
            for sq in range(4):
                for h in range(4):
                    kv = h // 2
                    qs = slice(sq * 256, (sq + 1) * 256)
                    steps = []
                    for c in range(2):
                        t0 = sq * 256 + c * 128
                        steps.append((kn[:, kv, t0:t0 + 128], [RgB[2]], qn[:, h, qs], [RgB[1]], 256, 0, None, [],
                                      vv[:, sq * 2 + c, kv * 128:(kv + 1) * 128], [RgB[2]]))
                    attn(steps, 256, catT[:, 8 + h, qs], [catB[8 + h]], gT[:, h, qs], [RgB[3]])
        else:
            ick = tmp()
            ckT = Tp[ick][:, :512].bitcast(BF16).rearrange("p (a t) -> p a t", a=2)
            cv = Tp[ick][:, 512:].bitcast(BF16).rearrange("p (a t) -> p a t", a=4)
            K.dma("pool", ckT, c_gkT[l], [], [TpB[ick]])
            K.dma("pool", cv, c_gv[l], [], [TpB[ick]])
            reserved.add(ick)
            for h in range(4):
                kv = h // 2
                for qb in range(2):
                    qs = slice(qb * 512, (qb + 1) * 512)
                    steps = []
                    for c in range(8):
                        steps.append((kn[:, kv, c * 128:(c + 1) * 128], [RgB[2]], qn[:, h, qs], [RgB[1]], 512, 0, None, [],
                                      vv[:, c, kv * 128:(kv + 1) * 128], [RgB[2]]))
                    for c in range(4):
                        steps.append((ckT[:, kv, c * 128:(c + 1) * 128], [TpB[ick]], qn[:, h, qs], [RgB[1]], 512, 0, None, [],
                                      cv[:, c, kv * 128:(kv + 1) * 128], [TpB[ick]]))
                    attn(steps, 512, catT[:, 8 + h, qs], [catB[8 + h]], gT[:, h, qs], [RgB[3]])
            reserved.clear()

    def branchD(l, u):
        pc = ppt[l % 2]
        pb = ppB[l % 2]
        qe, qo, kT, ktm, vv, og = R4(4), R4(5), R4(0), R8(1), R8(2), R8(3)
        K.op("dve", lambda: nc.vector.memset(Rg[4][:], 0.0), [], [RgB[4]], c=2.0)
        K.op("dve", lambda: nc.vector.memset(Rg[5][:], 0.0), [], [RgB[5]], c=2.0)
        s = w_get()

        def ev_q(j, hh, ps):
            pv = ps.ap.rearrange("p (t c k) -> p t c k", t=4, c=2)
            qev = qe[:, j, hh * 512:(hh + 1) * 512].rearrange("p (t c k) -> p t c k", t=4, c=2)
            qov = qo[:, j, hh * 512:(hh + 1) * 512].rearrange("p (t c k) -> p t c k", t=4, c=2)
            ts(qev[:, :, 0, :], pv[:, :, 0, :], SCL, None, ALU.mult, None, [ps.b], [RgB[4]])
            ts(qov[:, :, 1, :], pv[:, :, 1, :], SCL, None, ALU.mult, None, [ps.b], [RgB[5]])
        proj_fm(s, 0, 512, ev_q)
        s = w_get()
        proj_fm(s, 0, 512, lambda j, hh, ps: cp(kT[:, j, hh * 512:(hh + 1) * 512], ps.ap, [ps.b], [RgB[0]]))
        for t8 in range(8):
            pt = K.psum()
            ptb = pt.ap.bitcast(BF16)
            for h in range(4):
                K.op("pe", lambda h=h, t8=t8, ptb=ptb: nc.tensor.transpose(out=ptb[:, h * 128:(h + 1) * 128], in_=kT[:, h, t8 * 128:(t8 + 1) * 128], identity=IDb),
                     [RgB[0], csB], [pt.b], h == 3, c=0.1)
            cp(ktm[:, t8, :], ptb[:, 0:512], [pt.b], [RgB[1]])
        s = w_get()
        proj_tm(s, 0, 512, lambda t8, ps: cp(vv[:, t8, :], ps.ap, [ps.b], [RgB[2]]))
        s = w_get()
        proj_tm(s, 0, 512, lambda t8, ps: act(og[:, t8, :], ps.ap, AF.Sigmoid, [ps.b], [RgB[3]]))
        s = w_get()

        def ev_g(t8, ps):
            a = tmp()
            act(Tp[a][:, :512], ps.ap, AF.Silu, [ps.b], [TpB[a]])
            tt(og[:, t8, :], og[:, t8, :], Tp[a][:, :512], ALU.mult, [RgB[3], TpB[a]], [RgB[3]])
        proj_tm(s, 0, 512, ev_g)
        K.dma("pool", wif[:], w_ifT[l], [], [wifB])
        g0 = PP["gb"][0]
        proj_tm(None, 0, 16, lambda t8, ps: tt(gt[:, t8, :], ps.ap[:, :16], pc[:, g0:g0 + 16], ALU.add, [ps.b, pb], [gB]), wt=wif, wb=wifB)
        gv = gt[:].rearrange("p t (d w h) -> p d t w h", d=2, w=2)
        for dr in range(2):
            act(nlf[:, dr], gv[:, dr, :, 1, :], AF.Exp, [gB], [gB], scale=-1.0)
        ts(nlf[:], nlf[:], 1.0, None, ALU.add, None, [gB], [gB])
        act(nlf[:], nlf[:], AF.Ln, [gB], [gB])
        nlf2 = nlf[:].rearrange("p d t h -> p (d t h)")
        ps = K.psum()
        mm(ps.ap[:, 0:32], TRF, nlf2[:, 0:32], True, True, [gB, csB], [ps.b], False)
        mm(ps.ap[:, 32:64], TRB, nlf2[:, 32:64], True, True, [gB, csB], [ps.b], True)
        cp(nbt[:].rearrange("p d t h -> p (d t h)"), ps.ap[:, :64], [ps.b], [gB])
        ps = K.psum()
        mm(ps.ap[:, 0:64], H0, nlf2, True, True, [gB, csB], [ps.b], False)
        mm(ps.ap[:, 64:128], H1, nlf2, True, True, [gB, csB], [ps.b], True)
        act(eBt[:].rearrange("p c d t h -> p (c d t h)"), ps.ap[:, :128], AF.Exp, [ps.b], [gB], scale=-1.0)
        for dr in range(2):
            tt(eat[:, dr], gv[:, dr, :, 0, :], nbt[:, dr], ALU.add, [gB], [gB])
        if u == 0:
            nlv = nlf[:].rearrange("p d (s i) h -> p d i s h", i=2)
            pz = K.psum()

            def zc(i, d):
                return pz.ap[:, (i * 2 + d) * 16:(i * 2 + d + 1) * 16].rearrange("p (s h) -> p s h", s=4)
            mm(zc(0, 0), TFF, nlv[:, 0, 0], True, True, [gB, csB], [pz.b], False)
            mm(zc(1, 0), ONf, nlv[:, 0, 0], True, False, [gB, csB], [pz.b], False)
            mm(zc(1, 0), TFF, nlv[:, 0, 1], False, True, [gB, csB], [pz.b], False)
            mm(zc(1, 1), TFB, nlv[:, 1, 1], True, True, [gB, csB], [pz.b], False)
            mm(zc(0, 1), ONf, nlv[:, 1, 1], True, False, [gB, csB], [pz.b], False)
            mm(zc(0, 1), TFB, nlv[:, 1, 0], False, True, [gB, csB], [pz.b], True)
            liv = gt[:].rearrange("p (s i) (d w h) -> p i d s w h", i=2, d=2, w=2)
            for i in range(2):
                for d in range(2):
                    tt(zt[:, i, d * 16:(d + 1) * 16].rearrange("p (s h) -> p s h", s=4), liv[:, i, d, :, 0, :], zc(i, d), ALU.add, [gB, pz.b], [gB])
                    cp(nlfz[:, i, d * 16:(d + 1) * 16].rearrange("p (s h) -> p s h", s=4), nlv[:, d, i], [gB], [gB])
            pg = K.psum()
            pg2 = K.psum()
            for i in range(2):
                mm(pg.ap[0:32, i * 128:(i + 1) * 128], zt[:, i, :], IDf, True, True, [gB, csB], [pg.b], i == 1)
            for i in range(2):
                mm(pg2.ap[0:32, i * 128:(i + 1) * 128], nlfz[:, i, :], IDf, True, True, [gB, csB], [pg2.b], i == 1)
            K.op("dve", lambda: nc.vector.tensor_reduce(out=mfs[:, 0:1], in_=pg.ap[0:32, 0:256], axis=AX.X, op=ALU.max), [pg.b], [gB])
            K.op("dve", lambda: nc.vector.tensor_reduce(out=mfs[:, 1:2], in_=pg2.ap[0:32, 0:256], axis=AX.X, op=ALU.add), [pg2.b], [gB])
            ts(mfs[:, 2:3], mfs[:, 0:1], 0.0, mfs[:, 1:2], ALU.max, ALU.subtract, [gB], [gB])
            K.dma("sp", o_m[l], mfs[:, 2:3], [gB], [])
            act(mfs[:, 3:4], mfs[:, 2:3], AF.Exp, [gB], [gB], scale=-1.0)
            ia = tmp()
            ts(Tp[ia][0:32, 0:32], cs[0:32, 0, 0:32], mfs[:, 3:4], None, ALU.mult, None, [gB, csB], [TpB[ia]])
            pb2 = K.psum()
            mm(pb2.ap[:, 0:32], cs[0:32, 1, :], Tp[ia][0:32, 0:32], True, True, [TpB[ia], csB], [pb2.b], True)
            cp(emfbc[:], pb2.ap[:, 0:32], [pb2.b], [gB])
        act(enb[:], nbt[:], AF.Exp, [gB], [gB])
        act(eat[:], eat[:], AF.Exp, [gB], [gB])
        ts(ea01[:, 0], eat[:], cs[:, 7, 0:1], None, ALU.mult, None, [gB, csB], [gB])
        ts(ea01[:, 1], eat[:], cs[:, 8, 127:128], None, ALU.mult, None, [gB, csB], [gB])
        tt(ea01[:].rearrange("p c d t h -> p (c d t h)"), ea01[:].rearrange("p c d t h -> p (c d t h)"),
           eBt[:].rearrange("p c d t h -> p (c d t h)"), ALU.mult, [gB], [gB])
        if u == 1 and l + 1 < depth:
            mod_compute(l + 1)
        if u == 1:
            m00 = PP["m0"][0]
            act(em0[:], pc[:, m00:m00 + 8], AF.Exp, [pb], [gB])
            for dr in range(2):
                ia = tmp()
                c0 = Tp[ia][:, :516].rearrange("p (h k) -> p h k", h=4)
                K.dma("sp", c0, C0n0[l, :, dr], [], [TpB[ia]])
                for h in range(4):
                    ts(St[dr][h][:], c0[:, h, :], em0[:, dr * 4 + h: dr * 4 + h + 1], None, ALU.mult, None, [TpB[ia], gB], [StB[dr][h]])
                    act(Stb[dr][h][:], St[dr][h][:], AF.Copy, [StB[dr][h]], [StbB[dr][h]])
        nseq, tps = (4, 2) if u == 0 else (1, 8)
        won0 = PP["won"][0]
        hsI = [tmp() for _ in range(4)]
        reserved.update(hsI)

        def hs_tile(t8):
            return Tp[hsI[t8 // 2]][:, (t8 % 2) * 512:(t8 % 2 + 1) * 512].rearrange("p (h k) -> p h k", h=4), TpB[hsI[t8 // 2]]

        def instance(dr, h, t8, first_dir):
            tok = slice(t8 * 128, (t8 + 1) * 128)
            hsl = slice(h * 128, (h + 1) * 128)
            pss_ap, pss_b = K.pslot()
            mm(pss_ap[:, :128], kT[:, h, tok], qe[:, h, tok], True, False, [RgB[0], RgB[4]], [pss_b], False)
            mm(pss_ap[:, :128], kT[:, h, tok], qo[:, h, tok], False, True, [RgB[0], RgB[5]], [pss_b], True)
            ip = nxt("sm", NSM)
            stt(smt[ip][:], pss_ap[:, :128], eat[:, dr, t8, h:h + 1], MBF if dr == 0 else MBB, ALU.mult, ALU.mult, [pss_b, gB, csB], [smB[ip]])
            pn = K.psum()
            order = (0, 1) if dr == 0 else (1, 0)
            for ci, c in enumerate(order):
                qq = qe if c == 0 else qo
                mm(pn.ap[:, 0:129], qq[:, h, tok], Stb[dr][h][:], ci == 0, False, [RgB[4 + c], StbB[dr][h]], [pn.b], True)
                ik = nxt("sm", NSM)
                act(smt[ik][:], ktm[:, t8, hsl], AF.Copy, [RgB[1], gB], [smB[ik]], scale=ea01[:, c, dr, t8, h:h + 1])
                pu_ap, pu_b = K.pslot()
                mm(pu_ap[:, 0:128], smt[ik][:], vv[:, t8, hsl], True, True, [smB[ik], RgB[2]], [pu_b], False)
                mm(pu_ap[:, 128:129], smt[ik][:], ONb[:, 0:1], True, True, [smB[ik], csB], [pu_b], True)
                ebc = eBt[:, c, dr, t8, h:h + 1]
                stt(St[dr][h][:], St[dr][h][:], ebc, pu_ap[:, 0:129], ALU.mult, ALU.add, [pu_b, StB[dr][h], gB], [StB[dr][h]])
                act(Stb[dr][h][:], St[dr][h][:], AF.Copy, [StB[dr][h]], [StbB[dr][h]])
            mm(pn.ap[:, 0:128], smt[ip][:], vv[:, t8, hsl], False, False, [smB[ip], RgB[2]], [pn.b], False)
            mm(pn.ap[:, 128:129], smt[ip][:], ONb[:, 0:1], False, True, [smB[ip], csB], [pn.b], True)
            it = nxt("tiny", 6)
            act(tiny[it][:, 0:1], pn.ap[:, 128:129], AF.Abs, [pn.b], [tinyB[it]])
            ts(tiny[it][:, 0:1], tiny[it][:, 0:1], enb[:, dr, t8, h:h + 1], None, ALU.max, None, [tinyB[it], gB], [tinyB[it]])
            K.op("dve", lambda: nc.vector.reciprocal(out=tiny[it][:, 1:2], in_=tiny[it][:, 0:1]), [tinyB[it]], [tinyB[it]])
            hv, hb = hs_tile(t8)
            if first_dir:
                act(hv[:, h, :], pn.ap[:, 0:128], AF.Copy, [pn.b, tinyB[it]], [hb], scale=tiny[it][:, 1:2])
            else:
                stt(hv[:, h, :], pn.ap[:, 0:128], tiny[it][:, 1:2], hv[:, h, :], ALU.mult, ALU.add, [pn.b, tinyB[it], hb], [hb])

        def finish_tile(t8):
            hv, hb = hs_tile(t8)
            ia = tmp()
            sq = Tp[ia][:, :512].rearrange("p (h k) -> p h k", h=4)
            it = nxt("tiny", 6)
            tt(sq, hv, hv, ALU.mult, [hb], [TpB[ia]])
            K.op("dve", lambda: nc.vector.tensor_reduce(out=tiny[it][:, 0:4], in_=sq, axis=AX.X, op=ALU.add), [TpB[ia]], [tinyB[it]])
            rsq(tiny[it][:, 0:4], tiny[it][:, 0:4], 1.0 / 128, [tinyB[it]], [tinyB[it]])
            for h in range(4):
                stt(sq[:, h, :], hv[:, h, :], tiny[it][:, h:h + 1], pc[:, won0 + h * 128: won0 + (h + 1) * 128], ALU.mult, ALU.mult,
                    [hb, tinyB[it], pb, TpB[ia]], [TpB[ia]])
            od = Tp[ia][:, 512:].bitcast(BF16)[:, :512]
            tt(od, Tp[ia][:, :512], og[:, t8, :], ALU.mult, [TpB[ia], RgB[3]], [TpB[ia]])
            pt = K.psum()
            ptb = pt.ap.bitcast(BF16)
            for h in range(4):
                K.op("pe", lambda h=h: nc.tensor.transpose(out=ptb[:, h * 128:(h + 1) * 128], in_=od[:, h * 128:(h + 1) * 128], identity=IDb),
                     [TpB[ia], csB], [pt.b], h == 3)
            act(catT[:, 12:16, t8 * 128:(t8 + 1) * 128], ptb[:, 0:512].rearrange("p (h k) -> p h k", h=4), AF.Copy, [pt.b], [catB[12 + i] for i in range(4)])

        for sq in range(nseq):
            if u == 0:
                for dr in range(2):
                    for h in range(4):
                        K.op("dve", lambda dr=dr, h=h: nc.vector.memset(St[dr][h][:], 0.0), [], [StB[dr][h]])
                        K.op("dve", lambda dr=dr, h=h: nc.vector.memset(Stb[dr][h][:], 0.0), [], [StbB[dr][h]])
            done = {}
            for i in range(tps):
                for dr in range(2):
                    t8 = sq * tps + (i if dr == 0 else tps - 1 - i)
                    for h in range(4):
                        instance(dr, h, t8, t8 not in done)
                    done[t8] = done.get(t8, 0) + 1
                    if done[t8] == 2:
                        finish_tile(t8)
            if u == 0:
                for dr in range(2):
                    for h in range(4):
                        ig = nxt("stg", 2)
                        c = dr * 16 + sq * 4 + h
                        ts(stg[ig][:], St[dr][h][:], emfbc[:, c:c + 1], None, ALU.mult, None, [StB[dr][h], gB], [stgB[ig]])
                        K.dma("sp", o_C[l, :, dr, sq, h, :], stg[ig][:], [stgB[ig]], [])
        reserved.clear()

    def phaseW(l, u, last_layer):
        ssp = [K.pin(), K.pin()]
        for t in range(4):
            s = w_get()
            for j in range(4):
                fc = t * 4 + j
                for hh in range(2):
                    sl = slice(hh * 512, (hh + 1) * 512)
                    ps = K.ppsum()
                    for kc in range(16):
                        mm(ps.ap, Wt[s][:, kc, j * 128:(j + 1) * 128], catT[:, kc, sl], kc == 0, kc == 15, [Wb[s][kc // 4], catB[kc]], [ps.b], kc == 15)
                    a = tmp()
                    rds = [] if l == 0 else [xsb[u][fc][hh]]
                    K.dma("sp", Tp[a][:, :512], xsrc(l, u, fc)[:, sl], rds, [TpB[a]])
                    stt(Tp[a][:, :512], ps.ap, modL[l % 2][:, 32 + fc, u:u + 1], Tp[a][:, :512], ALU.mult, ALU.add, [ps.b, modBL[l % 2], TpB[a]], [TpB[a]])
                    K.dma("sp", xsc[u, fc][:, sl], Tp[a][:, :512], [TpB[a]], [xsb[u][fc][hh]])
                    sqb = Tp[a][:, 512:].bitcast(BF16)[:, :512]
                    act(sqb, Tp[a][:, :512], AF.Square, [TpB[a]], [TpB[a]])
                    mm(ssp[hh].ap, ONb, sqb, fc == 0, fc == 15, [TpB[a], csB], [ssp[hh].b], True)
        compute_rstd(u, ssp)
        K.unpin(ssp[0])
        K.unpin(ssp[1])
        if last_layer:
            for fc in range(16):
                a = tmp()
                K.dma("sp", Tp[a][:], xsc[u, fc], [xsb[u][fc][0], xsb[u][fc][1]], [TpB[a]])
                tt(Tp[a][:], Tp[a][:], rstd[:, u, :], ALU.mult, [TpB[a], rstdB[u]], [TpB[a]])
                ts(Tp[a][:], Tp[a][:], fnw_sb[:, fc:fc + 1], None, ALU.mult, None, [TpB[a], scB], [TpB[a]])
                K.dma("sp", yT[u, fc], Tp[a][:], [TpB[a]], [])

    def mod_compute(l):
        mod, gmod, modB = modL[l % 2], gmodL[l % 2], modBL[l % 2]
        K.dma("sp", abt[:], ppd[l][:, 0:64], [], [abtB])
        mp = K.pin()
        for t in range(12):
            s = w_get()
            for j in range(4):
                cc = t * 4 + j
                for kc in range(16):
                    mm(mp.ap[:, cc * 2:cc * 2 + 2], Wt[s][:, kc, j * 128:(j + 1) * 128], sc_bf[:, kc, :], kc == 0, kc == 15,
                       [Wb[s][kc // 4], scB], [mp.b], j == 3 and kc == 15)
        mpv = mp.ap[:, 0:96].rearrange("p (c u) -> p c u", u=2)
        for u in range(2):
            tt(mod[:, :, u], mpv[:, :, u], abt[:, 0:48], ALU.add, [mp.b, abtB], [modB])
            stt(gmod[:, :, u], mod[:, 16:32, u], 1.0, abt[:, 48:64], ALU.add, ALU.mult, [modB, abtB], [modB])
        K.unpin(mp)

    for u in range(2):
        prologue(u)
    K.marks.append((len(K.nodes), "L0 mod", K.npe))
    if stage > 0:
        mod_compute(0)
    for l in range(depth if stage > 0 else 0):
        pc = ppt[l % 2]
        pb = ppB[l % 2]
        K.dma("sp", pc[:], ppd[l], [], [pb])
        lm0 = PP["lam"][0]
        act(clam[:], pc[:, lm0:lm0 + 8], AF.Exp, [pb], [clamB], scale=-1.0)
        ts(clam[:], clam[:], 1.0, None, ALU.add, None, [clamB], [clamB])
        act(clam[:], clam[:], AF.Ln, [clamB], [clamB])
        ts(clam[:], clam[:], -8.0, None, ALU.mult, None, [clamB], [clamB])
        for u in range(2 if stage > 1 else 0):
            K.marks.append((len(K.nodes), "L%d u%d N" % (l, u), K.npe))
            phaseN(l, u, pc)
            K.marks.append((len(K.nodes), "L%d u%d A" % (l, u), K.npe))
            if stage > 2:
                branchA(l, u)
            else:
                [w_get() for _ in range(2)]
            K.marks.append((len(K.nodes), "L%d u%d B" % (l, u), K.npe))
            if stage > 3:
                branchB(l, u)
            else:
                [w_get() for _ in range(4)]
            K.marks.append((len(K.nodes), "L%d u%d C" % (l, u), K.npe))
            if stage > 4:
                branchC(l, u)
            else:
                [w_get() for _ in range(3)]
            K.marks.append((len(K.nodes), "L%d u%d D" % (l, u), K.npe))
            if stage > 5:
                branchD(l, u)
            else:
                [w_get() for _ in range(5)]
            if dbg and l == 0 and stage <= 6:
                for kc in range(16):
                    a = tmp()
                    cp(Tp[a][:], catT[:, kc, :], [catB[kc]], [TpB[a]])
                    K.dma("sp", o_dbg[u, kc], Tp[a][:], [TpB[a]], [])
            if stage <= 6:
                [w_get() for _ in range(4)]
                continue
            if dbg and l == 0:
                for kc in range(16):
                    a = tmp()
                    cp(Tp[a][:], catT[:, kc, :], [catB[kc]], [TpB[a]])
                    K.dma("sp", o_dbg[u, kc], Tp[a][:], [TpB[a]], [])
            K.marks.append((len(K.nodes), "L%d u%d W" % (l, u), K.npe))
            phaseW(l, u, l == depth - 1)
    K.marks.append((len(K.nodes), "end", K.npe))
    K.finish()
    return nc, K


def _consts():
    c = np.zeros((11, 128, 128), np.float32)
    i = np.arange(128)
    c[0] = np.eye(128)
    c[1] = 1.0
    partner = np.where((i % 64) < 32, i + 32, i - 32)
    c[2][partner, i] = 1.0
    same = (i[:, None] // 64) == (i[None, :] // 64)
    le = i[:, None] <= i[None, :]
    ge = i[:, None] >= i[None, :]
    c[3] = (same & le)
    c[4] = (same & ge)
    c[5] = le
    c[6] = ge
    c[7] = (i[:, None] < 64) * np.ones((1, 128))
    c[8] = (i[:, None] >= 64) * np.ones((1, 128))
    c[9] = (same & le)
    c[10] = (same & ge)
    return np.ascontiguousarray(c.transpose(1, 0, 2))


def _rope_tables():
    t = np.arange(NT)
    d = np.arange(128)
    pos = np.where(d[:, None] < 64, (t // 64)[None, :], (t % 64)[None, :]).astype(np.float32)
    nf = 32
    inv = (np.float32(10000.0) ** (-(np.arange(nf, dtype=np.float32)) / np.float32(nf))).astype(np.float32)
    ang = pos * inv[d % 32][:, None]
    C = np.cos(ang).astype(np.float32)
    S = np.sin(ang).astype(np.float32)
    sign = np.where((d % 64) < 32, -1.0, 1.0).astype(np.float32)
    return C, (S * sign[:, None]).astype(np.float32)


def _na_tables(rpb_l):
    col = np.arange(64)
    qcol = np.arange(64)
    qs = np.clip(qcol - 8, 0, 48)
    valid = (col[:, None] >= qs[None, :]) & (col[:, None] < qs[None, :] + 16)
    coff = np.clip(col[:, None] - qcol[None, :] + 15, 0, 30)
    g = np.zeros((4, 128, 16, 64), np.float32)
    m = np.zeros((128, 16, 64), np.float32)
    for s in range(16):
        for jj in range(2):
            if s < 14:
                ro, ok = s + jj, True
            elif s == 14:
                ro, ok = 2 + jj, jj == 1
            else:
                ro, ok = 10 + jj, jj == 0
            g[:, jj * 64:(jj + 1) * 64, s, :] = rpb_l[:, ro][:, coff]
            m[jj * 64:(jj + 1) * 64, s, :] = np.where(valid & ok, 0.0, -1e30)
    return g, m


_CACHE = {}


def _get_nc(depth=L, dbg=False):
    key = (depth, dbg)
    if key not in _CACHE:
        _CACHE[key] = build(depth, dbg)[0]
    return _CACHE[key]


def _tile_w(w, ntile):
    return np.ascontiguousarray(w.reshape(16, 128, ntile, 512).transpose(2, 1, 0, 3))


def prep_inputs(inp, depth=L):
    f = lambda a: np.ascontiguousarray(np.asarray(a, dtype=np.float32))
    g = {k: np.asarray(v) for k, v in inp.items()}
    sh = {}
    sh["ada_wT"] = np.stack([_tile_w(g["ada_w"][l], 12) for l in range(L)])
    sh["w_inT"] = np.stack([_tile_w(g["w_in"][l][:, :7168], 14) for l in range(L)])
    sh["w_ifT"] = f(g["w_in"][:, :, 7168:].reshape(L, 16, 128, 16).transpose(0, 2, 1, 3))
    sh["w_outT"] = np.stack([_tile_w(g["w_out"][l], 4) for l in range(L)])
    sh["fnw"] = f(g["final_norm_w"].reshape(16, 128).T)
    lw = np.stack([g["lru_wr"], g["lru_wi"]], axis=2)
    sh["lru_wT"] = f(lw.transpose(0, 4, 1, 2, 3, 5).reshape(L, 128, 16, 128))
    gm = [_na_tables(g["na_rpb"][l]) for l in range(L)]
    sh["rpbT"] = f(np.stack([x[0] for x in gm]))
    sh["maskT"] = f(gm[0][1])
    C, S = _rope_tables()
    sh["ropeC"], sh["ropeS"] = f(C), f(S)
    sh["cst"] = f(_consts())
    maps = []
    for c in range(NCORE):
        m = dict(sh)
        xp = g["x_prompt"][4 * c:4 * c + 4].reshape(NT, D)
        xs = g["x_sample"][c]
        m["x0T"] = f(np.stack([xp.T.reshape(16, 128, NT), xs.T.reshape(16, 128, NT)]))
        cond = np.stack([g["c_ctx"], g["c"][c]], axis=-1)
        m["condT"] = f(cond.reshape(16, 128, 2).transpose(1, 0, 2))
        pp = np.zeros((L, 128, NPP), np.float32)

        def put(name, arr):
            o, w = PP[name]
            pp[:, :, o:o + w] = arr.reshape(L, 128, w)
        put("ada_b", g["ada_b"].reshape(L, 48, 128).transpose(0, 2, 1))
        put("norm_w", g["norm_w"].reshape(L, 16, 128).transpose(0, 2, 1))
        put("conv_w", g["lru_conv_w"].reshape(L, 4, 4, 128).transpose(0, 3, 2, 1))
        put("conv_b", g["lru_conv_b"].reshape(L, 4, 128).transpose(0, 2, 1))
        lb = np.stack([g["lru_br"], g["lru_bi"]], axis=2)
        put("lru_b", lb.reshape(L, 2, 2, 4, 128).transpose(0, 4, 1, 2, 3))
        put("lam", g["lru_lambda"].reshape(L, 2, 4, 128).transpose(0, 3, 1, 2))
        put("qnw", g["gqa_qnorm"].reshape(L, 128, 1))
        put("knw", g["gqa_knorm"].reshape(L, 128, 1))
        put("gb", np.broadcast_to(g["ml_gate_b"][:, None, :], (L, 128, 16)))
        put("won", np.broadcast_to(g["ml_out_norm"][:, None, :], (L, 128, 512)))
        put("h0", g["state_lru"][c].reshape(L, 2, 4, 128).transpose(0, 3, 1, 2))
        put("m0", np.broadcast_to(g["state_mlstm_m"][c].reshape(L, 1, 8), (L, 128, 8)))
        m["pp"] = pp
        m["c_nakT"] = f(g["cache_na_k"][c].transpose(0, 3, 2, 1))
        m["c_nav"] = f(g["cache_na_v"][c].reshape(L, 4, 128, 512).transpose(0, 2, 1, 3))
        m["c_gkT"] = f(g["cache_gqa_k"][c].transpose(0, 3, 2, 1))
        m["c_gv"] = f(g["cache_gqa_v"][c].reshape(L, 4, 128, 256).transpose(0, 2, 1, 3))
        Cn = np.concatenate([g["state_mlstm_C"][c], g["state_mlstm_n"][c][..., None]], axis=-1)
        m["C0n0"] = f(Cn.transpose(0, 3, 1, 2, 4))
        maps.append(m)
    return maps


def assemble(res):
    B = 32
    y_p = np.zeros((B, 256, D), np.float32)
    y_s = np.zeros((8, NT, D), np.float32)
    nak = np.zeros((B, L, 256, 4, 128), np.float32)
    nav = np.zeros((B, L, 256, 4, 128), np.float32)
    gk = np.zeros((B, L, 256, 2, 128), np.float32)
    gv = np.zeros((B, L, 256, 2, 128), np.float32)
    lru = np.zeros((B, L, 2, 512), np.float32)
    Cm = np.zeros((B, L, 2, 4, 128, 128), np.float32)
    nm = np.zeros((B, L, 2, 4, 128), np.float32)
    mm_ = np.zeros((B, L, 2, 4), np.float32)
    for c, r in enumerate(res):
        bs = slice(4 * c, 4 * c + 4)
        yT = r["yT"]
        y_p[bs] = yT[0].reshape(D, NT).T.reshape(4, 256, D)
        y_s[c] = yT[1].reshape(D, NT).T
        nak[bs] = r["o_nak"].reshape(L, 128, 4, 4, 256).transpose(3, 0, 4, 2, 1)
        nav[bs] = r["o_nav"].transpose(0, 2, 1, 3).reshape(L, 4, 256, 4, 128).transpose(1, 0, 2, 3, 4)
        gk[bs] = r["o_gk"].reshape(L, 128, 2, 4, 256).transpose(3, 0, 4, 2, 1)
        gv[bs] = r["o_gv"].transpose(0, 2, 1, 3).reshape(L, 4, 256, 2, 128).transpose(1, 0, 2, 3, 4)
        lru[bs] = r["o_lru"].transpose(4, 0, 2, 3, 1).reshape(4, L, 2, 512)
        oc = r["o_C"].transpose(3, 0, 2, 4, 1, 5)
        Cm[bs] = oc[..., :128]
        nm[bs] = oc[..., 128]
        mm_[bs] = r["o_m"].reshape(L, 2, 4, 4).transpose(2, 0, 1, 3)
    return (y_p, y_s, nak, nav, gk, gv, lru, Cm, nm, mm_)


def kernel(**inputs):
    nc = _get_nc()
    maps = prep_inputs(inputs)
    out = run_bass_kernel_spmd(nc, maps, core_ids=list(range(NCORE)))
    return assemble(out.results)
```

```python
import numpy as np
import concourse.bass as bass
import concourse.mybir as mybir
from concourse.bass_utils import run_bass_kernel_spmd
from concourse.ap import AP

F32 = mybir.dt.float32
BF16 = mybir.dt.bfloat16
ALU = mybir.AluOpType
AF = mybir.ActivationFunctionType
AX = mybir.AxisListType

L = 4
D = 2048
NT = 1024
NCORE = 8
EPS = 1e-6
SCL = 128.0 ** -0.5
PP = {}
_o = 0
for _n, _w in (("ada_b", 48), ("norm_w", 16), ("conv_w", 16), ("conv_b", 4), ("lru_b", 16), ("lam", 8),
               ("qnw", 1), ("knw", 1), ("gb", 16), ("won", 512), ("h0", 8), ("m0", 8)):
    PP[_n] = (_o, _w)
    _o += _w
NPP = _o


class Buf:
    __slots__ = ("name", "w", "r")

    def __init__(self, name):
        self.name = name
        self.w = None
        self.r = []


class PS:
    def __init__(self, t, b):
        self.t = t
        self.b = b
        self.ap = t[:]


class Node:
    __slots__ = ("i", "e", "fns", "deps", "dur", "lat", "kind", "args", "succ", "nd", "ready", "fin", "tag", "raw", "fs")

    def __init__(self, i, e, kind):
        self.i = i
        self.e = e
        self.kind = kind
        self.fns = []
        self.deps = set()
        self.raw = set()
        self.fs = None
        self.dur = 0.0
        self.lat = 0.0
        self.args = None
        self.succ = []
        self.tag = None


class KB:
    def __init__(self, nc):
        self.nc = nc
        self.eng = {"pe": nc.tensor, "act": nc.scalar, "dve": nc.vector, "pool": nc.gpsimd, "sp": nc.sync}
        self.sem = {k: nc.alloc_semaphore("s_" + k) for k in ("pe", "act", "dve", "pool")}
        self.cnt = {k: 0 for k in self.sem}
        self.seen = {k: {} for k in self.eng}
        self.dsem = []
        self.dval = []
        self.dq = {"sp": [], "pool": [], "act": []}
        self.dqi = {"sp": 0, "pool": 0, "act": 0}
        for q, n in (("sp", 14), ("pool", 8), ("act", 2)):
            for i in range(n):
                self.dq[q].append(len(self.dsem))
                self.dsem.append(nc.alloc_semaphore("d_%s%d" % (q, i)))
                self.dval.append(0)
        self.banks = []
        for i in range(8):
            t = nc.alloc_psum_tensor("ps%d" % i, [128, 512], F32)
            self.banks.append(PS(t, Buf("ps%d" % i)))
        self.pinned = set()
        self.slots = []
        for b in (6, 7):
            for j in range(3):
                self.slots.append((self.banks[b].t[:, j * 170:(j + 1) * 170], Buf("slot%d_%d" % (b, j))))
        self.si = 0
        self.pbi = 0
        self.bi = 0
        self.nins = 0
        self.npe = 0
        self.marks = []
        self.nodes = []
        self.open_pe = None

    def _deps(self, node, reads, writes):
        d = node.deps
        for b in reads:
            if b.w is not None:
                d.add(b.w)
                node.raw.add(b.w)
        for b in writes:
            if b.w is not None:
                d.add(b.w)
            d.update(b.r)
        for b in reads:
            b.r.append(node.i)
        for b in writes:
            b.w = node.i
            b.r = []

    def op(self, e, fn, reads=(), writes=(), inc=True, c=0.1, fs=None):
        self.nins += 1
        if e == "pe":
            self.npe += 1
            if self.open_pe is not None:
                node = self.open_pe
            else:
                node = Node(len(self.nodes), e, "op")
                self.nodes.append(node)
            node.fns.append(fn)
            node.dur += c
            self._deps(node, reads, writes)
            node.deps.discard(node.i)
            self.open_pe = None if inc else node
            return
        node = Node(len(self.nodes), e, "op")
        self.nodes.append(node)
        node.fns.append(fn)
        node.dur = c
        node.fs = fs
        self._deps(node, reads, writes)

    def dma(self, q, out, in_, reads=(), writes=(), nbytes=65536):
        self.nins += 1
        node = Node(len(self.nodes), q, "dma")
        self.nodes.append(node)
        node.args = (out, in_)
        node.dur = 0.06
        node.lat = 2.0 + nbytes / 150e3
        self._deps(node, reads, writes)

    def schedule(self):
        import heapq
        nodes = self.nodes
        for n in nodes:
            n.nd = len(n.deps)
            n.ready = 0.0
        for n in nodes:
            for d in n.deps:
                nodes[d].succ.append(n.i)
        XL = 0.35
        tail = [0.0] * len(nodes)
        for n in reversed(nodes):
            t = 0.0
            for si in n.succ:
                x = tail[si] + ((0.25 if n.e == "dve" else 0.05) if nodes[si].e == n.e else XL)
                if x > t:
                    t = x
            tail[n.i] = t + n.dur + n.lat
        engs = list(self.eng.keys())
        waiting = {e: [] for e in engs}
        avail = {e: [] for e in engs}
        free_at = {e: 0.0 for e in engs}
        for n in nodes:
            if n.nd == 0:
                heapq.heappush(waiting[n.e], (0.0, n.i))
        order = []
        cur_fs = [None]
        remaining = len(nodes)
        while remaining:
            best = None
            for e in engs:
                w = waiting[e]
                a = avail[e]
                fa = free_at[e]
                while w and w[0][0] <= fa:
                    wi = heapq.heappop(w)[1]
                    heapq.heappush(a, (-tail[wi], wi))
                if a:
                    cand = (fa, a[0][1], e, True)
                elif w:
                    cand = (w[0][0], w[0][1], e, False)
                else:
                    continue
                if best is None or cand < best:
                    best = cand
            st, i, e, fromavail = best
            if fromavail:
                if e == "act":
                    a = avail[e]
                    cand_l = [heapq.heappop(a) for _ in range(min(8, len(a)))]
                    pick = 0
                    for ci_, (_, ni_) in enumerate(cand_l):
                        if nodes[ni_].fs is None or nodes[ni_].fs == cur_fs[0]:
                            pick = ci_
                            break
                    i = cand_l[pick][1]
                    for ci_, it_ in enumerate(cand_l):
                        if ci_ != pick:
                            heapq.heappush(a, it_)
                else:
                    heapq.heappop(avail[e])
            else:
                heapq.heappop(waiting[e])
            n = nodes[i]
            if e == "act" and n.fs is not None:
                if n.fs != cur_fs[0]:
                    st += 1.3
                cur_fs[0] = n.fs
            free_at[e] = st + n.dur
            n.fin = st + n.dur + n.lat
            order.append(i)
            remaining -= 1
            for si in n.succ:
                sn = nodes[si]
                r = n.fin + ((0.25 if e == "dve" else 0.05) if sn.e == e else XL)
                if r > sn.ready:
                    sn.ready = r
                sn.nd -= 1
                if sn.nd == 0:
                    heapq.heappush(waiting[sn.e], (sn.ready, si))
        self.makespan = max(free_at.values())
        return order

    def _wait(self, e, key, val):
        if self.seen[e].get(key, 0) >= val:
            return
        sem = self.sem[key] if isinstance(key, str) else self.dsem[key[1]]
        self.eng[e].wait_ge(sem, val)
        self.seen[e][key] = val

    def finish(self, sched=True):
        assert self.open_pe is None
        nodes = self.nodes
        order = self.schedule() if sched else list(range(len(nodes)))
        for i in order:
            n = nodes[i]
            e = n.e
            need = {}
            for d in n.deps:
                k, v = nodes[d].tag
                if k == e and (e == "pe" or (e == "act" and d not in n.raw)):
                    continue
                if need.get(k, 0) < v:
                    need[k] = v
            for k, v in need.items():
                self._wait(e, k, v)
            if n.kind == "dma":
                lst = self.dq[e]
                si = lst[self.dqi[e] % len(lst)]
                self.dqi[e] += 1
                self._wait(e, ("d", si), self.dval[si])
                ins = self.eng[e].dma_start(out=n.args[0], in_=n.args[1])
                self.dval[si] += 16
                ins.then_inc(self.dsem[si], 16)
                n.tag = (("d", si), self.dval[si])
            else:
                ins = None
                for fn in n.fns:
                    ins = fn()
                self.cnt[e] += 1
                ins.then_inc(self.sem[e], 1)
                n.tag = (e, self.cnt[e])
        for si in range(len(self.dsem)):
            if self.dval[si]:
                self._wait("sp", ("d", si), self.dval[si])
        for k in self.sem:
            if self.cnt[k]:
                self._wait("sp", k, self.cnt[k])

    def psum(self):
        for _ in range(16):
            i = 2 + self.bi % 6
            self.bi += 1
            if i not in self.pinned:
                return self.banks[i]
        raise RuntimeError("no psum")

    def ppsum(self):
        i = self.pbi % 2
        self.pbi += 1
        return self.banks[i]

    def pslot(self):
        p = self.psum()
        return p.ap, p.b

    def pin(self):
        p = self.psum()
        self.pinned.add(self.banks.index(p))
        return p

    def unpin(self, p):
        self.pinned.discard(self.banks.index(p))


def rev(a):
    pairs = [list(p) for p in a.ap]
    st, n = pairs[-1]
    pairs[-1] = [-st, n]
    return AP(a.tensor, a.offset + st * (n - 1), pairs)


def build(depth=L, dbg=False, stage=99):
    nc = bass.Bass("TRN2", target_bir_lowering=False)
    K = KB(nc)
    E = K.eng

    def din(name, shape):
        return nc.dram_tensor(name, list(shape), F32, kind="ExternalInput").ap()

    def dout(name, shape):
        return nc.dram_tensor(name, list(shape), F32, kind="ExternalOutput").ap()

    x0T = din("x0T", [2, 16, 128, NT])
    condT = din("condT", [128, 16, 2])
    ada_wT = din("ada_wT", [L, 12, 128, 16, 512])
    w_inT = din("w_inT", [L, 14, 128, 16, 512])
    w_ifT = din("w_ifT", [L, 128, 16, 16])
    w_outT = din("w_outT", [L, 4, 128, 16, 512])
    ppd = din("pp", [L, 128, NPP])
    fnw = din("fnw", [128, 16])
    lru_wT = din("lru_wT", [L, 128, 16, 128])
    rpbT = din("rpbT", [L, 4, 128, 16, 64])
    maskT = din("maskT", [128, 16, 64])
    c_nakT = din("c_nakT", [L, 128, 4, 512])
    c_nav = din("c_nav", [L, 128, 4, 512])
    c_gkT = din("c_gkT", [L, 128, 2, 512])
    c_gv = din("c_gv", [L, 128, 4, 256])
    ropeC = din("ropeC", [128, NT])
    ropeS = din("ropeS", [128, NT])
    cst = din("cst", [128, 11, 128])
    C0n0 = din("C0n0", [L, 128, 2, 4, 129])
    yT = dout("yT", [2, 16, 128, NT])
    o_nak = dout("o_nak", [L, 128, 4, NT])
    o_nav = dout("o_nav", [L, 128, 8, 512])
    o_gk = dout("o_gk", [L, 128, 2, NT])
    o_gv = dout("o_gv", [L, 128, 8, 256])
    o_lru = dout("o_lru", [L, 128, 2, 4, 4])
    o_C = dout("o_C", [L, 128, 2, 4, 4, 129])
    o_m = dout("o_m", [L, 32, 1])
    xsc = nc.dram_tensor("xsc", [2, 16, 128, NT], F32, kind="Internal").ap()
    if dbg:
        o_dbg = dout("o_dbg", [2, 16, 128, NT])
    xsb = [[[Buf("xs") for _ in range(2)] for _ in range(16)] for _ in range(2)]

    def sb(name, shape, dt):
        return nc.alloc_sbuf_tensor(name, list(shape), dt)

    Wt = [sb("W%d" % i, [128, 16, 512], BF16) for i in range(2)]
    Wb = [[Buf("W%d_%d" % (i, j)) for j in range(4)] for i in range(2)]
    xmT = sb("xmT", [128, 16, NT], BF16)
    xmB = [Buf("xm%d" % i) for i in range(16)]
    catT = sb("catT", [128, 16, NT], BF16)
    catB = [Buf("cat%d" % i) for i in range(16)]
    NREG = 6
    Rg = [sb("R%d" % i, [128, 4096], BF16) for i in range(NREG)]
    RgB = [Buf("R%d" % i) for i in range(NREG)]
    NTMP = 6
    Tp = [sb("T%d" % i, [128, 1024], F32) for i in range(NTMP)]
    TpB = [Buf("T%d" % i) for i in range(NTMP)]
    cs = sb("cs", [128, 11, 128], F32)
    csB = Buf("cs")
    cs_bf = sb("cs_bf", [128, 2, 128], BF16)
    rstd = sb("rstd", [128, 2, NT], F32)
    rstdB = [Buf("rstd0"), Buf("rstd1")]
    ppt = [sb("pp0", [128, NPP], F32)] * 2
    ppB = [Buf("pp0")] * 2
    fnw_sb = sb("fnw_sb", [128, 16], F32)
    sc_bf = sb("sc_bf", [128, 16, 2], BF16)
    cond_sb = sb("cond_sb", [128, 16, 2], F32)
    scB = Buf("sc")
    modL = [sb("mod%d" % i, [128, 48, 2], F32) for i in range(2)]
    gmodL = [sb("gmod%d" % i, [128, 16, 2], F32) for i in range(2)]
    modBL = [Buf("mod0"), Buf("mod1")]
    abt = sb("abt", [128, 64], F32)
    abtB = Buf("abt")
    wl = Rg[2][:, 2048:4096].rearrange("p (a t) -> p a t", a=16)
    wlB = RgB[2]
    clam = sb("clam", [128, 8], F32)
    clamB = Buf("clam")
    lruS = sb("lruS", [128, 2, 4, 4], F32)
    lruSB = Buf("lruS")
    maskS = sb("maskS", [128, 16, 64], F32)[:]
    maskB = Buf("mask")
    wif = sb("wif", [128, 16, 16], BF16)
    wifB = Buf("wif")
    NPT = 4
    pTt = [sb("pT%d" % i, [128, 512], BF16) for i in range(NPT)]
    pTB = [Buf("pT%d" % i) for i in range(NPT)]
    NSM = 8
    smt = [sb("sm%d" % i, [128, 128], BF16) for i in range(NSM)]
    smB = [Buf("sm%d" % i) for i in range(NSM)]
    St = [[sb("St%d%d" % (d, h), [128, 129], F32) for h in range(4)] for d in range(2)]
    Stb = [[sb("Stb%d%d" % (d, h), [128, 129], BF16) for h in range(4)] for d in range(2)]
    StB = [[Buf("St") for h in range(4)] for d in range(2)]
    StbB = [[Buf("Stb") for h in range(4)] for d in range(2)]
    gt = sb("gt", [128, 8, 16], F32)
    nlf = sb("nlf", [128, 2, 8, 4], F32)
    nbt = sb("nbt", [128, 2, 8, 4], F32)
    eat = sb("eat", [128, 2, 8, 4], F32)
    ea01 = sb("ea01", [128, 2, 2, 8, 4], F32)
    enb = sb("enb", [128, 2, 8, 4], F32)
    eBt = sb("eBt", [128, 2, 2, 8, 4], F32)
    zt = sb("zt", [128, 2, 32], F32)
    nlfz = sb("nlfz", [128, 2, 32], F32)
    gB = Buf("gates")
    mfs = sb("mfs", [32, 8], F32)
    emfbc = sb("emfbc", [128, 32], F32)
    em0 = sb("em0", [128, 8], F32)
    tiny = [sb("tiny%d" % i, [128, 8], F32) for i in range(6)]
    tinyB = [Buf("tiny%d" % i) for i in range(6)]
    stg = [sb("stg%d" % i, [128, 129], F32) for i in range(2)]
    stgB = [Buf("stg0"), Buf("stg1")]
    cnt = {"pt": 0, "sm": 0, "tiny": 0, "stg": 0, "tp": 0}

    def nxt(kind, n):
        i = cnt[kind] % n
        cnt[kind] += 1
        return i

    reserved = set()

    def tmp():
        while True:
            i = cnt["tp"] % NTMP
            cnt["tp"] += 1
            if i not in reserved:
                return i

    IDf, ONf, PMf, TRF, TRB, TFF, TFB, H0, H1, MBF, MBB = [cs[:, i, :] for i in range(11)]
    IDb = cs_bf[:, 0, :]
    ONb = cs_bf[:, 1, :]

    def fsz(a):
        n = 1
        for x in a.shape[1:]:
            n *= x
        return n

    def mm(out, lhsT, rhs, st, sp, R, Wr, inc):
        n = fsz(rhs)
        c = 0.03 + max(n, 64) * 0.00045
        if rhs.dtype == F32:
            c *= 4
        K.op("pe", lambda: nc.tensor.matmul(out, lhsT, rhs, start=st, stop=sp, skip_group_check=True), R, Wr, inc, c=c)

    def act(out, in_, func, R, Wr, scale=1.0, bias=0.0):
        fs = {AF.Exp: "e", AF.Ln: "e", AF.Silu: "s", AF.Sigmoid: "g", AF.Sqrt: "q"}.get(func)
        K.op("act", lambda: nc.scalar.activation(out=out, in_=in_, func=func, bias=bias, scale=scale), R, Wr, c=0.2 + fsz(out) * 0.00105, fs=fs)

    def tt(out, in0, in1, op, R, Wr, e="dve"):
        K.op(e, lambda: E[e].tensor_tensor(out=out, in0=in0, in1=in1, op=op), R, Wr, c=0.1 + fsz(out) * 0.00105)

    def ts(out, in0, s1, s2, op0, op1, R, Wr, e="dve"):
        c = 0.1 + fsz(out) * 0.0008
        if s2 is None:
            K.op(e, lambda: E[e].tensor_scalar(out=out, in0=in0, scalar1=s1, scalar2=None, op0=op0), R, Wr, c=c)
        else:
            K.op(e, lambda: E[e].tensor_scalar(out=out, in0=in0, scalar1=s1, scalar2=s2, op0=op0, op1=op1), R, Wr, c=c)

    def stt(out, in0, sc, in1, op0, op1, R, Wr, e="dve"):
        K.op(e, lambda: E[e].scalar_tensor_tensor(out=out, in0=in0, scalar=sc, in1=in1, op0=op0, op1=op1), R, Wr, c=0.1 + fsz(out) * 0.00105)

    def rsq(out, in_, mul, R, Wr):
        act(out, in_, AF.Ln, R, Wr, scale=mul, bias=EPS)
        act(out, out, AF.Exp, Wr, Wr, scale=-0.5)

    def cp(out, in_, R, Wr, e="dve"):
        K.op(e, lambda: E[e].tensor_copy(out=out, in_=in_), R, Wr, c=0.1 + fsz(out) * 0.0008)

    wsched = []
    for t in range(12):
        wsched.append(ada_wT[0, t])
    for l in range(depth):
        for u in range(2):
            for t in range(14):
                wsched.append(w_inT[l, t])
            if u == 1 and l + 1 < depth:
                for t in range(12):
                    wsched.append(ada_wT[l + 1, t])
            for t in range(4):
                wsched.append(w_outT[l, t])
    wstate = {"issued": 0, "used": 0}

    def w_issue():
        i = wstate["issued"]
        if i >= len(wsched):
            return
        s = i % 2
        for qd in range(4):
            K.dma("pool", Wt[s][:, qd * 4:(qd + 1) * 4, :], wsched[i][:, qd * 4:(qd + 1) * 4, :], [], [Wb[s][qd]], nbytes=1 << 20)
        wstate["issued"] += 1

    def w_get():
        i = wstate["used"]
        while wstate["issued"] <= min(i + 1, len(wsched) - 1):
            w_issue()
        wstate["used"] += 1
        return i % 2

    K.dma("sp", cs[:], cst, [], [csB])
    cp(cs_bf[:, 0, :], cs[:, 0, :], [csB], [csB])
    cp(cs_bf[:, 1, :], cs[:, 1, :], [csB], [csB])
    K.dma("sp", cond_sb[:], condT, [], [scB])
    act(sc_bf[:], cond_sb[:], AF.Silu, [scB], [scB])
    K.dma("sp", fnw_sb[:], fnw, [], [scB])
    K.dma("sp", maskS, maskT, [], [maskB])

    def compute_rstd(u, ssp):
        for hh in range(2):
            sl = slice(hh * 512, (hh + 1) * 512)
            rsq(rstd[:, u, sl], ssp[hh].ap, 1.0 / D, [ssp[hh].b], [rstdB[u]])

    def prologue(u):
        ssp = [K.pin(), K.pin()]
        for kc in range(16):
            a = tmp()
            b = tmp()
            K.dma("sp", Tp[a][:], x0T[u, kc], [], [TpB[a]])
            sqb = Tp[b][:].bitcast(BF16)
            act(sqb[:, :NT], Tp[a][:], AF.Square, [TpB[a]], [TpB[b]])
            for hh in range(2):
                mm(ssp[hh].ap, ONb, sqb[:, hh * 512:(hh + 1) * 512], kc == 0, kc == 15, [TpB[b], csB], [ssp[hh].b], True)
        compute_rstd(u, ssp)
        K.unpin(ssp[0])
        K.unpin(ssp[1])

    def xsrc(l, u, fc):
        return x0T[u, fc] if l == 0 else xsc[u, fc]

    def phaseN(l, u, pcur):
        pb = ppB[l % 2]
        sh0 = 0
        for kc in range(16):
            a = tmp()
            rds = [] if l == 0 else [xsb[u][kc][0], xsb[u][kc][1]]
            K.dma("sp", Tp[a][:], xsrc(l, u, kc), rds, [TpB[a]])
            tt(Tp[a][:], Tp[a][:], rstd[:, u, :], ALU.mult, [TpB[a], rstdB[u]], [TpB[a]])
            act(xmT[:, kc, :], Tp[a][:], AF.Identity, [TpB[a], modBL[l % 2]], [xmB[kc]],
                scale=gmodL[l % 2][:, kc, u:u + 1], bias=modL[l % 2][:, sh0 + kc, u:u + 1])

    def proj_fm(s, col0, ncols, evac):
        for j in range(ncols // 128):
            for hh in range(2):
                ps = K.ppsum()
                for kc in range(16):
                    mm(ps.ap, Wt[s][:, kc, col0 + j * 128: col0 + (j + 1) * 128], xmT[:, kc, hh * 512:(hh + 1) * 512],
                       kc == 0, kc == 15, [Wb[s][kc // 4], xmB[kc]], [ps.b], kc == 15)
                evac(j, hh, ps)

    def proj_tm(s, col0, ncols, evac, wt=None, wb=None):
        for t8 in range(8):
            ps = K.ppsum()
            for kc in range(16):
                if wt is None:
                    rhs = Wt[s][:, kc, col0:col0 + ncols]
                    rb = Wb[s][kc // 4]
                else:
                    rhs = wt[:, kc, :]
                    rb = wb
                mm(ps.ap[:, :ncols], xmT[:, kc, t8 * 128:(t8 + 1) * 128], rhs, kc == 0, kc == 15, [rb, xmB[kc]], [ps.b], kc == 15)
            evac(t8, ps)

    def R4(i):
        return Rg[i][:].rearrange("p (a t) -> p a t", a=4)

    def R8(i):
        return Rg[i][:].rearrange("p (a t) -> p a t", a=8)

    def Rf(i):
        return Rg[i][:].bitcast(F32).rearrange("p (a t) -> p a t", a=2)

    def attn(steps, nq, out_ap, outB, gate_ap, gateB):
        O = K.pin()
        Rr = K.pin()
        nst = len(steps)

        def emitS(st):
            kT, kR, q, qR, n, off, bias, bR, v, vR = st
            kTs = kT if isinstance(kT, list) else [kT]
            G = len(kTs)
            ps = K.psum()
            ps_ap, ps_b = ps.ap, ps.b
            for j in range(G):
                mm(ps_ap[:, j * n:(j + 1) * n], kTs[j], q, True, True, kR + qR, [ps_b], j == G - 1)
            pi = nxt("pt", NPT)
            w = G * n
            if bias is not None:
                a = tmp()
                if G > 1:
                    o_ = Tp[a][:, :w].rearrange("p (g t) -> p g t", g=G)
                    i_ = ps_ap[:, :w].rearrange("p (g t) -> p g t", g=G)
                else:
                    o_, i_ = Tp[a][:, :w], ps_ap[:, :w]
                stt(o_, i_, SCL, bias, ALU.mult, ALU.add, [ps_b] + bR, [TpB[a]])
                act(pTt[pi][:, :w], Tp[a][:, :w], AF.Exp, [TpB[a]], [pTB[pi]])
            else:
                act(pTt[pi][:, :w], ps_ap[:, :w], AF.Exp, [ps_b], [pTB[pi]], scale=SCL)
            return pi

        cur = emitS(steps[0])
        for i in range(nst):
            nx = emitS(steps[i + 1]) if i + 1 < nst else None
            kT, kR, q, qR, n, off, bias, bR, v, vR = steps[i]
            vs = v if isinstance(v, list) else [v]
            G = len(vs)
            last = i == nst - 1
            for j in range(G):
                lj = last and j == G - 1
                mm(O.ap[:, off:off + n], vs[j], pTt[cur][:, j * n:(j + 1) * n], i == 0 and j == 0, lj, vR + [pTB[cur]], [O.b], False)
                mm(Rr.ap[:, off:off + n], ONb, pTt[cur][:, j * n:(j + 1) * n], i == 0 and j == 0, lj, [pTB[cur], csB], [Rr.b, O.b], j == G - 1)
            cur = nx
        a = tmp()
        act(Tp[a][:, :nq], Rr.ap[:, :nq], AF.Ln, [Rr.b], [TpB[a]])
        act(Tp[a][:, :nq], Tp[a][:, :nq], AF.Exp, [TpB[a]], [TpB[a]], scale=-1.0)
        tt(Tp[a][:, :nq], O.ap[:, :nq], Tp[a][:, :nq], ALU.mult, [O.b, TpB[a]], [TpB[a]])
        tt(out_ap, Tp[a][:, :nq], gate_ap, ALU.mult, [TpB[a]] + gateB, outB)
        K.unpin(O)
        K.unpin(Rr)

    def branchA(l, u):
        pc = ppt[l % 2]
        pb = ppB[l % 2]
        XT = R4(0)
        GT = R4(1)
        xabB = Buf("xab")
        s = w_get()
        proj_fm(s, 0, 512, lambda j, hh, ps: cp(XT[:, j, hh * 512:(hh + 1) * 512], ps.ap, [ps.b], [RgB[0]]))
        s = w_get()
        proj_fm(s, 0, 512, lambda j, hh, ps: act(GT[:, j, hh * 512:(hh + 1) * 512], ps.ap, AF.Silu, [ps.b], [RgB[1]]))
        cw0 = PP["conv_w"][0]
        cb0 = PP["conv_b"][0]
        lb0 = PP["lru_b"][0]
        h00 = PP["h0"][0]
        nseq, Ts = (4, 256) if u == 0 else (1, 1024)
        tl = list(range(NTMP))
        K.dma("pool", wl, lru_wT[l], [], [wlB])
        for g in range(4):
            X = XT[:, g, :]
            XB = RgB[0]
            ixa, ir, ii, itm, ihf, ihb = tl
            xa = Tp[ixa][:]
            xab = Rg[2][:, :NT]
            ts(xa, X, pc[:, cw0 + g * 4 + 1: cw0 + g * 4 + 2], pc[:, cb0 + g: cb0 + g + 1], ALU.mult, ALU.add, [XB, pb], [TpB[ixa]])
            Xv = X.rearrange("p (s t) -> p s t", s=nseq)
            xav = xa.rearrange("p (s t) -> p s t", s=nseq)
            stt(xav[:, :, 1:], Xv[:, :, :Ts - 1], pc[:, cw0 + g * 4: cw0 + g * 4 + 1], xav[:, :, 1:], ALU.mult, ALU.add, [XB, pb, TpB[ixa]], [TpB[ixa]])
            stt(xav[:, :, :Ts - 1], Xv[:, :, 1:], pc[:, cw0 + g * 4 + 2: cw0 + g * 4 + 3], xav[:, :, :Ts - 1], ALU.mult, ALU.add, [XB, pb, TpB[ixa]], [TpB[ixa]])
            stt(xav[:, :, :Ts - 2], Xv[:, :, 2:], pc[:, cw0 + g * 4 + 3: cw0 + g * 4 + 4], xav[:, :, :Ts - 2], ALU.mult, ALU.add, [XB, pb, TpB[ixa]], [TpB[ixa]])
            act(xab, xa, AF.Copy, [TpB[ixa]], [RgB[2]])
            for dr in range(2):
                gts = [Tp[ir][:], Tp[ii][:]]
                gtb = [TpB[ir], TpB[ii]]
                for ri in range(2):
                    for hh in range(2):
                        ps = K.psum()
                        mm(ps.ap, wl[:, dr * 8 + ri * 4 + g, :], xab[:, hh * 512:(hh + 1) * 512], True, True, [wlB, RgB[2]], [ps.b], True)
                        c = lb0 + dr * 8 + ri * 4 + g
                        act(gts[ri][:, hh * 512:(hh + 1) * 512], ps.ap, AF.Sigmoid, [ps.b, pb], [gtb[ri]], bias=pc[:, c:c + 1])
                av = Tp[ir][:]
                bv = Tp[ii][:]
                tm = Tp[itm][:]
                act(av, av, AF.Exp, [TpB[ir], clamB], [TpB[ir]], scale=clam[:, dr * 4 + g: dr * 4 + g + 1])
                tt(tm, av, av, ALU.mult, [TpB[ir]], [TpB[itm]])
                act(tm, tm, AF.Sqrt, [TpB[itm]], [TpB[itm]], scale=-1.0, bias=1.0)
                tt(bv, tm, bv, ALU.mult, [TpB[itm], TpB[ii]], [TpB[ii]])
                tt(bv, bv, xa, ALU.mult, [TpB[ii], TpB[ixa]], [TpB[ii]])
                ih = ihf if dr == 0 else ihb
                hv = Tp[ih][:]
                for sq in range(nseq):
                    sl = slice(sq * Ts, (sq + 1) * Ts)
                    if u == 0:
                        init = 0.0
                    else:
                        init = pc[:, h00 + dr * 4 + g: h00 + dr * 4 + g + 1]
                    if dr == 0:
                        o_, a_, b_ = hv[:, sl], av[:, sl], bv[:, sl]
                    else:
                        o_, a_, b_ = rev(hv[:, sl]), rev(av[:, sl]), rev(bv[:, sl])
                    K.op("dve", lambda o_=o_, a_=a_, b_=b_, init=init: nc.vector.tensor_tensor_scan(
                        out=o_, data0=a_, data1=b_, initial=init, op0=ALU.mult, op1=ALU.add),
                        [TpB[ir], TpB[ii], pb], [TpB[ih]], c=0.1 + Ts * 0.00105)
                if u == 0:
                    hvv = hv.rearrange("p (s t) -> p s t", s=4)
                    col = 255 if dr == 0 else 0
                    act(lruS[:, dr, g, :], hvv[:, :, col], AF.Copy, [TpB[ih]], [lruSB])
            tt(Tp[ihf][:], Tp[ihf][:], Tp[ihb][:], ALU.add, [TpB[ihf], TpB[ihb]], [TpB[ihf]])
            tt(catT[:, g, :], Tp[ihf][:], GT[:, g, :], ALU.mult, [TpB[ihf], RgB[1]], [catB[g]])
        if u == 0:
            K.dma("sp", o_lru[l], lruS[:], [lruSB], [])

    def branchB(l, u):
        qT, kT, vv, gT = R4(3), R4(4), R8(5), R4(0)
        s = w_get()
        proj_fm(s, 0, 512, lambda j, hh, ps: cp(qT[:, j, hh * 512:(hh + 1) * 512], ps.ap, [ps.b], [RgB[3]]))
        s = w_get()

        def ev_k(j, hh, ps):
            sl = slice(hh * 512, (hh + 1) * 512)
            if u == 0:
                a = tmp()
                act(Tp[a][:, :512], ps.ap, AF.Copy, [ps.b], [TpB[a]])
                K.dma("sp", o_nak[l, :, j, sl], Tp[a][:, :512], [TpB[a]], [])
                cp(kT[:, j, sl], Tp[a][:, :512], [TpB[a]], [RgB[4]])
            else:
                cp(kT[:, j, sl], ps.ap, [ps.b], [RgB[4]])
        import os
        sub = int(os.environ.get("KSUB", "9"))
        if sub == -1:
            [w_get() for _ in range(3)]
            return
        proj_fm(s, 0, 512, ev_k)
        s = w_get()
        if sub == -2:
            [w_get() for _ in range(2)]
            return

        def ev_v(t8, ps):
            if u == 0:
                a = tmp()
                act(Tp[a][:, :512], ps.ap, AF.Copy, [ps.b], [TpB[a]])
                K.dma("sp", o_nav[l, :, t8, :], Tp[a][:, :512], [TpB[a]], [])
                cp(vv[:, t8, :], Tp[a][:, :512], [TpB[a]], [RgB[5]])
            else:
                cp(vv[:, t8, :], ps.ap, [ps.b], [RgB[5]])
        proj_tm(s, 0, 512, ev_v)
        s = w_get()
        proj_fm(s, 0, 512, lambda j, hh, ps: act(gT[:, j, hh * 512:(hh + 1) * 512], ps.ap, AF.Silu, [ps.b], [RgB[0]]))
        if sub == 0 or (sub == 1 and u == 1):
            return
        if u == 0:
            for sq in range(4):
                for h in range(4):
                    qs = slice(sq * 256, (sq + 1) * 256)
                    steps = []
                    for c in range(2):
                        t0 = sq * 256 + c * 128
                        steps.append((kT[:, h, t0:t0 + 128], [RgB[4]], qT[:, h, qs], [RgB[3]], 256, 0, None, [],
                                      vv[:, sq * 2 + c, h * 128:(h + 1) * 128], [RgB[5]]))
                    attn(steps, 256, catT[:, 4 + h, qs], [catB[4 + h]], gT[:, h, qs], [RgB[0]])
        else:
            ick = tmp()
            icv = tmp()
            ckT = Tp[ick][:].bitcast(BF16).rearrange("p (a t) -> p a t", a=4)
            cv = Tp[icv][:].bitcast(BF16).rearrange("p (a t) -> p a t", a=4)
            K.dma("pool", ckT, c_nakT[l], [], [TpB[ick]])
            K.dma("pool", cv, c_nav[l], [], [TpB[icv]])
            ibs = tmp()
            reserved.update([ick, icv, ibs])
            for h in range(4):
                bias = Tp[ibs][:].rearrange("p (a t) -> p a t", a=16)
                K.dma("sp", bias, rpbT[l, h], [], [TpB[ibs]])
                tt(bias, bias, maskS, ALU.add, [TpB[ibs], maskB], [TpB[ibs]])
                for qb in range(2):
                    qs = slice(qb * 512, (qb + 1) * 512)
                    steps = []
                    for c in range(4):
                        steps.append((ckT[:, h, c * 128:(c + 1) * 128], [TpB[ick]], qT[:, h, qs], [RgB[3]], 512, 0, None, [],
                                      cv[:, c, h * 128:(h + 1) * 128], [TpB[icv]]))
                    for r in range(8):
                        rr = qb * 8 + r
                        R0 = min(max(rr - 4, 0), 8)
                        q64 = qT[:, h, rr * 64:(rr + 1) * 64]
                        if R0 % 2 == 0:
                            lst = [(R0 // 2 + c, R0 + 2 * c - rr + 7) for c in range(4)]
                        else:
                            m0 = (R0 - 1) // 2
                            lst = [(m0, 14)] + [(m0 + c, 2 * (m0 + c) - rr + 7) for c in (1, 2, 3)] + [(m0 + 4, 15)]
                        bias2 = bias.rearrange("p (a two) t -> p a two t", two=2)

                        def grp(sub):
                            ms = [m for (m, _) in sub]
                            s0 = sub[0][1]
                            if len(sub) == 1:
                                bv = bias[:, s0, :]
                            else:
                                bv = bias2[:, s0 // 2: s0 // 2 + len(sub), s0 % 2, :]
                            steps.append(([kT[:, h, m * 128:(m + 1) * 128] for m in ms], [RgB[4]], q64, [RgB[3]], 64, r * 64,
                                          bv, [TpB[ibs]], [vv[:, m, h * 128:(h + 1) * 128] for m in ms], [RgB[5]]))
                        if R0 % 2 == 0:
                            grp(lst)
                        else:
                            grp(lst[0:1])
                            grp(lst[1:4])
                            grp(lst[4:5])
                    attn(steps, 512, catT[:, 4 + h, qs], [catB[4 + h]], gT[:, h, qs], [RgB[0]])
            reserved.clear()

    def branchC(l, u):
        pc = ppt[l % 2]
        pb = ppB[l % 2]
        qn, kn, vv, gT = R4(1), Rg[2][:, :2048].rearrange("p (a t) -> p a t", a=2), Rg[2][:, 2048:].rearrange("p (a t) -> p a t", a=8), R4(3)
        if u == 1:
            irc = tmp()
            irs = tmp()
            K.dma("sp", Tp[irc][:], ropeC, [], [TpB[irc]])
            K.dma("sp", Tp[irs][:], ropeS, [], [TpB[irs]])
            reserved.update([irc, irs])

        def normrope(ps, wcol, hh, out_bf, outB, dma_out):
            sl = slice(hh * 512, (hh + 1) * 512)
            ia = tmp()
            xf = Tp[ia][:, :512]
            sq = Tp[ia][:, 512:]
            act(xf, ps.ap, AF.Copy, [ps.b], [TpB[ia]])
            act(sq, ps.ap, AF.Square, [ps.b], [TpB[ia]])
            p2 = K.psum()
            mm(p2.ap, ONf, sq, True, True, [TpB[ia], csB], [p2.b], True)
            rsq(sq, p2.ap, 1.0 / 128, [p2.b], [TpB[ia]])
            stt(xf, xf, wcol, sq, ALU.mult, ALU.mult, [TpB[ia], pb], [TpB[ia]])
            if dma_out is not None:
                K.dma("sp", dma_out, xf, [TpB[ia]], [])
            if u == 0:
                cp(out_bf, xf, [TpB[ia]], outB)
            else:
                p3 = K.psum()
                mm(p3.ap, PMf, xf, True, True, [TpB[ia], csB], [p3.b], True)
                tt(sq, p3.ap, Tp[irs][:, sl], ALU.mult, [p3.b, TpB[irs]], [TpB[ia]])
                tt(xf, xf, Tp[irc][:, sl], ALU.mult, [TpB[ia], TpB[irc]], [TpB[ia]])
                tt(out_bf, xf, sq, ALU.add, [TpB[ia]], outB)

        qw = PP["qnw"][0]
        kw = PP["knw"][0]
        s = w_get()
        proj_fm(s, 0, 512, lambda j, hh, ps: normrope(ps, pc[:, qw:qw + 1], hh, qn[:, j, hh * 512:(hh + 1) * 512], [RgB[1]], None))
        s = w_get()
        proj_fm(s, 0, 256, lambda j, hh, ps: normrope(ps, pc[:, kw:kw + 1], hh, kn[:, j, hh * 512:(hh + 1) * 512], [RgB[2]],
                                                       o_gk[l, :, j, hh * 512:(hh + 1) * 512] if u == 0 else None))

        def ev_v(t8, ps):
            if u == 0:
                a = tmp()
                act(Tp[a][:, :256], ps.ap[:, :256], AF.Copy, [ps.b], [TpB[a]])
                K.dma("sp", o_gv[l, :, t8, :], Tp[a][:, :256], [TpB[a]], [])
                cp(vv[:, t8, :], Tp[a][:, :256], [TpB[a]], [RgB[2]])
            else:
                cp(vv[:, t8, :], ps.ap[:, :256], [ps.b], [RgB[2]])
        proj_tm(s, 256, 256, ev_v)
        s = w_get()
        proj_fm(s, 0, 512, lambda j, hh, ps: act(gT[:, j, hh * 512:(hh + 1) * 512], ps.ap, AF.Silu, [ps.b], [RgB[3]]))
        if u == 0:
            for sq in range(4):
                for h in range(4):
                    kv = h // 2
                    qs = slice(sq * 256, (sq + 1) * 256)
                    steps = []
                    for c in range(2):
                        t0 = sq * 256 + c * 128
                        steps.append((kn[:, kv, t0:t0 + 128], [RgB[2]], qn[:, h, qs], [RgB[1]], 256, 0, None, [],
                                      vv[:, sq * 2 + c, kv * 128:(kv + 1) * 128], [RgB[2]]))
                    attn(steps, 256, catT[:, 8 + h, qs], [catB[8 + h]], gT[:, h, qs], [RgB[3]])
        else:
            ick = tmp()
            ckT = Tp[ick][:, :512].bitcast(BF16).rearrange("p (a t) -> p a t", a=2)
            cv = Tp[ick][:, 512:].bitcast(BF16).rearrange("p (a t) -> p a t", a=4)
            K.dma("pool", ckT, c_gkT[l], [], [TpB[ick]])
            K.dma("pool", cv, c_gv[l], [], [TpB[ick]])
            reserved.add(ick)
            for h in range(4):
                kv = h // 2
                for qb in range(2):
                    qs = slice(qb * 512, (qb + 1) * 512)
                    steps = []
                    for c in range(8):
                        steps.append((kn[:, kv, c * 128:(c + 1) * 128], [RgB[2]], qn[:, h, qs], [RgB[1]], 512, 0, None, [],
                                      vv[:, c, kv * 128:(kv + 1) * 128], [RgB[2]]))
                    for c in range(4):
                        steps.append((ckT[:, kv, c * 128:(c + 1) * 128], [TpB[ick]], qn[:, h, qs], [RgB[1]], 512, 0, None, [],
                                      cv[:, c, kv * 128:(kv + 1) * 128], [TpB[ick]]))
                    attn(steps, 512, catT[:, 8 + h, qs], [catB[8 + h]], gT[:, h, qs], [RgB[3]])
            reserved.clear()

    def branchD(l, u):
        pc = ppt[l % 2]
        pb = ppB[l % 2]
        qe, qo, kT, ktm, vv, og = R4(4), R4(5), R4(0), R8(1), R8(2), R8(3)
        K.op("dve", lambda: nc.vector.memset(Rg[4][:], 0.0), [], [RgB[4]], c=2.0)
        K.op("dve", lambda: nc.vector.memset(Rg[5][:], 0.0), [], [RgB[5]], c=2.0)
        s = w_get()

        def ev_q(j, hh, ps):
            pv = ps.ap.rearrange("p (t c k) -> p t c k", t=4, c=2)
            qev = qe[:, j, hh * 512:(hh + 1) * 512].rearrange("p (t c k) -> p t c k", t=4, c=2)
            qov = qo[:, j, hh * 512:(hh + 1) * 512].rearrange("p (t c k) -> p t c k", t=4, c=2)
            ts(qev[:, :, 0, :], pv[:, :, 0, :], SCL, None, ALU.mult, None, [ps.b], [RgB[4]])
            ts(qov[:, :, 1, :], pv[:, :, 1, :], SCL, None, ALU.mult, None, [ps.b], [RgB[5]])
        proj_fm(s, 0, 512, ev_q)
        s = w_get()
        proj_fm(s, 0, 512, lambda j, hh, ps: cp(kT[:, j, hh * 512:(hh + 1) * 512], ps.ap, [ps.b], [RgB[0]]))
        for t8 in range(8):
            pt = K.psum()
            ptb = pt.ap.bitcast(BF16)
            for h in range(4):
                K.op("pe", lambda h=h, t8=t8, ptb=ptb: nc.tensor.transpose(out=ptb[:, h * 128:(h + 1) * 128], in_=kT[:, h, t8 * 128:(t8 + 1) * 128], identity=IDb),
                     [RgB[0], csB], [pt.b], h == 3, c=0.1)
            cp(ktm[:, t8, :], ptb[:, 0:512], [pt.b], [RgB[1]])
        s = w_get()
        proj_tm(s, 0, 512, lambda t8, ps: cp(vv[:, t8, :], ps.ap, [ps.b], [RgB[2]]))
        s = w_get()
        proj_tm(s, 0, 512, lambda t8, ps: act(og[:, t8, :], ps.ap, AF.Sigmoid, [ps.b], [RgB[3]]))
        s = w_get()

        def ev_g(t8, ps):
            a = tmp()
            act(Tp[a][:, :512], ps.ap, AF.Silu, [ps.b], [TpB[a]])
            tt(og[:, t8, :], og[:, t8, :], Tp[a][:, :512], ALU.mult, [RgB[3], TpB[a]], [RgB[3]])
        proj_tm(s, 0, 512, ev_g)
        K.dma("pool", wif[:], w_ifT[l], [], [wifB])
        g0 = PP["gb"][0]
        proj_tm(None, 0, 16, lambda t8, ps: tt(gt[:, t8, :], ps.ap[:, :16], pc[:, g0:g0 + 16], ALU.add, [ps.b, pb], [gB]), wt=wif, wb=wifB)
        gv = gt[:].rearrange("p t (d w h) -> p d t w h", d=2, w=2)
        for dr in range(2):
            act(nlf[:, dr], gv[:, dr, :, 1, :], AF.Exp, [gB], [gB], scale=-1.0)
        ts(nlf[:], nlf[:], 1.0, None, ALU.add, None, [gB], [gB])
        act(nlf[:], nlf[:], AF.Ln, [gB], [gB])
        nlf2 = nlf[:].rearrange("p d t h -> p (d t h)")
        ps = K.psum()
        mm(ps.ap[:, 0:32], TRF, nlf2[:, 0:32], True, True, [gB, csB], [ps.b], False)
        mm(ps.ap[:, 32:64], TRB, nlf2[:, 32:64], True, True, [gB, csB], [ps.b], True)
        cp(nbt[:].rearrange("p d t h -> p (d t h)"), ps.ap[:, :64], [ps.b], [gB])
        ps = K.psum()
        mm(ps.ap[:, 0:64], H0, nlf2, True, True, [gB, csB], [ps.b], False)
        mm(ps.ap[:, 64:128], H1, nlf2, True, True, [gB, csB], [ps.b], True)
        act(eBt[:].rearrange("p c d t h -> p (c d t h)"), ps.ap[:, :128], AF.Exp, [ps.b], [gB], scale=-1.0)
        for dr in range(2):
            tt(eat[:, dr], gv[:, dr, :, 0, :], nbt[:, dr], ALU.add, [gB], [gB])
        if u == 0:
            nlv = nlf[:].rearrange("p d (s i) h -> p d i s h", i=2)
            pz = K.psum()

            def zc(i, d):
                return pz.ap[:, (i * 2 + d) * 16:(i * 2 + d + 1) * 16].rearrange("p (s h) -> p s h", s=4)
            mm(zc(0, 0), TFF, nlv[:, 0, 0], True, True, [gB, csB], [pz.b], False)
            mm(zc(1, 0), ONf, nlv[:, 0, 0], True, False, [gB, csB], [pz.b], False)
            mm(zc(1, 0), TFF, nlv[:, 0, 1], False, True, [gB, csB], [pz.b], False)
            mm(zc(1, 1), TFB, nlv[:, 1, 1], True, True, [gB, csB], [pz.b], False)
            mm(zc(0, 1), ONf, nlv[:, 1, 1], True, False, [gB, csB], [pz.b], False)
            mm(zc(0, 1), TFB, nlv[:, 1, 0], False, True, [gB, csB], [pz.b], True)
            liv = gt[:].rearrange("p (s i) (d w h) -> p i d s w h", i=2, d=2, w=2)
            for i in range(2):
                for d in range(2):
                    tt(zt[:, i, d * 16:(d + 1) * 16].rearrange("p (s h) -> p s h", s=4), liv[:, i, d, :, 0, :], zc(i, d), ALU.add, [gB, pz.b], [gB])
                    cp(nlfz[:, i, d * 16:(d + 1) * 16].rearrange("p (s h) -> p s h", s=4), nlv[:, d, i], [gB], [gB])
            pg = K.psum()
            pg2 = K.psum()
            for i in range(2):
                mm(pg.ap[0:32, i * 128:(i + 1) * 128], zt[:, i, :], IDf, True, True, [gB, csB], [pg.b], i == 1)
            for i in range(2):
                mm(pg2.ap[0:32, i * 128:(i + 1) * 128], nlfz[:, i, :], IDf, True, True, [gB, csB], [pg2.b], i == 1)
            K.op("dve", lambda: nc.vector.tensor_reduce(out=mfs[:, 0:1], in_=pg.ap[0:32, 0:256], axis=AX.X, op=ALU.max), [pg.b], [gB])
            K.op("dve", lambda: nc.vector.tensor_reduce(out=mfs[:, 1:2], in_=pg2.ap[0:32, 0:256], axis=AX.X, op=ALU.add), [pg2.b], [gB])
            ts(mfs[:, 2:3], mfs[:, 0:1], 0.0, mfs[:, 1:2], ALU.max, ALU.subtract, [gB], [gB])
            K.dma("sp", o_m[l], mfs[:, 2:3], [gB], [])
            act(mfs[:, 3:4], mfs[:, 2:3], AF.Exp, [gB], [gB], scale=-1.0)
            ia = tmp()
            ts(Tp[ia][0:32, 0:32], cs[0:32, 0, 0:32], mfs[:, 3:4], None, ALU.mult, None, [gB, csB], [TpB[ia]])
            pb2 = K.psum()
            mm(pb2.ap[:, 0:32], cs[0:32, 1, :], Tp[ia][0:32, 0:32], True, True, [TpB[ia], csB], [pb2.b], True)
            cp(emfbc[:], pb2.ap[:, 0:32], [pb2.b], [gB])
        act(enb[:], nbt[:], AF.Exp, [gB], [gB])
        act(eat[:], eat[:], AF.Exp, [gB], [gB])
        ts(ea01[:, 0], eat[:], cs[:, 7, 0:1], None, ALU.mult, None, [gB, csB], [gB])
        ts(ea01[:, 1], eat[:], cs[:, 8, 127:128], None, ALU.mult, None, [gB, csB], [gB])
        tt(ea01[:].rearrange("p c d t h -> p (c d t h)"), ea01[:].rearrange("p c d t h -> p (c d t h)"),
           eBt[:].rearrange("p c d t h -> p (c d t h)"), ALU.mult, [gB], [gB])
        if u == 1 and l + 1 < depth:
            mod_compute(l + 1)
        if u == 1:
            m00 = PP["m0"][0]
            act(em0[:], pc[:, m00:m00 + 8], AF.Exp, [pb], [gB])
            for dr in range(2):
                ia = tmp()
                c0 = Tp[ia][:, :516].rearrange("p (h k) -> p h k", h=4)
                K.dma("sp", c0, C0n0[l, :, dr], [], [TpB[ia]])
                for h in range(4):
                    ts(St[dr][h][:], c0[:, h, :], em0[:, dr * 4 + h: dr * 4 + h + 1], None, ALU.mult, None, [TpB[ia], gB], [StB[dr][h]])
                    act(Stb[dr][h][:], St[dr][h][:], AF.Copy, [StB[dr][h]], [StbB[dr][h]])
        nseq, tps = (4, 2) if u == 0 else (1, 8)
        won0 = PP["won"][0]
        hsI = [tmp() for _ in range(4)]
        reserved.update(hsI)

        def hs_tile(t8):
            return Tp[hsI[t8 // 2]][:, (t8 % 2) * 512:(t8 % 2 + 1) * 512].rearrange("p (h k) -> p h k", h=4), TpB[hsI[t8 // 2]]

        def instance(dr, h, t8, first_dir):
            tok = slice(t8 * 128, (t8 + 1) * 128)
            hsl = slice(h * 128, (h + 1) * 128)
            pss_ap, pss_b = K.pslot()
            mm(pss_ap[:, :128], kT[:, h, tok], qe[:, h, tok], True, False, [RgB[0], RgB[4]], [pss_b], False)
            mm(pss_ap[:, :128], kT[:, h, tok], qo[:, h, tok], False, True, [RgB[0], RgB[5]], [pss_b], True)
            ip = nxt("sm", NSM)
            stt(smt[ip][:], pss_ap[:, :128], eat[:, dr, t8, h:h + 1], MBF if dr == 0 else MBB, ALU.mult, ALU.mult, [pss_b, gB, csB], [smB[ip]])
            pn = K.psum()
            order = (0, 1) if dr == 0 else (1, 0)
            for ci, c in enumerate(order):
                qq = qe if c == 0 else qo
                mm(pn.ap[:, 0:129], qq[:, h, tok], Stb[dr][h][:], ci == 0, False, [RgB[4 + c], StbB[dr][h]], [pn.b], True)
                ik = nxt("sm", NSM)
                act(smt[ik][:], ktm[:, t8, hsl], AF.Copy, [RgB[1], gB], [smB[ik]], scale=ea01[:, c, dr, t8, h:h + 1])
                pu_ap, pu_b = K.pslot()
                mm(pu_ap[:, 0:128], smt[ik][:], vv[:, t8, hsl], True, True, [smB[ik], RgB[2]], [pu_b], False)
                mm(pu_ap[:, 128:129], smt[ik][:], ONb[:, 0:1], True, True, [smB[ik], csB], [pu_b], True)
                ebc = eBt[:, c, dr, t8, h:h + 1]
                stt(St[dr][h][:], St[dr][h][:], ebc, pu_ap[:, 0:129], ALU.mult, ALU.add, [pu_b, StB[dr][h], gB], [StB[dr][h]])
                act(Stb[dr][h][:], St[dr][h][:], AF.Copy, [StB[dr][h]], [StbB[dr][h]])
            mm(pn.ap[:, 0:128], smt[ip][:], vv[:, t8, hsl], False, False, [smB[ip], RgB[2]], [pn.b], False)
            mm(pn.ap[:, 128:129], smt[ip][:], ONb[:, 0:1], False, True, [smB[ip], csB], [pn.b], True)
            it = nxt("tiny", 6)
            act(tiny[it][:, 0:1], pn.ap[:, 128:129], AF.Abs, [pn.b], [tinyB[it]])
            ts(tiny[it][:, 0:1], tiny[it][:, 0:1], enb[:, dr, t8, h:h + 1], None, ALU.max, None, [tinyB[it], gB], [tinyB[it]])
            K.op("dve", lambda: nc.vector.reciprocal(out=tiny[it][:, 1:2], in_=tiny[it][:, 0:1]), [tinyB[it]], [tinyB[it]])
            hv, hb = hs_tile(t8)
            if first_dir:
                act(hv[:, h, :], pn.ap[:, 0:128], AF.Copy, [pn.b, tinyB[it]], [hb], scale=tiny[it][:, 1:2])
            else:
                stt(hv[:, h, :], pn.ap[:, 0:128], tiny[it][:, 1:2], hv[:, h, :], ALU.mult, ALU.add, [pn.b, tinyB[it], hb], [hb])

        def finish_tile(t8):
            hv, hb = hs_tile(t8)
            ia = tmp()
            sq = Tp[ia][:, :512].rearrange("p (h k) -> p h k", h=4)
            it = nxt("tiny", 6)
            tt(sq, hv, hv, ALU.mult, [hb], [TpB[ia]])
            K.op("dve", lambda: nc.vector.tensor_reduce(out=tiny[it][:, 0:4], in_=sq, axis=AX.X, op=ALU.add), [TpB[ia]], [tinyB[it]])
            rsq(tiny[it][:, 0:4], tiny[it][:, 0:4], 1.0 / 128, [tinyB[it]], [tinyB[it]])
            for h in range(4):
                stt(sq[:, h, :], hv[:, h, :], tiny[it][:, h:h + 1], pc[:, won0 + h * 128: won0 + (h + 1) * 128], ALU.mult, ALU.mult,
                    [hb, tinyB[it], pb, TpB[ia]], [TpB[ia]])
            od = Tp[ia][:, 512:].bitcast(BF16)[:, :512]
            tt(od, Tp[ia][:, :512], og[:, t8, :], ALU.mult, [TpB[ia], RgB[3]], [TpB[ia]])
            pt = K.psum()
            ptb = pt.ap.bitcast(BF16)
            for h in range(4):
                K.op("pe", lambda h=h: nc.tensor.transpose(out=ptb[:, h * 128:(h + 1) * 128], in_=od[:, h * 128:(h + 1) * 128], identity=IDb),
                     [TpB[ia], csB], [pt.b], h == 3)
            act(catT[:, 12:16, t8 * 128:(t8 + 1) * 128], ptb[:, 0:512].rearrange("p (h k) -> p h k", h=4), AF.Copy, [pt.b], [catB[12 + i] for i in range(4)])

        for sq in range(nseq):
            if u == 0:
                for dr in range(2):
                    for h in range(4):
                        K.op("dve", lambda dr=dr, h=h: nc.vector.memset(St[dr][h][:], 0.0), [], [StB[dr][h]])
                        K.op("dve", lambda dr=dr, h=h: nc.vector.memset(Stb[dr][h][:], 0.0), [], [StbB[dr][h]])
            done = {}
            for i in range(tps):
                for dr in range(2):
                    t8 = sq * tps + (i if dr == 0 else tps - 1 - i)
                    for h in range(4):
                        instance(dr, h, t8, t8 not in done)
                    done[t8] = done.get(t8, 0) + 1
                    if done[t8] == 2:
                        finish_tile(t8)
            if u == 0:
                for dr in range(2):
                    for h in range(4):
                        ig = nxt("stg", 2)
                        c = dr * 16 + sq * 4 + h
                        ts(stg[ig][:], St[dr][h][:], emfbc[:, c:c + 1], None, ALU.mult, None, [StB[dr][h], gB], [stgB[ig]])
                        K.dma("sp", o_C[l, :, dr, sq, h, :], stg[ig][:], [stgB[ig]], [])
        reserved.clear()

    def phaseW(l, u, last_layer):
        ssp = [K.pin(), K.pin()]
        for t in range(4):
            s = w_get()
            for j in range(4):
                fc = t * 4 + j
                for hh in range(2):
                    sl = slice(hh * 512, (hh + 1) * 512)
                    ps = K.ppsum()
                    for kc in range(16):
                        mm(ps.ap, Wt[s][:, kc, j * 128:(j + 1) * 128], catT[:, kc, sl], kc == 0, kc == 15, [Wb[s][kc // 4], catB[kc]], [ps.b], kc == 15)
                    a = tmp()
                    rds = [] if l == 0 else [xsb[u][fc][hh]]
                    K.dma("sp", Tp[a][:, :512], xsrc(l, u, fc)[:, sl], rds, [TpB[a]])
                    stt(Tp[a][:, :512], ps.ap, modL[l % 2][:, 32 + fc, u:u + 1], Tp[a][:, :512], ALU.mult, ALU.add, [ps.b, modBL[l % 2], TpB[a]], [TpB[a]])
                    K.dma("sp", xsc[u, fc][:, sl], Tp[a][:, :512], [TpB[a]], [xsb[u][fc][hh]])
                    sqb = Tp[a][:, 512:].bitcast(BF16)[:, :512]
                    tt(sqb, Tp[a][:, :512], Tp[a][:, :512], ALU.mult, [TpB[a]], [TpB[a]])
                    mm(ssp[hh].ap, ONb, sqb, fc == 0, fc == 15, [TpB[a], csB], [ssp[hh].b], True)
        compute_rstd(u, ssp)
        K.unpin(ssp[0])
        K.unpin(ssp[1])
        if last_layer:
            for fc in range(16):
                a = tmp()
                K.dma("sp", Tp[a][:], xsc[u, fc], [xsb[u][fc][0], xsb[u][fc][1]], [TpB[a]])
                tt(Tp[a][:], Tp[a][:], rstd[:, u, :], ALU.mult, [TpB[a], rstdB[u]], [TpB[a]])
                ts(Tp[a][:], Tp[a][:], fnw_sb[:, fc:fc + 1], None, ALU.mult, None, [TpB[a], scB], [TpB[a]])
                K.dma("sp", yT[u, fc], Tp[a][:], [TpB[a]], [])

    def mod_compute(l):
        mod, gmod, modB = modL[l % 2], gmodL[l % 2], modBL[l % 2]
        K.dma("sp", abt[:], ppd[l][:, 0:64], [], [abtB])
        mp = K.pin()
        for t in range(12):
            s = w_get()
            for j in range(4):
                cc = t * 4 + j
                for kc in range(16):
                    mm(mp.ap[:, cc * 2:cc * 2 + 2], Wt[s][:, kc, j * 128:(j + 1) * 128], sc_bf[:, kc, :], kc == 0, kc == 15,
                       [Wb[s][kc // 4], scB], [mp.b], j == 3 and kc == 15)
        mpv = mp.ap[:, 0:96].rearrange("p (c u) -> p c u", u=2)
        for u in range(2):
            tt(mod[:, :, u], mpv[:, :, u], abt[:, 0:48], ALU.add, [mp.b, abtB], [modB])
            stt(gmod[:, :, u], mod[:, 16:32, u], 1.0, abt[:, 48:64], ALU.add, ALU.mult, [modB, abtB], [modB])
        K.unpin(mp)

    for u in range(2):
        prologue(u)
    K.marks.append((len(K.nodes), "L0 mod", K.npe))
    if stage > 0:
        mod_compute(0)
    for l in range(depth if stage > 0 else 0):
        pc = ppt[l % 2]
        pb = ppB[l % 2]
        K.dma("sp", pc[:], ppd[l], [], [pb])
        lm0 = PP["lam"][0]
        act(clam[:], pc[:, lm0:lm0 + 8], AF.Exp, [pb], [clamB], scale=-1.0)
        ts(clam[:], clam[:], 1.0, None, ALU.add, None, [clamB], [clamB])
        act(clam[:], clam[:], AF.Ln, [clamB], [clamB])
        ts(clam[:], clam[:], -8.0, None, ALU.mult, None, [clamB], [clamB])
        for u in range(2 if stage > 1 else 0):
            K.marks.append((len(K.nodes), "L%d u%d N" % (l, u), K.npe))
            phaseN(l, u, pc)
            K.marks.append((len(K.nodes), "L%d u%d A" % (l, u), K.npe))
            if stage > 2:
                branchA(l, u)
            else:
                [w_get() for _ in range(2)]
            K.marks.append((len(K.nodes), "L%d u%d B" % (l, u), K.npe))
            if stage > 3:
                branchB(l, u)
            else:
                [w_get() for _ in range(4)]
            K.marks.append((len(K.nodes), "L%d u%d C" % (l, u), K.npe))
            if stage > 4:
                branchC(l, u)
            else:
                [w_get() for _ in range(3)]
            K.marks.append((len(K.nodes), "L%d u%d D" % (l, u), K.npe))
            if stage > 5:
                branchD(l, u)
            else:
                [w_get() for _ in range(5)]
            if dbg and l == 0 and stage <= 6:
                for kc in range(16):
                    a = tmp()
                    cp(Tp[a][:], catT[:, kc, :], [catB[kc]], [TpB[a]])
                    K.dma("sp", o_dbg[u, kc], Tp[a][:], [TpB[a]], [])
            if stage <= 6:
                [w_get() for _ in range(4)]
                continue
            if dbg and l == 0:
                for kc in range(16):
                    a = tmp()
                    cp(Tp[a][:], catT[:, kc, :], [catB[kc]], [TpB[a]])
                    K.dma("sp", o_dbg[u, kc], Tp[a][:], [TpB[a]], [])
            K.marks.append((len(K.nodes), "L%d u%d W" % (l, u), K.npe))
            phaseW(l, u, l == depth - 1)
    K.marks.append((len(K.nodes), "end", K.npe))
    K.finish()
    return nc, K


def _consts():
    c = np.zeros((11, 128, 128), np.float32)
    i = np.arange(128)
    c[0] = np.eye(128)
    c[1] = 1.0
    partner = np.where((i % 64) < 32, i + 32, i - 32)
    c[2][partner, i] = 1.0
    same = (i[:, None] // 64) == (i[None, :] // 64)
    le = i[:, None] <= i[None, :]
    ge = i[:, None] >= i[None, :]
    c[3] = (same & le)
    c[4] = (same & ge)
    c[5] = le
    c[6] = ge
    c[7] = (i[:, None] < 64) * np.ones((1, 128))
    c[8] = (i[:, None] >= 64) * np.ones((1, 128))
    c[9] = (same & le)
    c[10] = (same & ge)
    return np.ascontiguousarray(c.transpose(1, 0, 2))


def _rope_tables():
    t = np.arange(NT)
    d = np.arange(128)
    pos = np.where(d[:, None] < 64, (t // 64)[None, :], (t % 64)[None, :]).astype(np.float32)
    nf = 32
    inv = (np.float32(10000.0) ** (-(np.arange(nf, dtype=np.float32)) / np.float32(nf))).astype(np.float32)
    ang = pos * inv[d % 32][:, None]
    C = np.cos(ang).astype(np.float32)
    S = np.sin(ang).astype(np.float32)
    sign = np.where((d % 64) < 32, -1.0, 1.0).astype(np.float32)
    return C, (S * sign[:, None]).astype(np.float32)


def _na_tables(rpb_l):
    col = np.arange(64)
    qcol = np.arange(64)
    qs = np.clip(qcol - 8, 0, 48)
    valid = (col[:, None] >= qs[None, :]) & (col[:, None] < qs[None, :] + 16)
    coff = np.clip(col[:, None] - qcol[None, :] + 15, 0, 30)
    g = np.zeros((4, 128, 16, 64), np.float32)
    m = np.zeros((128, 16, 64), np.float32)
    for s in range(16):
        for jj in range(2):
            if s < 14:
                ro, ok = s + jj, True
            elif s == 14:
                ro, ok = 2 + jj, jj == 1
            else:
                ro, ok = 10 + jj, jj == 0
            g[:, jj * 64:(jj + 1) * 64, s, :] = rpb_l[:, ro][:, coff]
            m[jj * 64:(jj + 1) * 64, s, :] = np.where(valid & ok, 0.0, -1e30)
    return g, m


_CACHE = {}


def _get_nc(depth=L, dbg=False):
    key = (depth, dbg)
    if key not in _CACHE:
        _CACHE[key] = build(depth, dbg)[0]
    return _CACHE[key]


def _tile_w(w, ntile):
    return np.ascontiguousarray(w.reshape(16, 128, ntile, 512).transpose(2, 1, 0, 3))


def prep_inputs(inp, depth=L):
    f = lambda a: np.ascontiguousarray(np.asarray(a, dtype=np.float32))
    g = {k: np.asarray(v) for k, v in inp.items()}
    sh = {}
    sh["ada_wT"] = np.stack([_tile_w(g["ada_w"][l], 12) for l in range(L)])
    sh["w_inT"] = np.stack([_tile_w(g["w_in"][l][:, :7168], 14) for l in range(L)])
    sh["w_ifT"] = f(g["w_in"][:, :, 7168:].reshape(L, 16, 128, 16).transpose(0, 2, 1, 3))
    sh["w_outT"] = np.stack([_tile_w(g["w_out"][l], 4) for l in range(L)])
    sh["fnw"] = f(g["final_norm_w"].reshape(16, 128).T)
    lw = np.stack([g["lru_wr"], g["lru_wi"]], axis=2)
    sh["lru_wT"] = f(lw.transpose(0, 4, 1, 2, 3, 5).reshape(L, 128, 16, 128))
    gm = [_na_tables(g["na_rpb"][l]) for l in range(L)]
    sh["rpbT"] = f(np.stack([x[0] for x in gm]))
    sh["maskT"] = f(gm[0][1])
    C, S = _rope_tables()
    sh["ropeC"], sh["ropeS"] = f(C), f(S)
    sh["cst"] = f(_consts())
    maps = []
    for c in range(NCORE):
        m = dict(sh)
        xp = g["x_prompt"][4 * c:4 * c + 4].reshape(NT, D)
        xs = g["x_sample"][c]
        m["x0T"] = f(np.stack([xp.T.reshape(16, 128, NT), xs.T.reshape(16, 128, NT)]))
        cond = np.stack([g["c_ctx"], g["c"][c]], axis=-1)
        m["condT"] = f(cond.reshape(16, 128, 2).transpose(1, 0, 2))
        pp = np.zeros((L, 128, NPP), np.float32)

        def put(name, arr):
            o, w = PP[name]
            pp[:, :, o:o + w] = arr.reshape(L, 128, w)
        put("ada_b", g["ada_b"].reshape(L, 48, 128).transpose(0, 2, 1))
        put("norm_w", g["norm_w"].reshape(L, 16, 128).transpose(0, 2, 1))
        put("conv_w", g["lru_conv_w"].reshape(L, 4, 4, 128).transpose(0, 3, 2, 1))
        put("conv_b", g["lru_conv_b"].reshape(L, 4, 128).transpose(0, 2, 1))
        lb = np.stack([g["lru_br"], g["lru_bi"]], axis=2)
        put("lru_b", lb.reshape(L, 2, 2, 4, 128).transpose(0, 4, 1, 2, 3))
        put("lam", g["lru_lambda"].reshape(L, 2, 4, 128).transpose(0, 3, 1, 2))
        put("qnw", g["gqa_qnorm"].reshape(L, 128, 1))
        put("knw", g["gqa_knorm"].reshape(L, 128, 1))
        put("gb", np.broadcast_to(g["ml_gate_b"][:, None, :], (L, 128, 16)))
        put("won", np.broadcast_to(g["ml_out_norm"][:, None, :], (L, 128, 512)))
        put("h0", g["state_lru"][c].reshape(L, 2, 4, 128).transpose(0, 3, 1, 2))
        put("m0", np.broadcast_to(g["state_mlstm_m"][c].reshape(L, 1, 8), (L, 128, 8)))
        m["pp"] = pp
        m["c_nakT"] = f(g["cache_na_k"][c].transpose(0, 3, 2, 1))
        m["c_nav"] = f(g["cache_na_v"][c].reshape(L, 4, 128, 512).transpose(0, 2, 1, 3))
        m["c_gkT"] = f(g["cache_gqa_k"][c].transpose(0, 3, 2, 1))
        m["c_gv"] = f(g["cache_gqa_v"][c].reshape(L, 4, 128, 256).transpose(0, 2, 1, 3))
        Cn = np.concatenate([g["state_mlstm_C"][c], g["state_mlstm_n"][c][..., None]], axis=-1)
        m["C0n0"] = f(Cn.transpose(0, 3, 1, 2, 4))
        maps.append(m)
    return maps


def assemble(res):
    B = 32
    y_p = np.zeros((B, 256, D), np.float32)
    y_s = np.zeros((8, NT, D), np.float32)
    nak = np.zeros((B, L, 256, 4, 128), np.float32)
    nav = np.zeros((B, L, 256, 4, 128), np.float32)
    gk = np.zeros((B, L, 256, 2, 128), np.float32)
    gv = np.zeros((B, L, 256, 2, 128), np.float32)
    lru = np.zeros((B, L, 2, 512), np.float32)
    Cm = np.zeros((B, L, 2, 4, 128, 128), np.float32)
    nm = np.zeros((B, L, 2, 4, 128), np.float32)
    mm_ = np.zeros((B, L, 2, 4), np.float32)
    for c, r in enumerate(res):
        bs = slice(4 * c, 4 * c + 4)
        yT = r["yT"]
        y_p[bs] = yT[0].reshape(D, NT).T.reshape(4, 256, D)
        y_s[c] = yT[1].reshape(D, NT).T
        nak[bs] = r["o_nak"].reshape(L, 128, 4, 4, 256).transpose(3, 0, 4, 2, 1)
        nav[bs] = r["o_nav"].transpose(0, 2, 1, 3).reshape(L, 4, 256, 4, 128).transpose(1, 0, 2, 3, 4)
        gk[bs] = r["o_gk"].reshape(L, 128, 2, 4, 256).transpose(3, 0, 4, 2, 1)
        gv[bs] = r["o_gv"].transpose(0, 2, 1, 3).reshape(L, 4, 256, 2, 128).transpose(1, 0, 2, 3, 4)
        lru[bs] = r["o_lru"].transpose(4, 0, 2, 3, 1).reshape(4, L, 2, 512)
        oc = r["o_C"].transpose(3, 0, 2, 4, 1, 5)
        Cm[bs] = oc[..., :128]
        nm[bs] = oc[..., 128]
        mm_[bs] = r["o_m"].reshape(L, 2, 4, 4).transpose(2, 0, 1, 3)
    return (y_p, y_s, nak, nav, gk, gv, lru, Cm, nm, mm_)


def kernel(**inputs):
    nc = _get_nc()
    maps = prep_inputs(inputs)
    out = run_bass_kernel_spmd(nc, maps, core_ids=list(range(NCORE)))
    return assemble(out.results)
```

```python
import numpy as np
import concourse.bass as bass
import concourse.mybir as mybir
from concourse.bass_utils import run_bass_kernel_spmd
from concourse.ap import AP

F32 = mybir.dt.float32
BF16 = mybir.dt.bfloat16
ALU = mybir.AluOpType
AF = mybir.ActivationFunctionType
AX = mybir.AxisListType

L = 4
D = 2048
NT = 1024
NCORE = 8
EPS = 1e-6
SCL = 128.0 ** -0.5
PP = {}
_o = 0
for _n, _w in (("ada_b", 48), ("norm_w", 16), ("conv_w", 16), ("conv_b", 4), ("lru_b", 16), ("lam", 8),
               ("qnw", 1), ("knw", 1), ("gb", 16), ("won", 512), ("h0", 8), ("m0", 8)):
    PP[_n] = (_o, _w)
    _o += _w
NPP = _o


class Buf:
    __slots__ = ("name", "w", "r")

    def __init__(self, name):
        self.name = name
        self.w = None
        self.r = []


class PS:
    def __init__(self, t, b):
        self.t = t
        self.b = b
        self.ap = t[:]


class Node:
    __slots__ = ("i", "e", "fns", "deps", "dur", "lat", "kind", "args", "succ", "nd", "ready", "fin", "tag", "raw", "fs")

    def __init__(self, i, e, kind):
        self.i = i
        self.e = e
        self.kind = kind
        self.fns = []
        self.deps = set()
        self.raw = set()
        self.fs = None
        self.dur = 0.0
        self.lat = 0.0
        self.args = None
        self.succ = []
        self.tag = None


class KB:
    def __init__(self, nc):
        self.nc = nc
        self.eng = {"pe": nc.tensor, "act": nc.scalar, "dve": nc.vector, "pool": nc.gpsimd, "sp": nc.sync}
        self.sem = {k: nc.alloc_semaphore("s_" + k) for k in ("pe", "act", "dve", "pool")}
        self.cnt = {k: 0 for k in self.sem}
        self.seen = {k: {} for k in self.eng}
        self.dsem = []
        self.dval = []
        self.dq = {"sp": [], "pool": [], "act": []}
        self.dqi = {"sp": 0, "pool": 0, "act": 0}
        for q, n in (("sp", 14), ("pool", 8), ("act", 2)):
            for i in range(n):
                self.dq[q].append(len(self.dsem))
                self.dsem.append(nc.alloc_semaphore("d_%s%d" % (q, i)))
                self.dval.append(0)
        self.banks = []
        for i in range(8):
            t = nc.alloc_psum_tensor("ps%d" % i, [128, 512], F32)
            self.banks.append(PS(t, Buf("ps%d" % i)))
        self.pinned = set()
        self.slots = []
        for b in (6, 7):
            for j in range(3):
                self.slots.append((self.banks[b].t[:, j * 170:(j + 1) * 170], Buf("slot%d_%d" % (b, j))))
        self.si = 0
        self.pbi = 0
        self.bi = 0
        self.nins = 0
        self.npe = 0
        self.marks = []
        self.nodes = []
        self.open_pe = None

    def _deps(self, node, reads, writes):
        d = node.deps
        for b in reads:
            if b.w is not None:
                d.add(b.w)
                node.raw.add(b.w)
        for b in writes:
            if b.w is not None:
                d.add(b.w)
            d.update(b.r)
        for b in reads:
            b.r.append(node.i)
        for b in writes:
            b.w = node.i
            b.r = []

    def op(self, e, fn, reads=(), writes=(), inc=True, c=0.1, fs=None):
        self.nins += 1
        if e == "pe":
            self.npe += 1
            if self.open_pe is not None:
                node = self.open_pe
            else:
                node = Node(len(self.nodes), e, "op")
                self.nodes.append(node)
            node.fns.append(fn)
            node.dur += c
            self._deps(node, reads, writes)
            node.deps.discard(node.i)
            self.open_pe = None if inc else node
            return
        node = Node(len(self.nodes), e, "op")
        self.nodes.append(node)
        node.fns.append(fn)
        node.dur = c
        node.fs = fs
        self._deps(node, reads, writes)

    def dma(self, q, out, in_, reads=(), writes=(), nbytes=65536):
        self.nins += 1
        node = Node(len(self.nodes), q, "dma")
        self.nodes.append(node)
        node.args = (out, in_)
        node.dur = 0.06
        node.lat = 2.0 + nbytes / 150e3
        self._deps(node, reads, writes)

    def schedule(self):
        import heapq
        nodes = self.nodes
        for n in nodes:
            n.nd = len(n.deps)
            n.ready = 0.0
        for n in nodes:
            for d in n.deps:
                nodes[d].succ.append(n.i)
        XL = 0.35
        tail = [0.0] * len(nodes)
        for n in reversed(nodes):
            t = 0.0
            for si in n.succ:
                x = tail[si] + ((0.25 if n.e == "dve" else (0.2 if n.e == "act" else 0.05)) if nodes[si].e == n.e else XL)
                if x > t:
                    t = x
            tail[n.i] = t + n.dur + n.lat
        engs = list(self.eng.keys())
        waiting = {e: [] for e in engs}
        avail = {e: [] for e in engs}
        free_at = {e: 0.0 for e in engs}
        for n in nodes:
            if n.nd == 0:
                heapq.heappush(waiting[n.e], (0.0, n.i))
        order = []
        cur_fs = [None]
        remaining = len(nodes)
        while remaining:
            best = None
            for e in engs:
                w = waiting[e]
                a = avail[e]
                fa = free_at[e]
                while w and w[0][0] <= fa:
                    wi = heapq.heappop(w)[1]
                    heapq.heappush(a, (-tail[wi], wi))
                if a:
                    cand = (fa, a[0][1], e, True)
                elif w:
                    cand = (w[0][0], w[0][1], e, False)
                else:
                    continue
                if best is None or cand < best:
                    best = cand
            st, i, e, fromavail = best
            if fromavail:
                if e == "act":
                    a = avail[e]
                    cand_l = [heapq.heappop(a) for _ in range(min(8, len(a)))]
                    pick = 0
                    for ci_, (_, ni_) in enumerate(cand_l):
                        if nodes[ni_].fs is None or nodes[ni_].fs == cur_fs[0]:
                            pick = ci_
                            break
                    i = cand_l[pick][1]
                    for ci_, it_ in enumerate(cand_l):
                        if ci_ != pick:
                            heapq.heappush(a, it_)
                else:
                    heapq.heappop(avail[e])
            else:
                heapq.heappop(waiting[e])
            n = nodes[i]
            if e == "act" and n.fs is not None:
                if n.fs != cur_fs[0]:
                    st += 1.3
                cur_fs[0] = n.fs
            free_at[e] = st + n.dur
            n.fin = st + n.dur + n.lat
            order.append(i)
            remaining -= 1
            for si in n.succ:
                sn = nodes[si]
                r = n.fin + ((0.25 if e == "dve" else (0.2 if e == "act" else 0.05)) if sn.e == e else XL)
                if r > sn.ready:
                    sn.ready = r
                sn.nd -= 1
                if sn.nd == 0:
                    heapq.heappush(waiting[sn.e], (sn.ready, si))
        self.makespan = max(free_at.values())
        return order

    def _wait(self, e, key, val):
        if self.seen[e].get(key, 0) >= val:
            return
        sem = self.sem[key] if isinstance(key, str) else self.dsem[key[1]]
        self.eng[e].wait_ge(sem, val)
        self.seen[e][key] = val

    def finish(self, sched=True):
        assert self.open_pe is None
        nodes = self.nodes
        order = self.schedule() if sched else list(range(len(nodes)))
        for i in order:
            n = nodes[i]
            e = n.e
            need = {}
            for d in n.deps:
                k, v = nodes[d].tag
                if k == e and (e == "pe" or (e == "act" and d not in n.raw)):
                    continue
                if need.get(k, 0) < v:
                    need[k] = v
            for k, v in need.items():
                self._wait(e, k, v)
            if n.kind == "dma":
                lst = self.dq[e]
                si = lst[self.dqi[e] % len(lst)]
                self.dqi[e] += 1
                self._wait(e, ("d", si), self.dval[si])
                ins = self.eng[e].dma_start(out=n.args[0], in_=n.args[1])
                self.dval[si] += 16
                ins.then_inc(self.dsem[si], 16)
                n.tag = (("d", si), self.dval[si])
            else:
                ins = None
                for fn in n.fns:
                    ins = fn()
                self.cnt[e] += 1
                ins.then_inc(self.sem[e], 1)
                n.tag = (e, self.cnt[e])
        for si in range(len(self.dsem)):
            if self.dval[si]:
                self._wait("sp", ("d", si), self.dval[si])
        for k in self.sem:
            if self.cnt[k]:
                self._wait("sp", k, self.cnt[k])

    def psum(self):
        for _ in range(16):
            i = 2 + self.bi % 6
            self.bi += 1
            if i not in self.pinned:
                return self.banks[i]
        raise RuntimeError("no psum")

    def ppsum(self):
        i = self.pbi % 2
        self.pbi += 1
        return self.banks[i]

    def pslot(self):
        p = self.psum()
        return p.ap, p.b

    def pin(self):
        p = self.psum()
        self.pinned.add(self.banks.index(p))
        return p

    def unpin(self, p):
        self.pinned.discard(self.banks.index(p))


def rev(a):
    pairs = [list(p) for p in a.ap]
    st, n = pairs[-1]
    pairs[-1] = [-st, n]
    return AP(a.tensor, a.offset + st * (n - 1), pairs)


def build(depth=L, dbg=False, stage=99):
    nc = bass.Bass("TRN2", target_bir_lowering=False)
    K = KB(nc)
    E = K.eng

    def din(name, shape):
        return nc.dram_tensor(name, list(shape), F32, kind="ExternalInput").ap()

    def dout(name, shape):
        return nc.dram_tensor(name, list(shape), F32, kind="ExternalOutput").ap()

    x0T = din("x0T", [2, 16, 128, NT])
    condT = din("condT", [128, 16, 2])
    ada_wT = din("ada_wT", [L, 12, 128, 16, 512])
    w_inT = din("w_inT", [L, 14, 128, 16, 512])
    w_ifT = din("w_ifT", [L, 128, 16, 16])
    w_outT = din("w_outT", [L, 4, 128, 16, 512])
    ppd = din("pp", [L, 128, NPP])
    fnw = din("fnw", [128, 16])
    lru_wT = din("lru_wT", [L, 128, 16, 128])
    rpbT = din("rpbT", [L, 4, 128, 16, 64])
    maskT = din("maskT", [128, 16, 64])
    c_nakT = din("c_nakT", [L, 128, 4, 512])
    c_nav = din("c_nav", [L, 128, 4, 512])
    c_gkT = din("c_gkT", [L, 128, 2, 512])
    c_gv = din("c_gv", [L, 128, 4, 256])
    ropeC = din("ropeC", [128, NT])
    ropeS = din("ropeS", [128, NT])
    cst = din("cst", [128, 11, 128])
    C0n0 = din("C0n0", [L, 128, 2, 4, 129])
    yT = dout("yT", [2, 16, 128, NT])
    o_nak = dout("o_nak", [L, 128, 4, NT])
    o_nav = dout("o_nav", [L, 128, 8, 512])
    o_gk = dout("o_gk", [L, 128, 2, NT])
    o_gv = dout("o_gv", [L, 128, 8, 256])
    o_lru = dout("o_lru", [L, 128, 2, 4, 4])
    o_C = dout("o_C", [L, 128, 2, 4, 4, 129])
    o_m = dout("o_m", [L, 32, 1])
    xsc = nc.dram_tensor("xsc", [2, 16, 128, NT], F32, kind="Internal").ap()
    if dbg:
        o_dbg = dout("o_dbg", [2, 16, 128, NT])
    xsb = [[[Buf("xs") for _ in range(2)] for _ in range(16)] for _ in range(2)]

    def sb(name, shape, dt):
        return nc.alloc_sbuf_tensor(name, list(shape), dt)

    Wt = [sb("W%d" % i, [128, 16, 512], BF16) for i in range(2)]
    Wb = [[Buf("W%d_%d" % (i, j)) for j in range(4)] for i in range(2)]
    xmT = sb("xmT", [128, 16, NT], BF16)
    xmB = [Buf("xm%d" % i) for i in range(16)]
    catT = sb("catT", [128, 16, NT], BF16)
    catB = [Buf("cat%d" % i) for i in range(16)]
    NREG = 6
    Rg = [sb("R%d" % i, [128, 4096], BF16) for i in range(NREG)]
    RgB = [Buf("R%d" % i) for i in range(NREG)]
    NTMP = 6
    Tp = [sb("T%d" % i, [128, 1024], F32) for i in range(NTMP)]
    TpB = [Buf("T%d" % i) for i in range(NTMP)]
    cs = sb("cs", [128, 11, 128], F32)
    csB = Buf("cs")
    cs_bf = sb("cs_bf", [128, 2, 128], BF16)
    rstd = sb("rstd", [128, 2, NT], F32)
    rstdB = [Buf("rstd0"), Buf("rstd1")]
    ppt = [sb("pp0", [128, NPP], F32)] * 2
    ppB = [Buf("pp0")] * 2
    fnw_sb = sb("fnw_sb", [128, 16], F32)
    sc_bf = sb("sc_bf", [128, 16, 2], BF16)
    cond_sb = sb("cond_sb", [128, 16, 2], F32)
    scB = Buf("sc")
    modL = [sb("mod%d" % i, [128, 48, 2], F32) for i in range(2)]
    gmodL = [sb("gmod%d" % i, [128, 16, 2], F32) for i in range(2)]
    modBL = [Buf("mod0"), Buf("mod1")]
    abt = sb("abt", [128, 64], F32)
    abtB = Buf("abt")
    wl = Rg[2][:, 2048:4096].rearrange("p (a t) -> p a t", a=16)
    wlB = RgB[2]
    clam = sb("clam", [128, 8], F32)
    clamB = Buf("clam")
    lruS = sb("lruS", [128, 2, 4, 4], F32)
    lruSB = Buf("lruS")
    maskS = sb("maskS", [128, 16, 64], F32)[:]
    maskB = Buf("mask")
    wif = sb("wif", [128, 16, 16], BF16)
    wifB = Buf("wif")
    NPT = 4
    pTt = [sb("pT%d" % i, [128, 512], BF16) for i in range(NPT)]
    pTB = [Buf("pT%d" % i) for i in range(NPT)]
    NSM = 8
    smt = [sb("sm%d" % i, [128, 128], BF16) for i in range(NSM)]
    smB = [Buf("sm%d" % i) for i in range(NSM)]
    St = [[sb("St%d%d" % (d, h), [128, 129], F32) for h in range(4)] for d in range(2)]
    Stb = [[sb("Stb%d%d" % (d, h), [128, 129], BF16) for h in range(4)] for d in range(2)]
    StB = [[Buf("St") for h in range(4)] for d in range(2)]
    StbB = [[Buf("Stb") for h in range(4)] for d in range(2)]
    gt = sb("gt", [128, 8, 16], F32)
    nlf = sb("nlf", [128, 2, 8, 4], F32)
    nbt = sb("nbt", [128, 2, 8, 4], F32)
    eat = sb("eat", [128, 2, 8, 4], F32)
    ea01 = sb("ea01", [128, 2, 2, 8, 4], F32)
    enb = sb("enb", [128, 2, 8, 4], F32)
    eBt = sb("eBt", [128, 2, 2, 8, 4], F32)
    zt = sb("zt", [128, 2, 32], F32)
    nlfz = sb("nlfz", [128, 2, 32], F32)
    gB = Buf("gates")
    mfs = sb("mfs", [32, 8], F32)
    emfbc = sb("emfbc", [128, 32], F32)
    em0 = sb("em0", [128, 8], F32)
    tiny = [sb("tiny%d" % i, [128, 8], F32) for i in range(6)]
    tinyB = [Buf("tiny%d" % i) for i in range(6)]
    stg = [sb("stg%d" % i, [128, 129], F32) for i in range(2)]
    stgB = [Buf("stg0"), Buf("stg1")]
    cnt = {"pt": 0, "sm": 0, "tiny": 0, "stg": 0, "tp": 0}

    def nxt(kind, n):
        i = cnt[kind] % n
        cnt[kind] += 1
        return i

    reserved = set()

    def tmp():
        while True:
            i = cnt["tp"] % NTMP
            cnt["tp"] += 1
            if i not in reserved:
                return i

    IDf, ONf, PMf, TRF, TRB, TFF, TFB, H0, H1, MBF, MBB = [cs[:, i, :] for i in range(11)]
    IDb = cs_bf[:, 0, :]
    ONb = cs_bf[:, 1, :]

    def fsz(a):
        n = 1
        for x in a.shape[1:]:
            n *= x
        return n

    def mm(out, lhsT, rhs, st, sp, R, Wr, inc):
        n = fsz(rhs)
        c = 0.03 + max(n, 64) * 0.00045
        if rhs.dtype == F32:
            c *= 4
        K.op("pe", lambda: nc.tensor.matmul(out, lhsT, rhs, start=st, stop=sp, skip_group_check=True), R, Wr, inc, c=c)

    def act(out, in_, func, R, Wr, scale=1.0, bias=0.0):
        fs = {AF.Exp: "e", AF.Ln: "e", AF.Silu: "s", AF.Sigmoid: "g", AF.Sqrt: "q"}.get(func)
        K.op("act", lambda: nc.scalar.activation(out=out, in_=in_, func=func, bias=bias, scale=scale), R, Wr, c=0.2 + fsz(out) * 0.00105, fs=fs)

    def tt(out, in0, in1, op, R, Wr, e="dve"):
        K.op(e, lambda: E[e].tensor_tensor(out=out, in0=in0, in1=in1, op=op), R, Wr, c=0.1 + fsz(out) * 0.00105)

    def ts(out, in0, s1, s2, op0, op1, R, Wr, e="dve"):
        c = 0.1 + fsz(out) * 0.0008
        if s2 is None:
            K.op(e, lambda: E[e].tensor_scalar(out=out, in0=in0, scalar1=s1, scalar2=None, op0=op0), R, Wr, c=c)
        else:
            K.op(e, lambda: E[e].tensor_scalar(out=out, in0=in0, scalar1=s1, scalar2=s2, op0=op0, op1=op1), R, Wr, c=c)

    def stt(out, in0, sc, in1, op0, op1, R, Wr, e="dve"):
        K.op(e, lambda: E[e].scalar_tensor_tensor(out=out, in0=in0, scalar=sc, in1=in1, op0=op0, op1=op1), R, Wr, c=0.1 + fsz(out) * 0.00105)

    def rsq(out, in_, mul, R, Wr):
        act(out, in_, AF.Ln, R, Wr, scale=mul, bias=EPS)
        act(out, out, AF.Exp, Wr, Wr, scale=-0.5)

    def cp(out, in_, R, Wr, e="dve"):
        K.op(e, lambda: E[e].tensor_copy(out=out, in_=in_), R, Wr, c=0.1 + fsz(out) * 0.0008)

    wsched = []
    for t in range(12):
        wsched.append(ada_wT[0, t])
    for l in range(depth):
        for u in range(2):
            for t in range(14):
                wsched.append(w_inT[l, t])
            if u == 1 and l + 1 < depth:
                for t in range(12):
                    wsched.append(ada_wT[l + 1, t])
            for t in range(4):
                wsched.append(w_outT[l, t])
    wstate = {"issued": 0, "used": 0}

    def w_issue():
        i = wstate["issued"]
        if i >= len(wsched):
            return
        s = i % 2
        for qd in range(4):
            K.dma("pool", Wt[s][:, qd * 4:(qd + 1) * 4, :], wsched[i][:, qd * 4:(qd + 1) * 4, :], [], [Wb[s][qd]], nbytes=1 << 20)
        wstate["issued"] += 1

    def w_get():
        i = wstate["used"]
        while wstate["issued"] <= min(i + 1, len(wsched) - 1):
            w_issue()
        wstate["used"] += 1
        return i % 2

    K.dma("sp", cs[:], cst, [], [csB])
    cp(cs_bf[:, 0, :], cs[:, 0, :], [csB], [csB])
    cp(cs_bf[:, 1, :], cs[:, 1, :], [csB], [csB])
    K.dma("sp", cond_sb[:], condT, [], [scB])
    act(sc_bf[:], cond_sb[:], AF.Silu, [scB], [scB])
    K.dma("sp", fnw_sb[:], fnw, [], [scB])
    K.dma("sp", maskS, maskT, [], [maskB])

    def compute_rstd(u, ssp):
        for hh in range(2):
            sl = slice(hh * 512, (hh + 1) * 512)
            rsq(rstd[:, u, sl], ssp[hh].ap, 1.0 / D, [ssp[hh].b], [rstdB[u]])

    def prologue(u):
        ssp = [K.pin(), K.pin()]
        for kc in range(16):
            a = tmp()
            b = tmp()
            K.dma("sp", Tp[a][:], x0T[u, kc], [], [TpB[a]])
            sqb = Tp[b][:].bitcast(BF16)
            act(sqb[:, :NT], Tp[a][:], AF.Square, [TpB[a]], [TpB[b]])
            for hh in range(2):
                mm(ssp[hh].ap, ONb, sqb[:, hh * 512:(hh + 1) * 512], kc == 0, kc == 15, [TpB[b], csB], [ssp[hh].b], True)
        compute_rstd(u, ssp)
        K.unpin(ssp[0])
        K.unpin(ssp[1])

    def xsrc(l, u, fc):
        return x0T[u, fc] if l == 0 else xsc[u, fc]

    def phaseN(l, u, pcur):
        pb = ppB[l % 2]
        sh0 = 0
        for kc in range(16):
            a = tmp()
            rds = [] if l == 0 else [xsb[u][kc][0], xsb[u][kc][1]]
            K.dma("sp", Tp[a][:], xsrc(l, u, kc), rds, [TpB[a]])
            tt(Tp[a][:], Tp[a][:], rstd[:, u, :], ALU.mult, [TpB[a], rstdB[u]], [TpB[a]])
            act(xmT[:, kc, :], Tp[a][:], AF.Identity, [TpB[a], modBL[l % 2]], [xmB[kc]],
                scale=gmodL[l % 2][:, kc, u:u + 1], bias=modL[l % 2][:, sh0 + kc, u:u + 1])

    def proj_fm(s, col0, ncols, evac):
        for j in range(ncols // 128):
            for hh in range(2):
                ps = K.ppsum()
                for kc in range(16):
                    mm(ps.ap, Wt[s][:, kc, col0 + j * 128: col0 + (j + 1) * 128], xmT[:, kc, hh * 512:(hh + 1) * 512],
                       kc == 0, kc == 15, [Wb[s][kc // 4], xmB[kc]], [ps.b], kc == 15)
                evac(j, hh, ps)

    def proj_tm(s, col0, ncols, evac, wt=None, wb=None):
        for t8 in range(8):
            ps = K.ppsum()
            for kc in range(16):
                if wt is None:
                    rhs = Wt[s][:, kc, col0:col0 + ncols]
                    rb = Wb[s][kc // 4]
                else:
                    rhs = wt[:, kc, :]
                    rb = wb
                mm(ps.ap[:, :ncols], xmT[:, kc, t8 * 128:(t8 + 1) * 128], rhs, kc == 0, kc == 15, [rb, xmB[kc]], [ps.b], kc == 15)
            evac(t8, ps)

    def R4(i):
        return Rg[i][:].rearrange("p (a t) -> p a t", a=4)

    def R8(i):
        return Rg[i][:].rearrange("p (a t) -> p a t", a=8)

    def Rf(i):
        return Rg[i][:].bitcast(F32).rearrange("p (a t) -> p a t", a=2)

    def attn(steps, nq, out_ap, outB, gate_ap, gateB):
        O = K.pin()
        Rr = K.pin()
        nst = len(steps)

        def emitS(st):
            kT, kR, q, qR, n, off, bias, bR, v, vR = st
            kTs = kT if isinstance(kT, list) else [kT]
            G = len(kTs)
            ps = K.psum()
            ps_ap, ps_b = ps.ap, ps.b
            for j in range(G):
                mm(ps_ap[:, j * n:(j + 1) * n], kTs[j], q, True, True, kR + qR, [ps_b], j == G - 1)
            pi = nxt("pt", NPT)
            w = G * n
            if bias is not None:
                a = tmp()
                if G > 1:
                    o_ = Tp[a][:, :w].rearrange("p (g t) -> p g t", g=G)
                    i_ = ps_ap[:, :w].rearrange("p (g t) -> p g t", g=G)
                else:
                    o_, i_ = Tp[a][:, :w], ps_ap[:, :w]
                stt(o_, i_, SCL, bias, ALU.mult, ALU.add, [ps_b] + bR, [TpB[a]])
                act(pTt[pi][:, :w], Tp[a][:, :w], AF.Exp, [TpB[a]], [pTB[pi]])
            else:
                act(pTt[pi][:, :w], ps_ap[:, :w], AF.Exp, [ps_b], [pTB[pi]], scale=SCL)
            return pi

        cur = emitS(steps[0])
        for i in range(nst):
            nx = emitS(steps[i + 1]) if i + 1 < nst else None
            kT, kR, q, qR, n, off, bias, bR, v, vR = steps[i]
            vs = v if isinstance(v, list) else [v]
            G = len(vs)
            last = i == nst - 1
            for j in range(G):
                lj = last and j == G - 1
                mm(O.ap[:, off:off + n], vs[j], pTt[cur][:, j * n:(j + 1) * n], i == 0 and j == 0, lj, vR + [pTB[cur]], [O.b], False)
                mm(Rr.ap[:, off:off + n], ONb, pTt[cur][:, j * n:(j + 1) * n], i == 0 and j == 0, lj, [pTB[cur], csB], [Rr.b, O.b], j == G - 1)
            cur = nx
        a = tmp()
        act(Tp[a][:, :nq], Rr.ap[:, :nq], AF.Ln, [Rr.b], [TpB[a]])
        act(Tp[a][:, :nq], Tp[a][:, :nq], AF.Exp, [TpB[a]], [TpB[a]], scale=-1.0)
        tt(Tp[a][:, :nq], O.ap[:, :nq], Tp[a][:, :nq], ALU.mult, [O.b, TpB[a]], [TpB[a]])
        tt(out_ap, Tp[a][:, :nq], gate_ap, ALU.mult, [TpB[a]] + gateB, outB)
        K.unpin(O)
        K.unpin(Rr)

    def branchA(l, u):
        pc = ppt[l % 2]
        pb = ppB[l % 2]
        XT = R4(0)
        GT = R4(1)
        xabB = Buf("xab")
        s = w_get()
        proj_fm(s, 0, 512, lambda j, hh, ps: cp(XT[:, j, hh * 512:(hh + 1) * 512], ps.ap, [ps.b], [RgB[0]]))
        s = w_get()
        proj_fm(s, 0, 512, lambda j, hh, ps: act(GT[:, j, hh * 512:(hh + 1) * 512], ps.ap, AF.Silu, [ps.b], [RgB[1]]))
        cw0 = PP["conv_w"][0]
        cb0 = PP["conv_b"][0]
        lb0 = PP["lru_b"][0]
        h00 = PP["h0"][0]
        nseq, Ts = (4, 256) if u == 0 else (1, 1024)
        tl = list(range(NTMP))
        K.dma("pool", wl, lru_wT[l], [], [wlB])
        for g in range(4):
            X = XT[:, g, :]
            XB = RgB[0]
            ixa, ir, ii, itm, ihf, ihb = tl
            xa = Tp[ixa][:]
            xab = Rg[2][:, :NT]
            ts(xa, X, pc[:, cw0 + g * 4 + 1: cw0 + g * 4 + 2], pc[:, cb0 + g: cb0 + g + 1], ALU.mult, ALU.add, [XB, pb], [TpB[ixa]])
            Xv = X.rearrange("p (s t) -> p s t", s=nseq)
            xav = xa.rearrange("p (s t) -> p s t", s=nseq)
            stt(xav[:, :, 1:], Xv[:, :, :Ts - 1], pc[:, cw0 + g * 4: cw0 + g * 4 + 1], xav[:, :, 1:], ALU.mult, ALU.add, [XB, pb, TpB[ixa]], [TpB[ixa]])
            stt(xav[:, :, :Ts - 1], Xv[:, :, 1:], pc[:, cw0 + g * 4 + 2: cw0 + g * 4 + 3], xav[:, :, :Ts - 1], ALU.mult, ALU.add, [XB, pb, TpB[ixa]], [TpB[ixa]])
            stt(xav[:, :, :Ts - 2], Xv[:, :, 2:], pc[:, cw0 + g * 4 + 3: cw0 + g * 4 + 4], xav[:, :, :Ts - 2], ALU.mult, ALU.add, [XB, pb, TpB[ixa]], [TpB[ixa]])
            act(xab, xa, AF.Copy, [TpB[ixa]], [RgB[2]])
            for dr in range(2):
                gts = [Tp[ir][:], Tp[ii][:]]
                gtb = [TpB[ir], TpB[ii]]
                for ri in range(2):
                    for hh in range(2):
                        ps = K.psum()
                        mm(ps.ap, wl[:, dr * 8 + ri * 4 + g, :], xab[:, hh * 512:(hh + 1) * 512], True, True, [wlB, RgB[2]], [ps.b], True)
                        c = lb0 + dr * 8 + ri * 4 + g
                        act(gts[ri][:, hh * 512:(hh + 1) * 512], ps.ap, AF.Sigmoid, [ps.b, pb], [gtb[ri]], bias=pc[:, c:c + 1])
                av = Tp[ir][:]
                bv = Tp[ii][:]
                tm = Tp[itm][:]
                act(av, av, AF.Exp, [TpB[ir], clamB], [TpB[ir]], scale=clam[:, dr * 4 + g: dr * 4 + g + 1])
                tt(tm, av, av, ALU.mult, [TpB[ir]], [TpB[itm]])
                act(tm, tm, AF.Sqrt, [TpB[itm]], [TpB[itm]], scale=-1.0, bias=1.0)
                tt(bv, tm, bv, ALU.mult, [TpB[itm], TpB[ii]], [TpB[ii]])
                tt(bv, bv, xa, ALU.mult, [TpB[ii], TpB[ixa]], [TpB[ii]])
                ih = ihf if dr == 0 else ihb
                hv = Tp[ih][:]
                for sq in range(nseq):
                    sl = slice(sq * Ts, (sq + 1) * Ts)
                    if u == 0:
                        init = 0.0
                    else:
                        init = pc[:, h00 + dr * 4 + g: h00 + dr * 4 + g + 1]
                    if dr == 0:
                        o_, a_, b_ = hv[:, sl], av[:, sl], bv[:, sl]
                    else:
                        o_, a_, b_ = rev(hv[:, sl]), rev(av[:, sl]), rev(bv[:, sl])
                    K.op("dve", lambda o_=o_, a_=a_, b_=b_, init=init: nc.vector.tensor_tensor_scan(
                        out=o_, data0=a_, data1=b_, initial=init, op0=ALU.mult, op1=ALU.add),
                        [TpB[ir], TpB[ii], pb], [TpB[ih]], c=0.1 + Ts * 0.00105)
                if u == 0:
                    hvv = hv.rearrange("p (s t) -> p s t", s=4)
                    col = 255 if dr == 0 else 0
                    act(lruS[:, dr, g, :], hvv[:, :, col], AF.Copy, [TpB[ih]], [lruSB])
            tt(Tp[ihf][:], Tp[ihf][:], Tp[ihb][:], ALU.add, [TpB[ihf], TpB[ihb]], [TpB[ihf]])
            tt(catT[:, g, :], Tp[ihf][:], GT[:, g, :], ALU.mult, [TpB[ihf], RgB[1]], [catB[g]])
        if u == 0:
            K.dma("sp", o_lru[l], lruS[:], [lruSB], [])

    def branchB(l, u):
        qT, kT, vv, gT = R4(3), R4(4), R8(5), R4(0)
        s = w_get()
        proj_fm(s, 0, 512, lambda j, hh, ps: cp(qT[:, j, hh * 512:(hh + 1) * 512], ps.ap, [ps.b], [RgB[3]]))
        s = w_get()

        def ev_k(j, hh, ps):
            sl = slice(hh * 512, (hh + 1) * 512)
            if u == 0:
                a = tmp()
                act(Tp[a][:, :512], ps.ap, AF.Copy, [ps.b], [TpB[a]])
                K.dma("sp", o_nak[l, :, j, sl], Tp[a][:, :512], [TpB[a]], [])
                cp(kT[:, j, sl], Tp[a][:, :512], [TpB[a]], [RgB[4]])
            else:
                cp(kT[:, j, sl], ps.ap, [ps.b], [RgB[4]])
        import os
        sub = int(os.environ.get("KSUB", "9"))
        if sub == -1:
            [w_get() for _ in range(3)]
            return
        proj_fm(s, 0, 512, ev_k)
        s = w_get()
        if sub == -2:
            [w_get() for _ in range(2)]
            return

        def ev_v(t8, ps):
            if u == 0:
                a = tmp()
                act(Tp[a][:, :512], ps.ap, AF.Copy, [ps.b], [TpB[a]])
                K.dma("sp", o_nav[l, :, t8, :], Tp[a][:, :512], [TpB[a]], [])
                cp(vv[:, t8, :], Tp[a][:, :512], [TpB[a]], [RgB[5]])
            else:
                cp(vv[:, t8, :], ps.ap, [ps.b], [RgB[5]])
        proj_tm(s, 0, 512, ev_v)
        s = w_get()
        proj_fm(s, 0, 512, lambda j, hh, ps: act(gT[:, j, hh * 512:(hh + 1) * 512], ps.ap, AF.Silu, [ps.b], [RgB[0]]))
        if sub == 0 or (sub == 1 and u == 1):
            return
        if u == 0:
            for sq in range(4):
                for h in range(4):
                    qs = slice(sq * 256, (sq + 1) * 256)
                    steps = []
                    for c in range(2):
                        t0 = sq * 256 + c * 128
                        steps.append((kT[:, h, t0:t0 + 128], [RgB[4]], qT[:, h, qs], [RgB[3]], 256, 0, None, [],
                                      vv[:, sq * 2 + c, h * 128:(h + 1) * 128], [RgB[5]]))
                    attn(steps, 256, catT[:, 4 + h, qs], [catB[4 + h]], gT[:, h, qs], [RgB[0]])
        else:
            ick = tmp()
            icv = tmp()
            ckT = Tp[ick][:].bitcast(BF16).rearrange("p (a t) -> p a t", a=4)
            cv = Tp[icv][:].bitcast(BF16).rearrange("p (a t) -> p a t", a=4)
            K.dma("pool", ckT, c_nakT[l], [], [TpB[ick]])
            K.dma("pool", cv, c_nav[l], [], [TpB[icv]])
            ibs = tmp()
            reserved.update([ick, icv, ibs])
            for h in range(4):
                bias = Tp[ibs][:].rearrange("p (a t) -> p a t", a=16)
                K.dma("sp", bias, rpbT[l, h], [], [TpB[ibs]])
                tt(bias, bias, maskS, ALU.add, [TpB[ibs], maskB], [TpB[ibs]])
                for qb in range(2):
                    qs = slice(qb * 512, (qb + 1) * 512)
                    steps = []
                    for c in range(4):
                        steps.append((ckT[:, h, c * 128:(c + 1) * 128], [TpB[ick]], qT[:, h, qs], [RgB[3]], 512, 0, None, [],
                                      cv[:, c, h * 128:(h + 1) * 128], [TpB[icv]]))
                    for r in range(8):
                        rr = qb * 8 + r
                        R0 = min(max(rr - 4, 0), 8)
                        q64 = qT[:, h, rr * 64:(rr + 1) * 64]
                        if R0 % 2 == 0:
                            lst = [(R0 // 2 + c, R0 + 2 * c - rr + 7) for c in range(4)]
                        else:
                            m0 = (R0 - 1) // 2
                            lst = [(m0, 14)] + [(m0 + c, 2 * (m0 + c) - rr + 7) for c in (1, 2, 3)] + [(m0 + 4, 15)]
                        bias2 = bias.rearrange("p (a two) t -> p a two t", two=2)

                        def grp(sub):
                            ms = [m for (m, _) in sub]
                            s0 = sub[0][1]
                            if len(sub) == 1:
                                bv = bias[:, s0, :]
                            else:
                                bv = bias2[:, s0 // 2: s0 // 2 + len(sub), s0 % 2, :]
                            steps.append(([kT[:, h, m * 128:(m + 1) * 128] for m in ms], [RgB[4]], q64, [RgB[3]], 64, r * 64,
                                          bv, [TpB[ibs]], [vv[:, m, h * 128:(h + 1) * 128] for m in ms], [RgB[5]]))
                        if R0 % 2 == 0:
                            grp(lst)
                        else:
                            grp(lst[0:1])
                            grp(lst[1:4])
                            grp(lst[4:5])
                    attn(steps, 512, catT[:, 4 + h, qs], [catB[4 + h]], gT[:, h, qs], [RgB[0]])
            reserved.clear()

    def branchC(l, u):
        pc = ppt[l % 2]
        pb = ppB[l % 2]
        qn, kn, vv, gT = R4(1), Rg[2][:, :2048].rearrange("p (a t) -> p a t", a=2), Rg[2][:, 2048:].rearrange("p (a t) -> p a t", a=8), R4(3)
        if u == 1:
            irc = tmp()
            irs = tmp()
            K.dma("sp", Tp[irc][:], ropeC, [], [TpB[irc]])
            K.dma("sp", Tp[irs][:], ropeS, [], [TpB[irs]])
            reserved.update([irc, irs])

        def normrope(ps, wcol, hh, out_bf, outB, dma_out):
            sl = slice(hh * 512, (hh + 1) * 512)
            ia = tmp()
            xf = Tp[ia][:, :512]
            sq = Tp[ia][:, 512:]
            act(xf, ps.ap, AF.Copy, [ps.b], [TpB[ia]])
            act(sq, ps.ap, AF.Square, [ps.b], [TpB[ia]])
            p2 = K.psum()
            mm(p2.ap, ONf, sq, True, True, [TpB[ia], csB], [p2.b], True)
            rsq(sq, p2.ap, 1.0 / 128, [p2.b], [TpB[ia]])
            stt(xf, xf, wcol, sq, ALU.mult, ALU.mult, [TpB[ia], pb], [TpB[ia]])
            if dma_out is not None:
                K.dma("sp", dma_out, xf, [TpB[ia]], [])
            if u == 0:
                cp(out_bf, xf, [TpB[ia]], outB)
            else:
                p3 = K.psum()
                mm(p3.ap, PMf, xf, True, True, [TpB[ia], csB], [p3.b], True)
                tt(sq, p3.ap, Tp[irs][:, sl], ALU.mult, [p3.b, TpB[irs]], [TpB[ia]])
                tt(xf, xf, Tp[irc][:, sl], ALU.mult, [TpB[ia], TpB[irc]], [TpB[ia]])
                tt(out_bf, xf, sq, ALU.add, [TpB[ia]], outB)

        qw = PP["qnw"][0]
        kw = PP["knw"][0]
        s = w_get()
        proj_fm(s, 0, 512, lambda j, hh, ps: normrope(ps, pc[:, qw:qw + 1], hh, qn[:, j, hh * 512:(hh + 1) * 512], [RgB[1]], None))
        s = w_get()
        proj_fm(s, 0, 256, lambda j, hh, ps: normrope(ps, pc[:, kw:kw + 1], hh, kn[:, j, hh * 512:(hh + 1) * 512], [RgB[2]],
                                                       o_gk[l, :, j, hh * 512:(hh + 1) * 512] if u == 0 else None))

        def ev_v(t8, ps):
            if u == 0:
                a = tmp()
                act(Tp[a][:, :256], ps.ap[:, :256], AF.Copy, [ps.b], [TpB[a]])
                K.dma("sp", o_gv[l, :, t8, :], Tp[a][:, :256], [TpB[a]], [])
                cp(vv[:, t8, :], Tp[a][:, :256], [TpB[a]], [RgB[2]])
            else:
                cp(vv[:, t8, :], ps.ap[:, :256], [ps.b], [RgB[2]])
        proj_tm(s, 256, 256, ev_v)
        s = w_get()
        proj_fm(s, 0, 512, lambda j, hh, ps: act(gT[:, j, hh * 512:(hh + 1) * 512], ps.ap, AF.Silu, [ps.b], [RgB[3]]))
        if u == 0:
            for sq in range(4):
                for h in range(4):
                    kv = h // 2
                    qs = slice(sq * 256, (sq + 1) * 256)
                    steps = []
                    for c in range(2):
                        t0 = sq * 256 + c * 128
                        steps.append((kn[:, kv, t0:t0 + 128], [RgB[2]], qn[:, h, qs], [RgB[1]], 256, 0, None, [],
                                      vv[:, sq * 2 + c, kv * 128:(kv + 1) * 128], [RgB[2]]))
                    attn(steps, 256, catT[:, 8 + h, qs], [catB[8 + h]], gT[:, h, qs], [RgB[3]])
        else:
            ick = tmp()
            ckT = Tp[ick][:, :512].bitcast(BF16).rearrange("p (a t) -> p a t", a=2)
            cv = Tp[ick][:, 512:].bitcast(BF16).rearrange("p (a t) -> p a t", a=4)
            K.dma("pool", ckT, c_gkT[l], [], [TpB[ick]])
            K.dma("pool", cv, c_gv[l], [], [TpB[ick]])
            reserved.add(ick)
            for h in range(4):
                kv = h // 2
                for qb in range(2):
                    qs = slice(qb * 512, (qb + 1) * 512)
                    steps = []
                    for c in range(8):
                        steps.append((kn[:, kv, c * 128:(c + 1) * 128], [RgB[2]], qn[:, h, qs], [RgB[1]], 512, 0, None, [],
                                      vv[:, c, kv * 128:(kv + 1) * 128], [RgB[2]]))
                    for c in range(4):
                        steps.append((ckT[:, kv, c * 128:(c + 1) * 128], [TpB[ick]], qn[:, h, qs], [RgB[1]], 512, 0, None, [],
                                      cv[:, c, kv * 128:(kv + 1) * 128], [TpB[ick]]))
                    attn(steps, 512, catT[:, 8 + h, qs], [catB[8 + h]], gT[:, h, qs], [RgB[3]])
            reserved.clear()

    def branchD(l, u):
        pc = ppt[l % 2]
        pb = ppB[l % 2]
        qe, qo, kT, ktm, vv, og = R4(4), R4(5), R4(0), R8(1), R8(2), R8(3)
        K.op("dve", lambda: nc.vector.memset(Rg[4][:], 0.0), [], [RgB[4]], c=2.0)
        K.op("dve", lambda: nc.vector.memset(Rg[5][:], 0.0), [], [RgB[5]], c=2.0)
        s = w_get()

        def ev_q(j, hh, ps):
            pv = ps.ap.rearrange("p (t c k) -> p t c k", t=4, c=2)
            qev = qe[:, j, hh * 512:(hh + 1) * 512].rearrange("p (t c k) -> p t c k", t=4, c=2)
            qov = qo[:, j, hh * 512:(hh + 1) * 512].rearrange("p (t c k) -> p t c k", t=4, c=2)
            ts(qev[:, :, 0, :], pv[:, :, 0, :], SCL, None, ALU.mult, None, [ps.b], [RgB[4]])
            ts(qov[:, :, 1, :], pv[:, :, 1, :], SCL, None, ALU.mult, None, [ps.b], [RgB[5]])
        proj_fm(s, 0, 512, ev_q)
        s = w_get()
        proj_fm(s, 0, 512, lambda j, hh, ps: cp(kT[:, j, hh * 512:(hh + 1) * 512], ps.ap, [ps.b], [RgB[0]]))
        for t8 in range(8):
            pt = K.psum()
            ptb = pt.ap.bitcast(BF16)
            for h in range(4):
                K.op("pe", lambda h=h, t8=t8, ptb=ptb: nc.tensor.transpose(out=ptb[:, h * 128:(h + 1) * 128], in_=kT[:, h, t8 * 128:(t8 + 1) * 128], identity=IDb),
                     [RgB[0], csB], [pt.b], h == 3, c=0.1)
            cp(ktm[:, t8, :], ptb[:, 0:512], [pt.b], [RgB[1]])
        s = w_get()
        proj_tm(s, 0, 512, lambda t8, ps: cp(vv[:, t8, :], ps.ap, [ps.b], [RgB[2]]))
        s = w_get()
        proj_tm(s, 0, 512, lambda t8, ps: act(og[:, t8, :], ps.ap, AF.Sigmoid, [ps.b], [RgB[3]]))
        s = w_get()

        def ev_g(t8, ps):
            a = tmp()
            act(Tp[a][:, :512], ps.ap, AF.Silu, [ps.b], [TpB[a]])
            tt(og[:, t8, :], og[:, t8, :], Tp[a][:, :512], ALU.mult, [RgB[3], TpB[a]], [RgB[3]])
        proj_tm(s, 0, 512, ev_g)
        K.dma("pool", wif[:], w_ifT[l], [], [wifB])
        g0 = PP["gb"][0]
        proj_tm(None, 0, 16, lambda t8, ps: tt(gt[:, t8, :], ps.ap[:, :16], pc[:, g0:g0 + 16], ALU.add, [ps.b, pb], [gB]), wt=wif, wb=wifB)
        gv = gt[:].rearrange("p t (d w h) -> p d t w h", d=2, w=2)
        for dr in range(2):
            act(nlf[:, dr], gv[:, dr, :, 1, :], AF.Exp, [gB], [gB], scale=-1.0)
        ts(nlf[:], nlf[:], 1.0, None, ALU.add, None, [gB], [gB])
        act(nlf[:], nlf[:], AF.Ln, [gB], [gB])
        nlf2 = nlf[:].rearrange("p d t h -> p (d t h)")
        ps = K.psum()
        mm(ps.ap[:, 0:32], TRF, nlf2[:, 0:32], True, True, [gB, csB], [ps.b], False)
        mm(ps.ap[:, 32:64], TRB, nlf2[:, 32:64], True, True, [gB, csB], [ps.b], True)
        cp(nbt[:].rearrange("p d t h -> p (d t h)"), ps.ap[:, :64], [ps.b], [gB])
        ps = K.psum()
        mm(ps.ap[:, 0:64], H0, nlf2, True, True, [gB, csB], [ps.b], False)
        mm(ps.ap[:, 64:128], H1, nlf2, True, True, [gB, csB], [ps.b], True)
        act(eBt[:].rearrange("p c d t h -> p (c d t h)"), ps.ap[:, :128], AF.Exp, [ps.b], [gB], scale=-1.0)
        for dr in range(2):
            tt(eat[:, dr], gv[:, dr, :, 0, :], nbt[:, dr], ALU.add, [gB], [gB])
        if u == 0:
            nlv = nlf[:].rearrange("p d (s i) h -> p d i s h", i=2)
            pz = K.psum()

            def zc(i, d):
                return pz.ap[:, (i * 2 + d) * 16:(i * 2 + d + 1) * 16].rearrange("p (s h) -> p s h", s=4)
            mm(zc(0, 0), TFF, nlv[:, 0, 0], True, True, [gB, csB], [pz.b], False)
            mm(zc(1, 0), ONf, nlv[:, 0, 0], True, False, [gB, csB], [pz.b], False)
            mm(zc(1, 0), TFF, nlv[:, 0, 1], False, True, [gB, csB], [pz.b], False)
            mm(zc(1, 1), TFB, nlv[:, 1, 1], True, True, [gB, csB], [pz.b], False)
            mm(zc(0, 1), ONf, nlv[:, 1, 1], True, False, [gB, csB], [pz.b], False)
            mm(zc(0, 1), TFB, nlv[:, 1, 0], False, True, [gB, csB], [pz.b], True)
            liv = gt[:].rearrange("p (s i) (d w h) -> p i d s w h", i=2, d=2, w=2)
            for i in range(2):
                for d in range(2):
                    tt(zt[:, i, d * 16:(d + 1) * 16].rearrange("p (s h) -> p s h", s=4), liv[:, i, d, :, 0, :], zc(i, d), ALU.add, [gB, pz.b], [gB])
                    cp(nlfz[:, i, d * 16:(d + 1) * 16].rearrange("p (s h) -> p s h", s=4), nlv[:, d, i], [gB], [gB])
            pg = K.psum()
            pg2 = K.psum()
            for i in range(2):
                mm(pg.ap[0:32, i * 128:(i + 1) * 128], zt[:, i, :], IDf, True, True, [gB, csB], [pg.b], i == 1)
            for i in range(2):
                mm(pg2.ap[0:32, i * 128:(i + 1) * 128], nlfz[:, i, :], IDf, True, True, [gB, csB], [pg2.b], i == 1)
            K.op("dve", lambda: nc.vector.tensor_reduce(out=mfs[:, 0:1], in_=pg.ap[0:32, 0:256], axis=AX.X, op=ALU.max), [pg.b], [gB])
            K.op("dve", lambda: nc.vector.tensor_reduce(out=mfs[:, 1:2], in_=pg2.ap[0:32, 0:256], axis=AX.X, op=ALU.add), [pg2.b], [gB])
            ts(mfs[:, 2:3], mfs[:, 0:1], 0.0, mfs[:, 1:2], ALU.max, ALU.subtract, [gB], [gB])
            K.dma("sp", o_m[l], mfs[:, 2:3], [gB], [])
            act(mfs[:, 3:4], mfs[:, 2:3], AF.Exp, [gB], [gB], scale=-1.0)
            ia = tmp()
            ts(Tp[ia][0:32, 0:32], cs[0:32, 0, 0:32], mfs[:, 3:4], None, ALU.mult, None, [gB, csB], [TpB[ia]])
            pb2 = K.psum()
            mm(pb2.ap[:, 0:32], cs[0:32, 1, :], Tp[ia][0:32, 0:32], True, True, [TpB[ia], csB], [pb2.b], True)
            cp(emfbc[:], pb2.ap[:, 0:32], [pb2.b], [gB])
        act(enb[:], nbt[:], AF.Exp, [gB], [gB])
        act(eat[:], eat[:], AF.Exp, [gB], [gB])
        ts(ea01[:, 0], eat[:], cs[:, 7, 0:1], None, ALU.mult, None, [gB, csB], [gB])
        ts(ea01[:, 1], eat[:], cs[:, 8, 127:128], None, ALU.mult, None, [gB, csB], [gB])
        tt(ea01[:].rearrange("p c d t h -> p (c d t h)"), ea01[:].rearrange("p c d t h -> p (c d t h)"),
           eBt[:].rearrange("p c d t h -> p (c d t h)"), ALU.mult, [gB], [gB])
        if u == 1 and l + 1 < depth:
            mod_compute(l + 1)
        if u == 1:
            m00 = PP["m0"][0]
            act(em0[:], pc[:, m00:m00 + 8], AF.Exp, [pb], [gB])
            for dr in range(2):
                ia = tmp()
                c0 = Tp[ia][:, :516].rearrange("p (h k) -> p h k", h=4)
                K.dma("sp", c0, C0n0[l, :, dr], [], [TpB[ia]])
                for h in range(4):
                    ts(St[dr][h][:], c0[:, h, :], em0[:, dr * 4 + h: dr * 4 + h + 1], None, ALU.mult, None, [TpB[ia], gB], [StB[dr][h]])
                    act(Stb[dr][h][:], St[dr][h][:], AF.Copy, [StB[dr][h]], [StbB[dr][h]])
        nseq, tps = (4, 2) if u == 0 else (1, 8)
        won0 = PP["won"][0]
        hsI = [tmp() for _ in range(4)]
        reserved.update(hsI)

        def hs_tile(t8):
            return Tp[hsI[t8 // 2]][:, (t8 % 2) * 512:(t8 % 2 + 1) * 512].rearrange("p (h k) -> p h k", h=4), TpB[hsI[t8 // 2]]

        def instance(dr, h, t8, first_dir):
            tok = slice(t8 * 128, (t8 + 1) * 128)
            hsl = slice(h * 128, (h + 1) * 128)
            pss_ap, pss_b = K.pslot()
            mm(pss_ap[:, :128], kT[:, h, tok], qe[:, h, tok], True, False, [RgB[0], RgB[4]], [pss_b], False)
            mm(pss_ap[:, :128], kT[:, h, tok], qo[:, h, tok], False, True, [RgB[0], RgB[5]], [pss_b], True)
            ip = nxt("sm", NSM)
            stt(smt[ip][:], pss_ap[:, :128], eat[:, dr, t8, h:h + 1], MBF if dr == 0 else MBB, ALU.mult, ALU.mult, [pss_b, gB, csB], [smB[ip]])
            pn = K.psum()
            order = (0, 1) if dr == 0 else (1, 0)
            for ci, c in enumerate(order):
                qq = qe if c == 0 else qo
                mm(pn.ap[:, 0:129], qq[:, h, tok], Stb[dr][h][:], ci == 0, False, [RgB[4 + c], StbB[dr][h]], [pn.b], True)
                ik = nxt("sm", NSM)
                act(smt[ik][:], ktm[:, t8, hsl], AF.Copy, [RgB[1], gB], [smB[ik]], scale=ea01[:, c, dr, t8, h:h + 1])
                pu_ap, pu_b = K.pslot()
                mm(pu_ap[:, 0:128], smt[ik][:], vv[:, t8, hsl], True, True, [smB[ik], RgB[2]], [pu_b], False)
                mm(pu_ap[:, 128:129], smt[ik][:], ONb[:, 0:1], True, True, [smB[ik], csB], [pu_b], True)
                ebc = eBt[:, c, dr, t8, h:h + 1]
                stt(St[dr][h][:], St[dr][h][:], ebc, pu_ap[:, 0:129], ALU.mult, ALU.add, [pu_b, StB[dr][h], gB], [StB[dr][h]])
                act(Stb[dr][h][:], St[dr][h][:], AF.Copy, [StB[dr][h]], [StbB[dr][h]])
            mm(pn.ap[:, 0:128], smt[ip][:], vv[:, t8, hsl], False, False, [smB[ip], RgB[2]], [pn.b], False)
            mm(pn.ap[:, 128:129], smt[ip][:], ONb[:, 0:1], False, True, [smB[ip], csB], [pn.b], True)
            it = nxt("tiny", 6)
            act(tiny[it][:, 0:1], pn.ap[:, 128:129], AF.Abs, [pn.b], [tinyB[it]])
            ts(tiny[it][:, 0:1], tiny[it][:, 0:1], enb[:, dr, t8, h:h + 1], None, ALU.max, None, [tinyB[it], gB], [tinyB[it]])
            K.op("dve", lambda: nc.vector.reciprocal(out=tiny[it][:, 1:2], in_=tiny[it][:, 0:1]), [tinyB[it]], [tinyB[it]])
            hv, hb = hs_tile(t8)
            if first_dir:
                act(hv[:, h, :], pn.ap[:, 0:128], AF.Copy, [pn.b, tinyB[it]], [hb], scale=tiny[it][:, 1:2])
            else:
                stt(hv[:, h, :], pn.ap[:, 0:128], tiny[it][:, 1:2], hv[:, h, :], ALU.mult, ALU.add, [pn.b, tinyB[it], hb], [hb])

        def finish_tile(t8):
            hv, hb = hs_tile(t8)
            ia = tmp()
            sq = Tp[ia][:, :512].rearrange("p (h k) -> p h k", h=4)
            it = nxt("tiny", 6)
            tt(sq, hv, hv, ALU.mult, [hb], [TpB[ia]])
            K.op("dve", lambda: nc.vector.tensor_reduce(out=tiny[it][:, 0:4], in_=sq, axis=AX.X, op=ALU.add), [TpB[ia]], [tinyB[it]])
            rsq(tiny[it][:, 0:4], tiny[it][:, 0:4], 1.0 / 128, [tinyB[it]], [tinyB[it]])
            for h in range(4):
                stt(sq[:, h, :], hv[:, h, :], tiny[it][:, h:h + 1], pc[:, won0 + h * 128: won0 + (h + 1) * 128], ALU.mult, ALU.mult,
                    [hb, tinyB[it], pb, TpB[ia]], [TpB[ia]])
            od = Tp[ia][:, 512:].bitcast(BF16)[:, :512]
            tt(od, Tp[ia][:, :512], og[:, t8, :], ALU.mult, [TpB[ia], RgB[3]], [TpB[ia]])
            pt = K.psum()
            ptb = pt.ap.bitcast(BF16)
            for h in range(4):
                K.op("pe", lambda h=h: nc.tensor.transpose(out=ptb[:, h * 128:(h + 1) * 128], in_=od[:, h * 128:(h + 1) * 128], identity=IDb),
                     [TpB[ia], csB], [pt.b], h == 3)
            act(catT[:, 12:16, t8 * 128:(t8 + 1) * 128], ptb[:, 0:512].rearrange("p (h k) -> p h k", h=4), AF.Copy, [pt.b], [catB[12 + i] for i in range(4)])

        for sq in range(nseq):
            if u == 0:
                for dr in range(2):
                    for h in range(4):
                        K.op("dve", lambda dr=dr, h=h: nc.vector.memset(St[dr][h][:], 0.0), [], [StB[dr][h]])
                        K.op("dve", lambda dr=dr, h=h: nc.vector.memset(Stb[dr][h][:], 0.0), [], [StbB[dr][h]])
            done = {}
            for i in range(tps):
                for dr in range(2):
                    t8 = sq * tps + (i if dr == 0 else tps - 1 - i)
                    for h in range(4):
                        instance(dr, h, t8, t8 not in done)
                    done[t8] = done.get(t8, 0) + 1
                    if done[t8] == 2:
                        finish_tile(t8)
            if u == 0:
                for dr in range(2):
                    for h in range(4):
                        ig = nxt("stg", 2)
                        c = dr * 16 + sq * 4 + h
                        ts(stg[ig][:], St[dr][h][:], emfbc[:, c:c + 1], None, ALU.mult, None, [StB[dr][h], gB], [stgB[ig]])
                        K.dma("sp", o_C[l, :, dr, sq, h, :], stg[ig][:], [stgB[ig]], [])
        reserved.clear()

    def phaseW(l, u, last_layer):
        ssp = [K.pin(), K.pin()]
        for t in range(4):
            s = w_get()
            for j in range(4):
                fc = t * 4 + j
                for hh in range(2):
                    sl = slice(hh * 512, (hh + 1) * 512)
                    ps = K.ppsum()
                    for kc in range(16):
                        mm(ps.ap, Wt[s][:, kc, j * 128:(j + 1) * 128], catT[:, kc, sl], kc == 0, kc == 15, [Wb[s][kc // 4], catB[kc]], [ps.b], kc == 15)
                    a = tmp()
                    rds = [] if l == 0 else [xsb[u][fc][hh]]
                    K.dma("sp", Tp[a][:, :512], xsrc(l, u, fc)[:, sl], rds, [TpB[a]])
                    stt(Tp[a][:, :512], ps.ap, modL[l % 2][:, 32 + fc, u:u + 1], Tp[a][:, :512], ALU.mult, ALU.add, [ps.b, modBL[l % 2], TpB[a]], [TpB[a]])
                    K.dma("sp", xsc[u, fc][:, sl], Tp[a][:, :512], [TpB[a]], [xsb[u][fc][hh]])
                    sqb = Tp[a][:, 512:].bitcast(BF16)[:, :512]
                    act(sqb, Tp[a][:, :512], AF.Square, [TpB[a]], [TpB[a]])
                    mm(ssp[hh].ap, ONb, sqb, fc == 0, fc == 15, [TpB[a], csB], [ssp[hh].b], True)
        compute_rstd(u, ssp)
        K.unpin(ssp[0])
        K.unpin(ssp[1])
        if last_layer:
            for fc in range(16):
                a = tmp()
                K.dma("sp", Tp[a][:], xsc[u, fc], [xsb[u][fc][0], xsb[u][fc][1]], [TpB[a]])
                tt(Tp[a][:], Tp[a][:], rstd[:, u, :], ALU.mult, [TpB[a], rstdB[u]], [TpB[a]])
                ts(Tp[a][:], Tp[a][:], fnw_sb[:, fc:fc + 1], None, ALU.mult, None, [TpB[a], scB], [TpB[a]])
                K.dma("sp", yT[u, fc], Tp[a][:], [TpB[a]], [])

    def mod_compute(l):
        mod, gmod, modB = modL[l % 2], gmodL[l % 2], modBL[l % 2]
        K.dma("sp", abt[:], ppd[l][:, 0:64], [], [abtB])
        mp = K.pin()
        for t in range(12):
            s = w_get()
            for j in range(4):
                cc = t * 4 + j
                for kc in range(16):
                    mm(mp.ap[:, cc * 2:cc * 2 + 2], Wt[s][:, kc, j * 128:(j + 1) * 128], sc_bf[:, kc, :], kc == 0, kc == 15,
                       [Wb[s][kc // 4], scB], [mp.b], j == 3 and kc == 15)
        mpv = mp.ap[:, 0:96].rearrange("p (c u) -> p c u", u=2)
        for u in range(2):
            tt(mod[:, :, u], mpv[:, :, u], abt[:, 0:48], ALU.add, [mp.b, abtB], [modB])
            stt(gmod[:, :, u], mod[:, 16:32, u], 1.0, abt[:, 48:64], ALU.add, ALU.mult, [modB, abtB], [modB])
        K.unpin(mp)

    for u in range(2):
        prologue(u)
    K.marks.append((len(K.nodes), "L0 mod", K.npe))
    if stage > 0:
        mod_compute(0)
    for l in range(depth if stage > 0 else 0):
        pc = ppt[l % 2]
        pb = ppB[l % 2]
        K.dma("sp", pc[:], ppd[l], [], [pb])
        lm0 = PP["lam"][0]
        act(clam[:], pc[:, lm0:lm0 + 8], AF.Exp, [pb], [clamB], scale=-1.0)
        ts(clam[:], clam[:], 1.0, None, ALU.add, None, [clamB], [clamB])
        act(clam[:], clam[:], AF.Ln, [clamB], [clamB])
        ts(clam[:], clam[:], -8.0, None, ALU.mult, None, [clamB], [clamB])
        for u in range(2 if stage > 1 else 0):
            K.marks.append((len(K.nodes), "L%d u%d N" % (l, u), K.npe))
            phaseN(l, u, pc)
            K.marks.append((len(K.nodes), "L%d u%d A" % (l, u), K.npe))
            if stage > 2:
                branchA(l, u)
            else:
                [w_get() for _ in range(2)]
            K.marks.append((len(K.nodes), "L%d u%d B" % (l, u), K.npe))
            if stage > 3:
                branchB(l, u)
            else:
                [w_get() for _ in range(4)]
            K.marks.append((len(K.nodes), "L%d u%d C" % (l, u), K.npe))
            if stage > 4:
                branchC(l, u)
            else:
                [w_get() for _ in range(3)]
            K.marks.append((len(K.nodes), "L%d u%d D" % (l, u), K.npe))
            if stage > 5:
                branchD(l, u)
            else:
                [w_get() for _ in range(5)]
            if dbg and l == 0 and stage <= 6:
                for kc in range(16):
                    a = tmp()
                    cp(Tp[a][:], catT[:, kc, :], [catB[kc]], [TpB[a]])
                    K.dma("sp", o_dbg[u, kc], Tp[a][:], [TpB[a]], [])
            if stage <= 6:
                [w_get() for _ in range(4)]
                continue
            if dbg and l == 0:
                for kc in range(16):
                    a = tmp()
                    cp(Tp[a][:], catT[:, kc, :], [catB[kc]], [TpB[a]])
                    K.dma("sp", o_dbg[u, kc], Tp[a][:], [TpB[a]], [])
            K.marks.append((len(K.nodes), "L%d u%d W" % (l, u), K.npe))
            phaseW(l, u, l == depth - 1)
    K.marks.append((len(K.nodes), "end", K.npe))
    K.finish()
    return nc, K


def _consts():
    c = np.zeros((11, 128, 128), np.float32)
    i = np.arange(128)
    c[0] = np.eye(128)
    c[1] = 1.0
    partner = np.where((i % 64) < 32, i + 32, i - 32)
    c[2][partner, i] = 1.0
    same = (i[:, None] // 64) == (i[None, :] // 64)
    le = i[:, None] <= i[None, :]
    ge = i[:, None] >= i[None, :]
    c[3] = (same & le)
    c[4] = (same & ge)
    c[5] = le
    c[6] = ge
    c[7] = (i[:, None] < 64) * np.ones((1, 128))
    c[8] = (i[:, None] >= 64) * np.ones((1, 128))
    c[9] = (same & le)
    c[10] = (same & ge)
    return np.ascontiguousarray(c.transpose(1, 0, 2))


def _rope_tables():
    t = np.arange(NT)
    d = np.arange(128)
    pos = np.where(d[:, None] < 64, (t // 64)[None, :], (t % 64)[None, :]).astype(np.float32)
    nf = 32
    inv = (np.float32(10000.0) ** (-(np.arange(nf, dtype=np.float32)) / np.float32(nf))).astype(np.float32)
    ang = pos * inv[d % 32][:, None]
    C = np.cos(ang).astype(np.float32)
    S = np.sin(ang).astype(np.float32)
    sign = np.where((d % 64) < 32, -1.0, 1.0).astype(np.float32)
    return C, (S * sign[:, None]).astype(np.float32)


def _na_tables(rpb_l):
    col = np.arange(64)
    qcol = np.arange(64)
    qs = np.clip(qcol - 8, 0, 48)
    valid = (col[:, None] >= qs[None, :]) & (col[:, None] < qs[None, :] + 16)
    coff = np.clip(col[:, None] - qcol[None, :] + 15, 0, 30)
    g = np.zeros((4, 128, 16, 64), np.float32)
    m = np.zeros((128, 16, 64), np.float32)
    for s in range(16):
        for jj in range(2):
            if s < 14:
                ro, ok = s + jj, True
            elif s == 14:
                ro, ok = 2 + jj, jj == 1
            else:
                ro, ok = 10 + jj, jj == 0
            g[:, jj * 64:(jj + 1) * 64, s, :] = rpb_l[:, ro][:, coff]
            m[jj * 64:(jj + 1) * 64, s, :] = np.where(valid & ok, 0.0, -1e30)
    return g, m


_CACHE = {}


def _get_nc(depth=L, dbg=False):
    key = (depth, dbg)
    if key not in _CACHE:
        _CACHE[key] = build(depth, dbg)[0]
    return _CACHE[key]


def _tile_w(w, ntile):
    return np.ascontiguousarray(w.reshape(16, 128, ntile, 512).transpose(2, 1, 0, 3))


def prep_inputs(inp, depth=L):
    f = lambda a: np.ascontiguousarray(np.asarray(a, dtype=np.float32))
    g = {k: np.asarray(v) for k, v in inp.items()}
    sh = {}
    sh["ada_wT"] = np.stack([_tile_w(g["ada_w"][l], 12) for l in range(L)])
    sh["w_inT"] = np.stack([_tile_w(g["w_in"][l][:, :7168], 14) for l in range(L)])
    sh["w_ifT"] = f(g["w_in"][:, :, 7168:].reshape(L, 16, 128, 16).transpose(0, 2, 1, 3))
    sh["w_outT"] = np.stack([_tile_w(g["w_out"][l], 4) for l in range(L)])
    sh["fnw"] = f(g["final_norm_w"].reshape(16, 128).T)
    lw = np.stack([g["lru_wr"], g["lru_wi"]], axis=2)
    sh["lru_wT"] = f(lw.transpose(0, 4, 1, 2, 3, 5).reshape(L, 128, 16, 128))
    gm = [_na_tables(g["na_rpb"][l]) for l in range(L)]
    sh["rpbT"] = f(np.stack([x[0] for x in gm]))
    sh["maskT"] = f(gm[0][1])
    C, S = _rope_tables()
    sh["ropeC"], sh["ropeS"] = f(C), f(S)
    sh["cst"] = f(_consts())
    maps = []
    for c in range(NCORE):
        m = dict(sh)
        xp = g["x_prompt"][4 * c:4 * c + 4].reshape(NT, D)
        xs = g["x_sample"][c]
        m["x0T"] = f(np.stack([xp.T.reshape(16, 128, NT), xs.T.reshape(16, 128, NT)]))
        cond = np.stack([g["c_ctx"], g["c"][c]], axis=-1)
        m["condT"] = f(cond.reshape(16, 128, 2).transpose(1, 0, 2))
        pp = np.zeros((L, 128, NPP), np.float32)

        def put(name, arr):
            o, w = PP[name]
            pp[:, :, o:o + w] = arr.reshape(L, 128, w)
        put("ada_b", g["ada_b"].reshape(L, 48, 128).transpose(0, 2, 1))
        put("norm_w", g["norm_w"].reshape(L, 16, 128).transpose(0, 2, 1))
        put("conv_w", g["lru_conv_w"].reshape(L, 4, 4, 128).transpose(0, 3, 2, 1))
        put("conv_b", g["lru_conv_b"].reshape(L, 4, 128).transpose(0, 2, 1))
        lb = np.stack([g["lru_br"], g["lru_bi"]], axis=2)
        put("lru_b", lb.reshape(L, 2, 2, 4, 128).transpose(0, 4, 1, 2, 3))
        put("lam", g["lru_lambda"].reshape(L, 2, 4, 128).transpose(0, 3, 1, 2))
        put("qnw", g["gqa_qnorm"].reshape(L, 128, 1))
        put("knw", g["gqa_knorm"].reshape(L, 128, 1))
        put("gb", np.broadcast_to(g["ml_gate_b"][:, None, :], (L, 128, 16)))
        put("won", np.broadcast_to(g["ml_out_norm"][:, None, :], (L, 128, 512)))
        put("h0", g["state_lru"][c].reshape(L, 2, 4, 128).transpose(0, 3, 1, 2))
        put("m0", np.broadcast_to(g["state_mlstm_m"][c].reshape(L, 1, 8), (L, 128, 8)))
        m["pp"] = pp
        m["c_nakT"] = f(g["cache_na_k"][c].transpose(0, 3, 2, 1))
        m["c_nav"] = f(g["cache_na_v"][c].reshape(L, 4, 128, 512).transpose(0, 2, 1, 3))
        m["c_gkT"] = f(g["cache_gqa_k"][c].transpose(0, 3, 2, 1))
        m["c_gv"] = f(g["cache_gqa_v"][c].reshape(L, 4, 128, 256).transpose(0, 2, 1, 3))
        Cn = np.concatenate([g["state_mlstm_C"][c], g["state_mlstm_n"][c][..., None]], axis=-1)
        m["C0n0"] = f(Cn.transpose(0, 3, 1, 2, 4))
        maps.append(m)
    return maps


def assemble(res):
    B = 32
    y_p = np.zeros((B, 256, D), np.float32)
    y_s = np.zeros((8, NT, D), np.float32)
    nak = np.zeros((B, L, 256, 4, 128), np.float32)
    nav = np.zeros((B, L, 256, 4, 128), np.float32)
    gk = np.zeros((B, L, 256, 2, 128), np.float32)
    gv = np.zeros((B, L, 256, 2, 128), np.float32)
    lru = np.zeros((B, L, 2, 512), np.float32)
    Cm = np.zeros((B, L, 2, 4, 128, 128), np.float32)
    nm = np.zeros((B, L, 2, 4, 128), np.float32)
    mm_ = np.zeros((B, L, 2, 4), np.float32)
    for c, r in enumerate(res):
        bs = slice(4 * c, 4 * c + 4)
        yT = r["yT"]
        y_p[bs] = yT[0].reshape(D, NT).T.reshape(4, 256, D)
        y_s[c] = yT[1].reshape(D, NT).T
        nak[bs] = r["o_nak"].reshape(L, 128, 4, 4, 256).transpose(3, 0, 4, 2, 1)
        nav[bs] = r["o_nav"].transpose(0, 2, 1, 3).reshape(L, 4, 256, 4, 128).transpose(1, 0, 2, 3, 4)
        gk[bs] = r["o_gk"].reshape(L, 128, 2, 4, 256).transpose(3, 0, 4, 2, 1)
        gv[bs] = r["o_gv"].transpose(0, 2, 1, 3).reshape(L, 4, 256, 2, 128).transpose(1, 0, 2, 3, 4)
        lru[bs] = r["o_lru"].transpose(4, 0, 2, 3, 1).reshape(4, L, 2, 512)
        oc = r["o_C"].transpose(3, 0, 2, 4, 1, 5)
        Cm[bs] = oc[..., :128]
        nm[bs] = oc[..., 128]
        mm_[bs] = r["o_m"].reshape(L, 2, 4, 4).transpose(2, 0, 1, 3)
    return (y_p, y_s, nak, nav, gk, gv, lru, Cm, nm, mm_)


def kernel(**inputs):
    nc = _get_nc()
    maps = prep_inputs(inputs)
    out = run_bass_kernel_spmd(nc, maps, core_ids=list(range(NCORE)))
    return assemble(out.results)
```
